# Optimizing a Trainium2 kernel written in Bass

```python
import math
import jax, jax.numpy as jnp
from jax import lax
import numpy as np

D_MODEL = 1024
BATCH = 4
SEQ = 8192
DEPTH = 2
DEC_BATCH = 16
DEC_SEQ = 4096
PAST_LEN = 128

GRID_W = 64
HEAD_DIM = 64
NA_HEADS = 4
NA_ROWS = 8
NA_COLS = 16
NA_D = NA_HEADS * HEAD_DIM
MLA_HEADS = 6
MLA_Q_RANK = 384
MLA_KV_RANK = 256
MLA_NOPE = 64
MLA_ROPE = 32
MLA_V = 64
MLA_QK = MLA_NOPE + MLA_ROPE
GQA_HEADS = 6
GQA_KV_HEADS = 2
GQA_GROUP = GQA_HEADS // GQA_KV_HEADS
MIX_WIDTH = NA_D + MLA_HEADS * MLA_V + GQA_HEADS * HEAD_DIM
IN_SIZES = (NA_D, NA_D, NA_D,
            MLA_Q_RANK, MLA_KV_RANK, MLA_ROPE,
            GQA_HEADS * HEAD_DIM, GQA_KV_HEADS * HEAD_DIM, GQA_KV_HEADS * HEAD_DIM)
IN_WIDTH = sum(IN_SIZES)
IN_SPLITS = [int(v) for v in np.cumsum(IN_SIZES)[:-1]]
D_FF = ((-(-8 * D_MODEL // 3)) + 255) // 256 * 256
Q_BLOCK = 128
ROPE_BASE = 10000.0
EPS = 1e-6

kernel_name = "hybrid_natten_mla_axialgqa_encoder"


def rms_norm(x, g):
    xf = x.astype(jnp.float32)
    y = xf * lax.rsqrt(jnp.mean(xf * xf, axis=-1, keepdims=True) + EPS)
    return (y * g.astype(jnp.float32)).astype(x.dtype)


def rope(x, pos):
    d = x.shape[-1]
    half = d // 2
    inv = ROPE_BASE ** (-(jnp.arange(half, dtype=jnp.float32) * 2.0) / d)
    ang = pos[:, None] * inv[None, :]
    cos = jnp.cos(ang)[None, :, None, :]
    sin = jnp.sin(ang)[None, :, None, :]
    xf = x.astype(jnp.float32)
    x1, x2 = xf[..., :half], xf[..., half:]
    return jnp.concatenate([x1 * cos - x2 * sin, x2 * cos + x1 * sin], axis=-1).astype(x.dtype)


def axial_rope(x, row, col):
    a = x.shape[-1] // 2
    return jnp.concatenate([rope(x[..., :a], row), rope(x[..., a:], col)], axis=-1)


def block_attention(q, k, v, scale):
    B, S, Hk, G, Dq = q.shape
    Dv = v.shape[-1]
    nb = S // Q_BLOCK
    qb = q.reshape(B, nb, Q_BLOCK, Hk, G, Dq).transpose(1, 0, 2, 3, 4, 5)

    def one_block(qi):
        s = jnp.einsum('bqhgd,bkhd->bhgqk', qi, k).astype(jnp.float32) * scale
        p = jax.nn.softmax(s, axis=-1)
        return jnp.einsum('bhgqk,bkhd->bqhgd', p.astype(v.dtype), v)

    o = lax.map(one_block, qb)
    return o.transpose(1, 0, 2, 3, 4, 5).reshape(B, S, Hk * G, Dv)


def neighbourhood_attention(q, k, v, rpb):
    B, S, H, D = q.shape
    rows = S // GRID_W
    kr = min(NA_ROWS, rows)
    scale = 1.0 / math.sqrt(D)
    qg = q.reshape(B, rows, GRID_W, H, D)
    kg = k.reshape(B, rows, GRID_W, H, D)
    vg = v.reshape(B, rows, GRID_W, H, D)
    cols = jnp.arange(GRID_W)
    col_start = jnp.clip(cols - NA_COLS // 2, 0, GRID_W - NA_COLS)
    col_idx = col_start[:, None] + jnp.arange(NA_COLS)[None, :]
    dc = col_idx - cols[:, None] + (NA_COLS - 1)
    bias_c = rpb.astype(jnp.float32)[:, :, dc]

    def one_row(r):
        rs = jnp.clip(r - kr // 2, 0, rows - kr)
        qr = lax.dynamic_index_in_dim(qg, r, axis=1, keepdims=False)
        kb = lax.dynamic_slice_in_dim(kg, rs, kr, axis=1)
        vb = lax.dynamic_slice_in_dim(vg, rs, kr, axis=1)
        kw = kb[:, :, col_idx]
        vw = vb[:, :, col_idx]
        s = jnp.einsum('bchd,bjckhd->bhcjk', qr, kw).astype(jnp.float32) * scale
        dr = rs + jnp.arange(kr) - r + (NA_ROWS - 1)
        bias = jnp.take(bias_c, dr, axis=1).transpose(0, 2, 1, 3)
        s = (s + bias[None]).reshape(B, H, GRID_W, kr * NA_COLS)
        p = jax.nn.softmax(s, axis=-1).reshape(B, H, GRID_W, kr, NA_COLS)
        return jnp.einsum('bhcjk,bjckhd->bchd', p.astype(v.dtype), vw)

    o = lax.map(one_row, jnp.arange(rows))
    return o.transpose(1, 0, 2, 3, 4).reshape(B, S, H, D)


def hybrid_mixer(h, w_in, q_a_norm, kv_a_norm, w_uq, w_ukv, q_norm_c, k_norm_c, rpb,
                 g_out_a, g_out_b, g_out_c, w_out):
    B, S, _ = h.shape
    tok = jnp.arange(S)
    t = tok.astype(jnp.float32)
    row = (tok // GRID_W).astype(jnp.float32)
    col = (tok % GRID_W).astype(jnp.float32)

    proj = h @ w_in
    (qa, ka, va, cq, ckv, kpe, qc, kc, vc) = jnp.split(proj, IN_SPLITS, axis=-1)

    oa = neighbourhood_attention(qa.reshape(B, S, NA_HEADS, HEAD_DIM),
                                 ka.reshape(B, S, NA_HEADS, HEAD_DIM),
                                 va.reshape(B, S, NA_HEADS, HEAD_DIM), rpb)
    oa = oa.reshape(B, S, NA_D)

    qb = (rms_norm(cq, q_a_norm) @ w_uq).reshape(B, S, MLA_HEADS, MLA_QK)
    qb = jnp.concatenate([qb[..., :MLA_NOPE], rope(qb[..., MLA_NOPE:], t)], axis=-1)
    kv = (rms_norm(ckv, kv_a_norm) @ w_ukv).reshape(B, S, MLA_HEADS, MLA_NOPE + MLA_V)
    k_nope, vb = kv[..., :MLA_NOPE], kv[..., MLA_NOPE:]
    k_pe = rope(kpe.reshape(B, S, 1, MLA_ROPE), t)
    kb = jnp.concatenate([k_nope, jnp.broadcast_to(k_pe, (B, S, MLA_HEADS, MLA_ROPE))], axis=-1)
    ob = block_attention(qb[:, :, :, None, :], kb, vb, 1.0 / math.sqrt(MLA_QK))
    ob = ob.reshape(B, S, MLA_HEADS * MLA_V)

    qc = axial_rope(rms_norm(qc.reshape(B, S, GQA_HEADS, HEAD_DIM), q_norm_c), row, col)
    kc = axial_rope(rms_norm(kc.reshape(B, S, GQA_KV_HEADS, HEAD_DIM), k_norm_c), row, col)
    vc = vc.reshape(B, S, GQA_KV_HEADS, HEAD_DIM)
    oc = block_attention(qc.reshape(B, S, GQA_KV_HEADS, GQA_GROUP, HEAD_DIM), kc, vc,
                         1.0 / math.sqrt(HEAD_DIM))
    oc = oc.reshape(B, S, GQA_HEADS * HEAD_DIM)

    merged = jnp.concatenate([rms_norm(oa, g_out_a), rms_norm(ob, g_out_b),
                              rms_norm(oc, g_out_c)], axis=-1)
    return merged @ w_out


def swiglu(h, w_gate, w_up, w_down):
    return (jax.nn.silu(h @ w_gate) * (h @ w_up)) @ w_down


def run_trunk(x, attn_norm, w_in, q_a_norm, kv_a_norm, w_uq, w_ukv, q_norm_c, k_norm_c, rpb,
              g_out_a, g_out_b, g_out_c, w_out, ffn_norm, w_gate, w_up, w_down, final_norm):
    for l in range(DEPTH):
        h = rms_norm(x, attn_norm[l])
        x = x + hybrid_mixer(h, w_in[l], q_a_norm[l], kv_a_norm[l], w_uq[l], w_ukv[l],
                             q_norm_c[l], k_norm_c[l], rpb[l],
                             g_out_a[l], g_out_b[l], g_out_c[l], w_out[l])
        h = rms_norm(x, ffn_norm[l])
        x = x + swiglu(h, w_gate[l], w_up[l], w_down[l])
    return rms_norm(x, final_norm)


def setup_inputs(seed: int = 0) -> dict:
    key = jax.random.key(seed)
    ks = jax.random.split(key, 24)

    def w(k, shape, fan_in):
        return jax.random.normal(k, shape, jnp.float32) * (fan_in ** -0.5)

    def gain(k, shape):
        return 1.0 + 0.02 * jax.random.normal(k, shape, jnp.float32)

    L = DEPTH
    return {
        "x_prompt": jax.random.normal(ks[0], (BATCH, SEQ, D_MODEL), jnp.float32),
        "x_sample": jax.random.normal(ks[1], (DEC_BATCH, DEC_SEQ, D_MODEL), jnp.float32),
        "attn_norm": gain(ks[2], (L, D_MODEL)),
        "w_in": w(ks[3], (L, D_MODEL, IN_WIDTH), D_MODEL),
        "q_a_norm": gain(ks[4], (L, MLA_Q_RANK)),
        "kv_a_norm": gain(ks[5], (L, MLA_KV_RANK)),
        "w_uq": w(ks[6], (L, MLA_Q_RANK, MLA_HEADS * MLA_QK), MLA_Q_RANK),
        "w_ukv": w(ks[7], (L, MLA_KV_RANK, MLA_HEADS * (MLA_NOPE + MLA_V)), MLA_KV_RANK),
        "q_norm_c": gain(ks[8], (L, HEAD_DIM)),
        "k_norm_c": gain(ks[9], (L, HEAD_DIM)),
        "rpb": 0.1 * jax.random.normal(ks[10], (L, NA_HEADS, 2 * NA_ROWS - 1, 2 * NA_COLS - 1), jnp.float32),
        "g_out_a": gain(ks[11], (L, NA_D)),
        "g_out_b": gain(ks[12], (L, MLA_HEADS * MLA_V)),
        "g_out_c": gain(ks[13], (L, GQA_HEADS * HEAD_DIM)),
        "w_out": w(ks[14], (L, MIX_WIDTH, D_MODEL), MIX_WIDTH),
        "ffn_norm": gain(ks[15], (L, D_MODEL)),
        "w_gate": w(ks[16], (L, D_MODEL, D_FF), D_MODEL),
        "w_up": w(ks[17], (L, D_MODEL, D_FF), D_MODEL),
        "w_down": w(ks[18], (L, D_FF, D_MODEL), D_FF),
        "final_norm": gain(ks[19], (D_MODEL,)),
    }


def reference(x_prompt, x_sample, attn_norm, w_in, q_a_norm, kv_a_norm, w_uq, w_ukv,
              q_norm_c, k_norm_c, rpb, g_out_a, g_out_b, g_out_c, w_out, ffn_norm,
              w_gate, w_up, w_down, final_norm):
    y_prompt = run_trunk(x_prompt, attn_norm, w_in, q_a_norm, kv_a_norm, w_uq, w_ukv,
                         q_norm_c, k_norm_c, rpb, g_out_a, g_out_b, g_out_c, w_out,
                         ffn_norm, w_gate, w_up, w_down, final_norm)
    y_sample = run_trunk(x_sample, attn_norm, w_in, q_a_norm, kv_a_norm, w_uq, w_ukv,
                         q_norm_c, k_norm_c, rpb, g_out_a, g_out_b, g_out_c, w_out,
                         ffn_norm, w_gate, w_up, w_down, final_norm)
    return (y_prompt, y_sample)
```

```python
import contextlib
import math

import ml_dtypes
import numpy as np

import concourse.bass as bass
import concourse.mybir as mybir
from concourse.bass_utils import run_bass_kernel_spmd

F32 = mybir.dt.float32
BF16 = mybir.dt.bfloat16
ALU = mybir.AluOpType
AF = mybir.ActivationFunctionType
AX = mybir.AxisListType

D = 1024
L = 2
GRID_W = 64
DFF = 2816
NFF = DFF // 128
EPS = 1e-6
NEG = -30000.0
NIN = 2752
C_QA, C_KA, C_CQ, C_CKV, C_KPA, C_KPB, C_QCA, C_QCB, C_KCA, C_KCB, C_V = (
    0, 256, 512, 896, 1152, 1248, 1344, 1728, 2112, 2240, 2368)
NA_OFFS = (-3, -2, -1, 0, 1, 2, 3)
HS = (0, 2, 1, 3)

SAME_ENGINE_SYNC = {"pe": False, "act": True, "dve": True, "pool": True, "sp": False}


class Op:
    __slots__ = ("stream", "fn", "deps", "flag", "count", "semkey", "nparts", "waits")

    def __init__(self, stream, fn, semkey=None, nparts=0):
        self.stream = stream
        self.fn = fn
        self.deps = ()
        self.flag = False
        self.count = None
        self.semkey = semkey
        self.nparts = nparts
        self.waits = None


class Prog:
    def __init__(self):
        self.ops = []
        self.slots = {}
        self.streams = {s: [] for s in ("pe", "act", "dve", "pool", "sp")}
        self.last = {}

    def _slot(self, name):
        s = self.slots.get(name)
        if s is None:
            s = [None, {}]
            self.slots[name] = s
        return s

    def op(self, stream, fn, reads=(), writes=(), semkey=None, nparts=0, extra_deps=()):
        o = Op(stream, fn, semkey, nparts)
        deps = set(extra_deps)
        okey = semkey if semkey is not None else stream
        for r in reads:
            s = self._slot(r)
            if s[0] is not None:
                deps.add(s[0])
            if r.startswith("ps"):
                for k, rd in s[1].items():
                    if k != okey:
                        deps.add(rd)
        for w in writes:
            s = self._slot(w)
            if s[0] is not None:
                deps.add(s[0])
            for rd in s[1].values():
                deps.add(rd)
        for r in reads:
            self._slot(r)[1][okey] = o
        for w in writes:
            s = self._slot(w)
            s[0] = o
            s[1] = {}
        deps.discard(o)
        o.deps = tuple(deps)
        self.ops.append(o)
        self.streams[stream].append(o)
        if fn is not None:
            self.last[okey] = o
        return o

    def dma(self, stream, fn, semkey, nparts, reads=(), writes=()):
        return self.op(stream, fn, reads, writes, semkey=semkey, nparts=nparts)

    def barrier(self):
        lasts = list(self.last.values())
        for s in self.streams:
            self.op(s, None, extra_deps=lasts)
        self.slots = {k: v for k, v in self.slots.items() if k.startswith("D:")}

    def resolve(self):
        known = {s: {} for s in self.streams}
        seq = {}
        cnt = {}
        for o in self.ops:
            key = o.semkey if o.semkey is not None else o.stream
            cnt[key] = cnt.get(key, 0) + 1
            seq[o] = (key, cnt[key])
        for o in self.ops:
            need = {}
            kn = known[o.stream]
            for d in o.deps:
                key, n = seq[d]
                if d.semkey is None and d.stream == o.stream and not SAME_ENGINE_SYNC[o.stream]:
                    continue
                if kn.get(key, 0) >= n:
                    continue
                if key not in need or seq[need[key]][1] < n:
                    need[key] = d
            o.waits = list(need.values())
            for key, d in need.items():
                kn[key] = seq[d][1]
                d.flag = True
            o.deps = ()
        ccount = {}
        for o in self.ops:
            if o.semkey is not None:
                ccount[o.semkey] = ccount.get(o.semkey, 0) + 16 * o.nparts
                o.count = ccount[o.semkey]
            elif o.flag:
                ccount[o.stream] = ccount.get(o.stream, 0) + 1
                o.count = ccount[o.stream]
        return ccount

    def emit(self, nc, stack):
        ccount = self.resolve()
        sems = {}
        for i, key in enumerate(ccount):
            sems[key] = stack.enter_context(nc.semaphore("s%d" % i))
        engmap = {"pe": "tensor", "act": "scalar", "dve": "vector", "pool": "gpsimd", "sp": "sync"}
        block = stack.enter_context(nc.Block())
        for sname, ops in self.streams.items():
            if not ops:
                continue

            def body(e, ops=ops, sname=sname):
                for o in ops:
                    for d in o.waits:
                        key = d.semkey if d.semkey is not None else d.stream
                        e.wait_ge(sems[key], d.count)
                    if o.fn is None:
                        continue
                    if o.semkey is not None:
                        o.fn(e, sems[o.semkey])
                    else:
                        ins = o.fn(e)
                        if o.flag:
                            ins.then_inc(sems[sname], 1)

            getattr(block, engmap[sname])(body)
        return ccount

    def final_wait(self, stream="sp"):
        self.op(stream, None, extra_deps=list(self.last.values()))


class Arena:
    def __init__(self, base_ap, nwords):
        self.base = base_ap
        self.nwords = nwords
        self.off = 0

    def reset(self):
        self.off = 0

    def alloc(self, shape, dt):
        n = 1
        for s in shape[1:]:
            n *= s
        nb = n * (2 if dt == BF16 else 4)
        n4 = (nb + 3) // 4
        n4 = (n4 + 7) // 8 * 8
        assert self.off + n4 <= self.nwords, ("SBUF arena overflow", self.off + n4, self.nwords)
        a = self.base[:, self.off:self.off + n4]
        self.off += n4
        if dt == BF16:
            a = a.bitcast(BF16)
        a = a[:, 0:n]
        if len(shape) == 3:
            a = a.rearrange("p (a b) -> p a b", b=shape[2])
        elif len(shape) == 4:
            a = a.rearrange("p (a b c) -> p a b c", b=shape[2], c=shape[3])
        return a


def na_specials(S, nact):
    nb = S // 128
    half = nb // 2
    if nact == S:
        sp = {0, 1, nb - 2, nb - 1}
    else:
        sp = {0, 1, nb - 2, nb - 1, half - 2, half - 1, half, half + 1}
    return sorted(x for x in sp if 0 <= x < nb)


def build_program(slots, debug=False, stop_after=None):
    nc = bass.Bass("TRN2", target_bir_lowering=False)
    T = sum(S for S, _ in slots)
    TA = sum(a for _, a in slots)
    NCH = T // 512
    slot_tok0 = np.cumsum([0] + [S for S, _ in slots]).tolist()
    slot_out0 = np.cumsum([0] + [a for _, a in slots]).tolist()
    nsp_tot = sum(len(na_specials(S, a)) for S, a in slots)

    def din(name, shape, dt=F32):
        return nc.dram_tensor(name, list(shape), dt, kind="ExternalInput").ap()

    skind = "ExternalOutput" if debug else "Internal"

    def dscr(name, shape, dt):
        return nc.dram_tensor(name, list(shape), dt, kind=skind).ap()

    x_in = din("x_in", [T, D])
    y_out = nc.dram_tensor("y_out", [TA, D], F32, kind="ExternalOutput").ap()
    win = din("win", [L, D, NIN])
    wuq = din("wuq", [L, 384, 1152])
    wukv = din("wukv", [L, 256, 768])
    wout = din("wout", [L, D, D])
    wg = din("wg", [L, D, DFF])
    wu = din("wu", [L, D, DFF])
    wd = din("wd", [L, DFF, D])
    gsm = din("gsm", [L, 128, 17])
    gbc = din("gbc", [L, 2, D])
    gfin = din("gfin", [D])
    ropeC = din("ropeC", [2, 128, T])
    ropeB = din("ropeB", [2, 32, T])
    nab_int = din("nab_int", [L, 128, 5, 512])
    nab_sp = din("nab_sp", [L, nsp_tot * 7, 128, 512])
    ident_d = din("ident", [128, 128], BF16)
    bdones_d = din("bdones", [128, 128], BF16)

    x1 = dscr("x1", [T, D], F32)
    xmid = dscr("xmid", [T, D], F32)
    qaT = dscr("qaT", [256, T], BF16)
    kaT = dscr("kaT", [256, T], BF16)
    va = dscr("va", [T, 4, 65], BF16)
    qbT = dscr("qbT", [6, 96, T], BF16)
    kbT = dscr("kbT", [6, 96, T], BF16)
    vb = dscr("vb", [T, 6, 65], BF16)
    qcT = dscr("qcT", [384, T], BF16)
    kcT = dscr("kcT", [128, T], BF16)
    vc = dscr("vc", [T, 2, 65], BF16)
    oT = dscr("oT", [D, T], F32)

    P = Prog()
    st = contextlib.ExitStack()
    AW = 51200
    arena_t = st.enter_context(nc.sbuf_tensor("arena", [128, AW], F32))
    A = Arena(arena_t[:], AW)
    psall_t = st.enter_context(nc.psum_tensor("psall", [128, 4096], F32))
    psall = psall_t[:]
    PS = [psall[:, 512 * i:512 * (i + 1)] for i in range(8)]
    PSB = [psall[:, 512 * i:512 * (i + 1)].bitcast(BF16) for i in range(8)]

    dq = ["sp"]

    def dma(fn, key, n, reads=(), writes=(), q=None):
        return P.dma(q or "sp", fn, "q_" + key, n, reads, writes)

    def simple_dma(out, in_, key, reads=(), writes=(), q=None):
        return dma(lambda e, s: e.dma_start(out=out, in_=in_).then_inc(s, 16), key, 1, reads, writes, q)

    def mm(out, lhsT, rhs, start, stop, reads, writes):
        return P.op("pe", lambda e: e.matmul(out, lhsT=lhsT, rhs=rhs, start=start, stop=stop), reads, writes)

    def act(out, in_, func, reads, writes, **kw):
        return P.op("act", lambda e: e.activation(out=out, in_=in_, func=func, **kw), reads, writes)

    def vcopy(eng, out, in_, reads, writes):
        if eng == "act":
            return act(out, in_, AF.Copy, reads, writes)
        return P.op(eng, lambda e: e.tensor_copy(out=out, in_=in_), reads, writes)

    def tt(eng, out, in0, in1, op, reads, writes):
        return P.op(eng, lambda e: e.tensor_tensor(out=out, in0=in0, in1=in1, op=op), reads, writes)

    def stt(eng, out, in0, scalar, in1, op0, op1, reads, writes):
        return P.op(eng, lambda e: e.scalar_tensor_tensor(out=out, in0=in0, scalar=scalar, in1=in1, op0=op0, op1=op1),
                    reads, writes)

    consts = A.alloc([128, 8], F32)
    ident = A.alloc([128, 128], BF16)
    bdones = A.alloc([128, 128], BF16)
    ones_bf = A.alloc([128, 128], BF16)
    sel_b = A.alloc([128, 64], BF16)
    gfin_b = None
    GLOBAL_OFF = None

    P.op("dve", lambda e: e.memset(consts[:, 0:1], EPS), writes=["consts"])
    P.op("dve", lambda e: e.memset(consts[:, 1:2], 1.0), writes=["consts"])
    P.op("dve", lambda e: e.memset(ones_bf, 1.0), writes=["ones_bf"])
    P.op("dve", lambda e: e.memset(sel_b, 0.0), writes=["sel_b"])
    P.op("dve", lambda e: e.memset(sel_b[64:65, :], 1.0), writes=["sel_b"])
    simple_dma(ident, ident_d, "c0", writes=["ident"])
    simple_dma(bdones, bdones_d, "c1", writes=["bdones"])
    GLOBAL_OFF = A.off

    def eps_col(ap):
        b = ap.base_partition()
        return consts[b:b + ap.shape[0], 0:1]

    def rsqrt(ap, n, src, reads, slot):
        act(ap, src, AF.Sqrt, list(reads) + ["consts"], [slot], scale=1.0 / n, bias=eps_col(ap))
        P.op("dve", lambda e: e.reciprocal(out=ap, in_=ap), [slot], [slot])

    def chunk_slot(c):
        t = 512 * c
        for i in range(len(slots)):
            if slot_tok0[i] <= t < slot_tok0[i + 1]:
                return i
        raise AssertionError

    def chunk_active(c, l):
        if l == 0:
            return True
        i = chunk_slot(c)
        return 512 * c - slot_tok0[i] < slots[i][1]

    def out_row0(c):
        i = chunk_slot(c)
        return slot_out0[i] + 512 * c - slot_tok0[i]

    def phase_A(l):
        P.barrier()
        A.off = GLOBAL_OFF
        x_src = x_in if l == 0 else x1
        win_sb = A.alloc([128, 8, NIN], BF16)
        wuq_sb = A.alloc([128, 3, 1152], BF16)
        wukv_sb = A.alloc([128, 2, 768], BF16)
        gsm_sb = A.alloc([128, 17], F32)
        gA = A.alloc([128, D], F32)
        xa = [A.alloc([128, 4, D], F32) for _ in range(2)]
        junk = A.alloc([128, D], F32)
        ss = [A.alloc([128, 4], F32) for _ in range(2)]
        hb = [A.alloc([128, D], BF16) for _ in range(2)]
        hT = [A.alloc([128, 8, 512], BF16) for _ in range(2)]
        rC = [A.alloc([128, 2, 512], F32) for _ in range(2)]
        rB = [A.alloc([128, 2, 512], F32) for _ in range(2)]
        NSTG = 8
        stg = [A.alloc([128, 512], BF16) for _ in range(NSTG)]
        cqf = A.alloc([128, 5, 512], F32)
        sqb = [A.alloc([128, 512], BF16) for _ in range(2)]
        cn = A.alloc([128, 5, 512], BF16)
        rs = [A.alloc([128, 512], F32) for _ in range(2)]
        tA = [A.alloc([128, 512], F32) for _ in range(2)]
        tB = [A.alloc([128, 512], F32) for _ in range(2)]
        vst = A.alloc([128, 4, 6, 65], BF16)
        vsb = A.alloc([128, 4, 6, 65], BF16)
        kpe_s = A.alloc([128, 512], BF16)

        for k in range(8):
            dma(lambda e, s, k=k: e.dma_start(out=win_sb[:, k, :], in_=win[l, 128 * k:128 * (k + 1), :]).then_inc(s, 16),
                "w_in%d" % k, 1, writes=["win_sb"], q="pool")
        dma(lambda e, s: e.dma_start(out=wuq_sb, in_=wuq[l].rearrange("(k p) n -> p k n", p=128)).then_inc(s, 16),
            "w_uq", 1, writes=["wuq_sb"], q="pool")
        dma(lambda e, s: e.dma_start(out=wukv_sb, in_=wukv[l].rearrange("(k p) n -> p k n", p=128)).then_inc(s, 16),
            "w_ukv", 1, writes=["wukv_sb"], q="pool")
        simple_dma(gsm_sb, gsm[l], "g0", writes=["gsm_sb"])
        simple_dma(gA, gbc[l, 0].partition_broadcast(128), "g1", writes=["gA"])
        P.op("pool", lambda e: e.memset(vst, 1.0), writes=["vst"])
        P.op("pool", lambda e: e.memset(vsb, 1.0), writes=["vsb"])

        bank_rr = [0]

        def nbank():
            b = 2 + bank_rr[0] % 4
            bank_rr[0] += 1
            return b

        stg_rr = [0]

        def nstg():
            i = stg_rr[0] % NSTG
            stg_rr[0] += 1
            return i

        ev_rr = [0]

        def ev_eng():
            ev_rr[0] += 1
            return "act" if ev_rr[0] % 2 else "dve"

        def front(c):
            i = c % 2
            t0 = 512 * c
            dma(lambda e, s: e.dma_start(out=xa[i], in_=x_src[t0:t0 + 512, :].rearrange("(b p) d -> p b d", p=128)).then_inc(s, 16),
                "xa%d" % i, 1, reads=["D:x%d:%d" % (l, c)], writes=["xa%d" % i])
            dma(lambda e, s: e.dma_start(out=rC[i], in_=ropeC[:, :, t0:t0 + 512].rearrange("a p t -> p a t")).then_inc(s, 16),
                "rC%d" % i, 1, writes=["rC%d" % i])
            dma(lambda e, s: e.dma_start(out=rB[i][64:96], in_=ropeB[:, :, t0:t0 + 512].rearrange("a p t -> p a t")).then_inc(s, 16),
                "rB%d" % i, 1, writes=["rB%d" % i])
            for b in range(4):
                act(junk, xa[i][:, b, :], AF.Square, ["xa%d" % i], ["junk", "ss%d" % i], accum_out=ss[i][:, b:b + 1])
            rsqrt(ss[i], D, ss[i], ["ss%d" % i], "ss%d" % i)
            for b in range(4):
                j = b % 2
                stt("dve", hb[j], xa[i][:, b, :], ss[i][:, b:b + 1], gA, ALU.mult, ALU.mult,
                    ["xa%d" % i, "ss%d" % i, "gA"], ["hb%d" % j])
                for k in range(8):
                    P.op("pe", lambda e, k=k, j=j: e.transpose(out=PSB[j][:, 128 * k:128 * (k + 1)], in_=hb[j][:, 128 * k:128 * (k + 1)], identity=ident),
                         ["hb%d" % j, "ident"], ["ps%d" % j])
                vcopy("act" if b % 2 else "dve", hT[i][:, :, 128 * b:128 * (b + 1)],
                      PSB[j].rearrange("p (k t) -> p k t", t=128), ["ps%d" % j], ["hT%d" % i])

        def fm(i, col0, M, wsb=None, nk=8, rhs_fn=None, rslot=None):
            wsb = win_sb if wsb is None else wsb
            b = nbank()
            for k in range(nk):
                rhs = hT[i][:, k, :] if rhs_fn is None else rhs_fn(k)
                mm(PS[b][0:M, :], wsb[:, k, col0:col0 + M], rhs, k == 0, k == nk - 1,
                   (list(rslot) if rslot else ["hT%d" % i]) + ["win_sb", "wuq_sb", "wukv_sb"], ["ps%d" % b])
            return b

        def store_fm(ap_sb, dst, key_i, c, name, npart=128):
            simple_dma(dst, ap_sb, "stg%d" % key_i, reads=["stg%d" % key_i], writes=["D:%s:%d" % (name, c)])

        def back(c):
            i = c % 2
            t0 = 512 * c
            ts = slice(t0, t0 + 512)
            hs = "hT%d" % i
            for name, col0, dst in (("qaT", C_QA, qaT), ("kaT", C_KA, kaT)):
                for j in range(2):
                    b = fm(i, col0 + 128 * j, 128)
                    si = nstg()
                    vcopy(ev_eng(), stg[si], PS[b], ["ps%d" % b], ["stg%d" % si])
                    store_fm(stg[si], dst[128 * j:128 * (j + 1), ts], si, c, name)
            if CUT == 3:
                return
            for (col0, nchk, off, gcol, nfeat, nb_) in ((C_CQ, 3, 0, 0, 384, 6), (C_CKV, 2, 3, 3, 256, 7)):
                for j in range(nchk):
                    b = fm(i, col0 + 128 * j, 128)
                    q = j % 2
                    act(sqb[q], PS[b], AF.Square, ["ps%d" % b], ["sqb%d" % q])
                    vcopy("dve", cqf[:, off + j, :], PS[b], ["ps%d" % b], ["cqf%d" % (off + j)])
                    mm(PS[nb_], ones_bf, sqb[q], j == 0, j == nchk - 1, ["sqb%d" % q, "ones_bf"], ["ps%d" % nb_])
                r = (off // 3) % 2
                rsqrt(rs[r], nfeat, PS[nb_], ["ps%d" % nb_], "rs%d" % r)
                for j in range(nchk):
                    stt("dve", cn[:, off + j, :], cqf[:, off + j, :], gsm_sb[:, gcol + j:gcol + j + 1], rs[r], ALU.mult, ALU.mult,
                        ["cqf%d" % (off + j), "gsm_sb", "rs%d" % r], ["cn%d" % (off + j)])
            if CUT == 4:
                return
            bA = fm(i, C_KPA, 96)
            bB = fm(i, C_KPB, 96)
            tt("dve", tA[0][64:96], PS[bA][64:96], rB[i][64:96, 0, :], ALU.mult, ["ps%d" % bA, "rB%d" % i], ["tA0"])
            tt("dve", tB[0][64:96], PS[bB][64:96], rB[i][64:96, 1, :], ALU.mult, ["ps%d" % bB, "rB%d" % i], ["tB0"])
            tt("dve", kpe_s[64:96], tA[0][64:96], tB[0][64:96], ALU.add, ["tA0", "tB0"], ["kpe_s"])
            dma(lambda e, s: [e.dma_start(out=kbT[h, 64:96, ts], in_=kpe_s[64:96]).then_inc(s, 16) for h in range(6)],
                "kpe", 6, reads=["kpe_s"], writes=["D:kbTp:%d" % c])
            if CUT == 5:
                return
            for (colA, colB, gc, name, dst, row0) in ([(C_QCA + 128 * j, C_QCB + 128 * j, 5, "qcT", qcT, 128 * j) for j in range(3)]
                                                      + [(C_KCA, C_KCB, 7, "kcT", kcT, 0)]):
                bA = fm(i, colA, 128)
                bB = fm(i, colB, 128)
                q = ev_rr[0] % 2
                ev_rr[0] += 1
                act(sqb[q], PS[bA], AF.Square, ["ps%d" % bA], ["sqb%d" % q])
                act(tA[q], PS[bA], AF.Copy, ["ps%d" % bA, "gsm_sb"], ["tA%d" % q], scale=gsm_sb[:, gc:gc + 1])
                act(tB[q], PS[bB], AF.Copy, ["ps%d" % bB, "gsm_sb"], ["tB%d" % q], scale=gsm_sb[:, gc + 1:gc + 2])
                nb_ = 6 + q
                mm(PS[nb_], bdones, sqb[q], True, True, ["sqb%d" % q, "bdones"], ["ps%d" % nb_])
                rsqrt(rs[q], 64, PS[nb_], ["ps%d" % nb_], "rs%d" % q)
                tt("pool", tB[q], tB[q], rC[i][:, 1, :], ALU.mult, ["tB%d" % q, "rC%d" % i], ["tB%d" % q])
                tt("dve", tA[q], tA[q], rC[i][:, 0, :], ALU.mult, ["tA%d" % q, "rC%d" % i], ["tA%d" % q])
                tt("dve", tA[q], tA[q], tB[q], ALU.add, ["tA%d" % q, "tB%d" % q], ["tA%d" % q])
                si = nstg()
                tt("dve", stg[si], tA[q], rs[q], ALU.mult, ["tA%d" % q, "rs%d" % q], ["stg%d" % si])
                store_fm(stg[si], dst[row0:row0 + 128, ts], si, c, name)
            if CUT == 6:
                return
            for h in range(6):
                bA = fm(i, 192 * h, 96, wsb=wuq_sb, nk=3, rhs_fn=lambda k: cn[:, k, :], rslot=("cn0", "cn1", "cn2"))
                bB = fm(i, 192 * h + 96, 96, wsb=wuq_sb, nk=3, rhs_fn=lambda k: cn[:, k, :], rslot=("cn0", "cn1", "cn2"))
                si = nstg()
                q = h % 2
                vcopy("act", stg[si][0:64], PS[bA][0:64], ["ps%d" % bA], ["stg%d" % si])
                tt("dve", tA[q][64:96], PS[bA][64:96], rB[i][64:96, 0, :], ALU.mult, ["ps%d" % bA, "rB%d" % i], ["tA%d" % q])
                tt("dve", tB[q][64:96], PS[bB][64:96], rB[i][64:96, 1, :], ALU.mult, ["ps%d" % bB, "rB%d" % i], ["tB%d" % q])
                tt("dve", stg[si][64:96], tA[q][64:96], tB[q][64:96], ALU.add, ["tA%d" % q, "tB%d" % q], ["stg%d" % si])
                simple_dma(qbT[h, :, ts], stg[si][0:96], "stg%d" % si, reads=["stg%d" % si], writes=["D:qbT:%d" % c])
            if CUT == 7:
                return
            for h in range(6):
                b = fm(i, 64 * h, 64, wsb=wukv_sb, nk=2, rhs_fn=lambda k: cn[:, 3 + k, :], rslot=("cn3", "cn4"))
                si = nstg()
                vcopy(ev_eng(), stg[si][0:64], PS[b][0:64], ["ps%d" % b], ["stg%d" % si])
                simple_dma(kbT[h, 0:64, ts], stg[si][0:64], "stg%d" % si, reads=["stg%d" % si], writes=["D:kbTn:%d" % c])
            if CUT == 8:
                return
            for b4 in range(4):
                b = nbank()
                for k in range(8):
                    mm(PS[b][:, 0:384], hT[i][:, k, 128 * b4:128 * (b4 + 1)], win_sb[:, k, C_V:C_V + 384], k == 0, k == 7,
                       [hs, "win_sb"], ["ps%d" % b])
                vcopy(ev_eng(), vst[:, b4, :, 0:64], PS[b][:, 0:384].rearrange("p (h d) -> p h d", d=64), ["ps%d" % b], ["vst"])
                b = nbank()
                for k in range(2):
                    mm(PS[b][:, 0:384], cn[:, 3 + k, 128 * b4:128 * (b4 + 1)], wukv_sb[:, k, 384:768], k == 0, k == 1,
                       ["cn3", "cn4", "wukv_sb"], ["ps%d" % b])
                vcopy(ev_eng(), vsb[:, b4, :, 0:64], PS[b][:, 0:384].rearrange("p (h d) -> p h d", d=64), ["ps%d" % b], ["vsb"])
            simple_dma(va[ts].rearrange("(b p) h d -> p b h d", p=128), vst[:, :, 0:4, :], "vst_a", reads=["vst"], writes=["D:va:%d" % c])
            simple_dma(vc[ts].rearrange("(b p) h d -> p b h d", p=128), vst[:, :, 4:6, :], "vst_c", reads=["vst"], writes=["D:vc:%d" % c])
            simple_dma(vb[ts].rearrange("(b p) h d -> p b h d", p=128), vsb, "vsb", reads=["vsb"], writes=["D:vb:%d" % c])

        import os
        CUT = int(os.environ.get("K_CUT", "0"))
        if CUT == 1:
            return
        front(0)
        if CUT == 2:
            return
        for c in range(NCH):
            if c + 1 < NCH:
                front(c + 1)
            back(c)
            if CUT:
                return

    def attn_epilogue(acc_bank, bc_bank, ebuf, row0, t0, ntok, c_list, tag):
        osb, rc, hi, lo = ebuf
        vcopy("dve", osb[0:65, 0:ntok], PS[acc_bank][0:65, 0:ntok], ["ps%d" % acc_bank], [tag + "osb"])
        P.op("dve", lambda e: e.reciprocal(out=rc[64:65, 0:ntok], in_=osb[64:65, 0:ntok]), [tag + "osb"], [tag + "rc"])
        vcopy("dve", hi[64:65, 0:ntok], rc[64:65, 0:ntok], [tag + "rc"], [tag + "hi"])
        tt("dve", lo[64:65, 0:ntok], rc[64:65, 0:ntok], hi[64:65, 0:ntok], ALU.subtract, [tag + "rc", tag + "hi"], [tag + "lo"])
        mm(PS[bc_bank][0:64, 0:ntok], sel_b, hi[:, 0:ntok], True, False, [tag + "hi", "sel_b"], ["ps%d" % bc_bank])
        mm(PS[bc_bank][0:64, 0:ntok], sel_b, lo[:, 0:ntok], False, True, [tag + "lo", "sel_b"], ["ps%d" % bc_bank])
        tt("dve", osb[0:64, 0:ntok], osb[0:64, 0:ntok], PS[bc_bank][0:64, 0:ntok], ALU.mult,
           [tag + "osb", "ps%d" % bc_bank], [tag + "osb"])
        simple_dma(oT[row0:row0 + 64, t0:t0 + ntok], osb[0:64, 0:ntok], tag + "osb", reads=[tag + "osb"],
                   writes=["D:oT%d:%d" % (row0, c) for c in c_list])

    def phase_B_dense(l):
        P.barrier()
        A.off = GLOBAL_OFF
        SM = max(S for S, _ in slots)
        NB = SM // 128
        kbuf = [A.alloc([128, SM], BF16) for _ in range(2)]
        vbuf = [A.alloc([128, NB, 65], BF16) for _ in range(2)]
        qbuf = [A.alloc([128, SM], BF16) for _ in range(2)]
        NPT = 3
        pt = [A.alloc([128, 1024], BF16) for _ in range(NPT)]
        ebufs = [(A.alloc([128, 512], F32), A.alloc([128, 512], F32), A.alloc([128, 512], BF16), A.alloc([128, 512], BF16))
                 for _ in range(2)]
        for i_, eb_ in enumerate(ebufs):
            P.op("pool", lambda e, eb_=eb_: e.memset(eb_[2], 0.0), writes=["e%dhi" % i_])
            P.op("pool", lambda e, eb_=eb_: e.memset(eb_[3], 0.0), writes=["e%dlo" % i_])
        work = []
        for si, (S, nact) in enumerate(slots):
            na = S if l == 0 else nact
            for h in range(6):
                work.append((si, "b", h, [h], na))
            for kv in range(2):
                work.append((si, "c", kv, [3 * kv, 3 * kv + 1, 3 * kv + 2], na))
        qjobs = []
        for wi, (si, kind, kv, qhs, na) in enumerate(work):
            for qh in qhs:
                qjobs.append((wi, qh))
        for i_ in range(2):
            P.op("pool", lambda e, i_=i_: e.memset(kbuf[i_], 0.0), writes=["kbuf%d" % i_])
            P.op("pool", lambda e, i_=i_: e.memset(qbuf[i_], 0.0), writes=["qbuf%d" % i_])

        def load_kv(wi):
            si, kind, kv, qhs, na = work[wi]
            S = slots[si][0]
            t0 = slot_tok0[si]
            i = wi % 2
            cl = range(t0 // 512, (t0 + S) // 512)
            if kind == "b":
                simple_dma(kbuf[i][0:96, 0:S], kbT[kv, :, t0:t0 + S], "kb%d" % i,
                           reads=["D:kbTn:%d" % c for c in cl] + ["D:kbTp:%d" % c for c in cl], writes=["kbuf%d" % i])
                simple_dma(vbuf[i][:, 0:S // 128, :], vb[t0:t0 + S, kv, :].rearrange("(b p) d -> p b d", p=128), "vb%d" % i,
                           reads=["D:vb:%d" % c for c in cl], writes=["vbuf%d" % i])
            else:
                P.op("pool", lambda e: e.memset(kbuf[i][64:128, 0:S], 0.0), writes=["kbuf%d" % i])
                simple_dma(kbuf[i][0:64, 0:S], kcT[64 * kv:64 * kv + 64, t0:t0 + S], "kb%d" % i,
                           reads=["D:kcT:%d" % c for c in cl], writes=["kbuf%d" % i])
                simple_dma(vbuf[i][:, 0:S // 128, :], vc[t0:t0 + S, kv, :].rearrange("(b p) d -> p b d", p=128), "vb%d" % i,
                           reads=["D:vc:%d" % c for c in cl], writes=["vbuf%d" % i])

        def load_q(qi):
            wi, qh = qjobs[qi]
            si, kind, kv, qhs, na = work[wi]
            t0 = slot_tok0[si]
            i = qi % 2
            cl = range(t0 // 512, (t0 + na) // 512)
            if kind == "b":
                simple_dma(qbuf[i][0:96, 0:na], qbT[qh, :, t0:t0 + na], "qb%d" % i,
                           reads=["D:qbT:%d" % c for c in cl], writes=["qbuf%d" % i])
            else:
                simple_dma(qbuf[i][0:64, 0:na], qcT[64 * qh:64 * qh + 64, t0:t0 + na], "qb%d" % i,
                           reads=["D:qcT:%d" % c for c in cl], writes=["qbuf%d" % i])

        load_kv(0)
        load_q(0)
        it = [0]
        nacc = [0]
        for qi, (wi, qh) in enumerate(qjobs):
            si, kind, kv, qhs, na = work[wi]
            S = slots[si][0]
            t0 = slot_tok0[si]
            if qi + 1 < len(qjobs):
                nwi = qjobs[qi + 1][0]
                if nwi != wi:
                    load_kv(nwi)
                load_q(qi + 1)
            ki = wi % 2
            qb_ = qi % 2
            dk = 96 if kind == "b" else 128
            scale = 1.0 / math.sqrt(96 if kind == "b" else 64)
            row0 = (256 + 64 * qh) if kind == "b" else (640 + 64 * qh)
            nkb = S // 128
            npair = nkb // 2
            for qc in range(na // 512):
                ab = 4 + nacc[0] % 2
                bb = 6 + nacc[0] % 2
                eb = ebufs[nacc[0] % 2]
                etag = "e%d" % (nacc[0] % 2)
                nacc[0] += 1
                pend = []

                def qk(kp):
                    n = it[0]
                    it[0] += 1
                    sp_ = n % 2
                    pi = n % NPT
                    for u in range(2):
                        kb = 2 * kp + u
                        mm(PS[2 * sp_ + u], kbuf[ki][0:dk, 128 * kb:128 * (kb + 1)], qbuf[qb_][0:dk, 512 * qc:512 * (qc + 1)], True, True,
                           ["kbuf%d" % ki, "qbuf%d" % qb_], ["ps%d" % (2 * sp_ + u)])
                    act(pt[pi], psall[:, 1024 * sp_:1024 * (sp_ + 1)], AF.Exp, ["ps%d" % (2 * sp_), "ps%d" % (2 * sp_ + 1)],
                        ["pt%d" % pi], scale=scale)
                    return pi

                def pv(kp, pi):
                    for u in range(2):
                        kb = 2 * kp + u
                        mm(PS[ab][0:65, :], vbuf[ki][:, kb, :], pt[pi][:, 512 * u:512 * (u + 1)], kb == 0, kb == nkb - 1,
                           ["vbuf%d" % ki, "pt%d" % pi], ["ps%d" % ab])

                LOOK = 1
                for kp in range(npair + LOOK):
                    if kp < npair:
                        pend.append(qk(kp))
                    if kp >= LOOK:
                        pv(kp - LOOK, pend[kp - LOOK])
                tq = t0 + 512 * qc
                attn_epilogue(ab, bb, eb, row0, tq, 512, [tq // 512], etag)

    def phase_B_na(l):
        import os
        NACUT = int(os.environ.get("K_NACUT", "0"))
        P.barrier()
        A.off = GLOBAL_OFF
        SM = max(S for S, _ in slots)
        NB = SM // 128
        qa_sb = A.alloc([128, 2, SM], BF16)
        ka_sb = A.alloc([128, 2, SM], BF16)
        va_sb = A.alloc([128, NB, 4 * 65], BF16)
        bint = A.alloc([128, 5, 512], F32)
        bsp = [A.alloc([128, 512], F32) for _ in range(3)]
        sbias = [A.alloc([128, 512], F32) for _ in range(2)]
        pt = [A.alloc([128, 512], BF16) for _ in range(3)]
        ebufs = [(A.alloc([128, 512], F32), A.alloc([128, 512], F32), A.alloc([128, 512], BF16), A.alloc([128, 512], BF16))
                 for _ in range(2)]
        for i_, eb_ in enumerate(ebufs):
            P.op("pool", lambda e, eb_=eb_: e.memset(eb_[2], 0.0), writes=["e%dhi" % i_])
            P.op("pool", lambda e, eb_=eb_: e.memset(eb_[3], 0.0), writes=["e%dlo" % i_])
        simple_dma(bint, nab_int[l], "bint", writes=["bint"])
        sp_base = 0
        it = [0]
        spn = [0]
        nacc = [0]
        for si, (S, nact) in enumerate(slots):
            na = S if l == 0 else nact
            t0 = slot_tok0[si]
            nb = S // 128
            cl = range(t0 // 512, (t0 + S) // 512)
            specials = na_specials(S, nact)
            simple_dma(qa_sb[:, :, 0:S], qaT[:, t0:t0 + S].rearrange("(j p) t -> p j t", p=128), "qa_sb",
                       reads=["D:qaT:%d" % c for c in cl], writes=["qa_sb"])
            simple_dma(ka_sb[:, :, 0:S], kaT[:, t0:t0 + S].rearrange("(j p) t -> p j t", p=128), "ka_sb",
                       reads=["D:kaT:%d" % c for c in cl], writes=["ka_sb"])
            simple_dma(va_sb[:, 0:nb, :], va[t0:t0 + S].rearrange("(b p) h d -> p b (h d)", p=128), "va_sb",
                       reads=["D:va:%d" % c for c in cl], writes=["va_sb"])
            for g in range(na // 512):
                for qi4 in range(4):
                    i = 4 * g + qi4
                    if i in specials:
                        sidx = sp_base + specials.index(i)
                        offs = [(d, ("sp", sidx * 7 + di)) for di, d in enumerate(NA_OFFS)]
                    else:
                        offs = [(d, ("int", d + 2)) for d in (-2, -1, 0, 1, 2)]
                    for oi, (d, (kind, bidx)) in enumerate(offs):
                        j = (i + d) % nb
                        n = it[0]
                        it[0] += 1
                        bk = (2 * (n % 2), 2 * (n % 2) + 1)
                        for c4 in range(4):
                            hd = HS[c4]
                            pr = slice(64 * (hd % 2), 64 * (hd % 2) + 64)
                            mm(PS[bk[c4 // 2]][:, 128 * (c4 % 2):128 * (c4 % 2 + 1)], ka_sb[pr, hd // 2, 128 * j:128 * (j + 1)],
                               qa_sb[pr, hd // 2, 128 * i:128 * (i + 1)], True, True, ["ka_sb", "qa_sb"], ["ps%d" % bk[c4 // 2]])
                        if kind == "sp":
                            bi = spn[0] % 3
                            spn[0] += 1
                            simple_dma(bsp[bi], nab_sp[l, bidx], "bsp%d" % bi, writes=["bsp%d" % bi])
                            bias_ap, bias_slot = bsp[bi], "bsp%d" % bi
                        else:
                            bias_ap, bias_slot = bint[:, bidx, :], "bint"
                        sbi = n % 2
                        for hf in range(2):
                            stt("dve", sbias[sbi][:, 256 * hf:256 * (hf + 1)], PS[bk[hf]][:, 0:256], 0.125,
                                bias_ap[:, 256 * hf:256 * (hf + 1)], ALU.mult, ALU.add,
                                ["ps%d" % bk[hf], bias_slot], ["sbias%d" % sbi])
                        pi = n % 3
                        act(pt[pi], sbias[sbi], AF.Exp, ["sbias%d" % sbi], ["pt%d" % pi])
                        if NACUT == 2:
                            continue
                        for c4 in range(4):
                            hd = HS[c4]
                            mm(PS[4 + c4][0:65, 128 * qi4:128 * (qi4 + 1)], va_sb[:, j, 65 * hd:65 * (hd + 1)],
                               pt[pi][:, 128 * c4:128 * (c4 + 1)], oi == 0, oi == len(offs) - 1,
                               ["va_sb", "pt%d" % pi], ["ps%d" % (4 + c4)])
                tq = t0 + 512 * g
                if NACUT in (2, 3):
                    continue
                for c4 in range(4):
                    eb = ebufs[nacc[0] % 2]
                    etag = "e%d" % (nacc[0] % 2)
                    bb = 2 * (nacc[0] % 2)
                    nacc[0] += 1
                    attn_epilogue(4 + c4, bb, eb, 64 * HS[c4], tq, 512, [tq // 512], etag)
            sp_base += len(specials)

    def phase_C1(l):
        P.barrier()
        A.off = GLOBAL_OFF
        x_src = x_in if l == 0 else x1
        wo_sb = A.alloc([128, 8, D], BF16)
        gsm_sb = A.alloc([128, 17], F32)
        ot = [A.alloc([128, 8, 512], F32) for _ in range(2)]
        sqb = [A.alloc([128, 512], BF16) for _ in range(3)]
        rs = [A.alloc([128, 512], F32) for _ in range(3)]
        mT = A.alloc([128, 8, 512], BF16)
        xa = [A.alloc([128, D], F32) for _ in range(4)]
        dma(lambda e, s: e.dma_start(out=wo_sb, in_=wout[l].rearrange("(k p) n -> p k n", p=128)).then_inc(s, 16),
            "w_o", 1, writes=["wo_sb"], q="pool")
        simple_dma(gsm_sb, gsm[l], "g0", writes=["gsm_sb"])
        chunks = [c for c in range(NCH) if chunk_active(c, l)]
        groups = ((0, 2, 256), (2, 5, 384), (5, 8, 384))

        def load(ci):
            c = chunks[ci]
            i = ci % 2
            simple_dma(ot[i], oT[:, 512 * c:512 * (c + 1)].rearrange("(j p) t -> p j t", p=128), "ot%d" % i,
                       reads=["D:oT%d:%d" % (r, c) for r in range(0, D, 64)], writes=["ot%d" % i])

        load(0)
        xn = [0]
        for ci, c in enumerate(chunks):
            i = ci % 2
            if ci + 1 < len(chunks):
                load(ci + 1)
            for gi, (j0, j1, nf) in enumerate(groups):
                for j in range(j0, j1):
                    q = j % 3
                    act(sqb[q], ot[i][:, j, :], AF.Square, ["ot%d" % i], ["sqb%d" % q])
                    mm(PS[gi], ones_bf, sqb[q], j == j0, j == j1 - 1, ["sqb%d" % q, "ones_bf"], ["ps%d" % gi])
                rsqrt(rs[gi], nf, PS[gi], ["ps%d" % gi], "rs%d" % gi)
                for j in range(j0, j1):
                    stt("dve", mT[:, j, :], ot[i][:, j, :], gsm_sb[:, 9 + j:10 + j], rs[gi], ALU.mult, ALU.mult,
                        ["ot%d" % i, "gsm_sb", "rs%d" % gi], ["mT"])
            for b4 in range(4):
                xi = xn[0] % 4
                xn[0] += 1
                r0 = 512 * c + 128 * b4
                simple_dma(xa[xi], x_src[r0:r0 + 128, :], "xa%d" % xi, reads=["D:x%d:%d" % (l, c)], writes=["xa%d" % xi])
                for hf in range(2):
                    b = 4 + (2 * b4 + hf) % 4
                    for j in range(8):
                        mm(PS[b], mT[:, j, 128 * b4:128 * (b4 + 1)], wo_sb[:, j, 512 * hf:512 * (hf + 1)], j == 0, j == 7,
                           ["mT", "wo_sb"], ["ps%d" % b])
                    tt("dve", xa[xi][:, 512 * hf:512 * (hf + 1)], xa[xi][:, 512 * hf:512 * (hf + 1)], PS[b], ALU.add,
                       ["xa%d" % xi, "ps%d" % b], ["xa%d" % xi])
                simple_dma(xmid[r0:r0 + 128, :], xa[xi], "xa%d" % xi, reads=["xa%d" % xi], writes=["D:xmid:%d" % c])

    def phase_C2(l):
        P.barrier()
        A.off = GLOBAL_OFF
        wg_sb = A.alloc([128, 8, DFF], BF16)
        wu_sb = A.alloc([128, 8, DFF], BF16)
        wd_sb = A.alloc([128, NFF, D], BF16)
        gF = A.alloc([128, D], F32)
        gL = A.alloc([128, D], F32)
        xm = [A.alloc([128, D], F32) for _ in range(4)]
        junk = A.alloc([128, D], BF16)
        ssv = A.alloc([128, 8], F32)
        hb = [A.alloc([128, D], BF16) for _ in range(2)]
        hT = A.alloc([128, 8, 512], BF16)
        aT = A.alloc([128, NFF, 512], BF16)
        sg = [A.alloc([128, 512], F32) for _ in range(2)]
        for k in range(8):
            dma(lambda e, s, k=k: e.dma_start(out=wg_sb[:, k, :], in_=wg[l, 128 * k:128 * (k + 1), :]).then_inc(s, 16),
                "w_g%d" % k, 1, writes=["wg_sb"], q="pool")
            dma(lambda e, s, k=k: e.dma_start(out=wu_sb[:, k, :], in_=wu[l, 128 * k:128 * (k + 1), :]).then_inc(s, 16),
                "w_u%d" % k, 1, writes=["wu_sb"], q="pool")
        for f0 in range(0, NFF, 4):
            f1 = min(NFF, f0 + 4)
            dma(lambda e, s, f0=f0, f1=f1: e.dma_start(out=wd_sb[:, f0:f1, :], in_=wd[l, 128 * f0:128 * f1, :].rearrange("(k p) n -> p k n", p=128)).then_inc(s, 16),
                "w_d%d" % f0, 1, writes=["wd_sb"], q="pool")
        simple_dma(gF, gbc[l, 1].partition_broadcast(128), "g1", writes=["gF"])
        if l == L - 1:
            simple_dma(gL, gfin.partition_broadcast(128), "g2", writes=["gL"])
        chunks = [c for c in range(NCH) if chunk_active(c, l)]

        def load(ci):
            c = chunks[ci]
            for b4 in range(4):
                r0 = 512 * c + 128 * b4
                simple_dma(xm[b4], xmid[r0:r0 + 128, :], "xm%d" % b4, reads=["D:xmid:%d" % c], writes=["xm%d" % b4])

        load(0)
        for ci, c in enumerate(chunks):
            for b4 in range(4):
                j = b4 % 2
                act(junk, xm[b4], AF.Square, ["xm%d" % b4], ["junk", "ssv"], accum_out=ssv[:, b4:b4 + 1])
                rsqrt(ssv[:, b4:b4 + 1], D, ssv[:, b4:b4 + 1], ["ssv"], "ssv")
                stt("dve", hb[j], xm[b4], ssv[:, b4:b4 + 1], gF, ALU.mult, ALU.mult, ["xm%d" % b4, "ssv", "gF"], ["hb%d" % j])
                for k in range(8):
                    P.op("pe", lambda e, k=k, j=j: e.transpose(out=PSB[j][:, 128 * k:128 * (k + 1)], in_=hb[j][:, 128 * k:128 * (k + 1)], identity=ident),
                         ["hb%d" % j, "ident"], ["ps%d" % j])
                vcopy("act" if b4 % 2 else "dve", hT[:, :, 128 * b4:128 * (b4 + 1)],
                      PSB[j].rearrange("p (k t) -> p k t", t=128), ["ps%d" % j], ["hT"])
            for f in range(NFF):
                bg = 2 + f % 2
                bu = 4 + f % 2
                for k in range(8):
                    mm(PS[bg], wg_sb[:, k, 128 * f:128 * (f + 1)], hT[:, k, :], k == 0, k == 7, ["wg_sb", "hT"], ["ps%d" % bg])
                for k in range(8):
                    mm(PS[bu], wu_sb[:, k, 128 * f:128 * (f + 1)], hT[:, k, :], k == 0, k == 7, ["wu_sb", "hT"], ["ps%d" % bu])
                q = f % 2
                act(sg[q], PS[bg], AF.Silu, ["ps%d" % bg], ["sg%d" % q])
                tt("dve", aT[:, f, :], sg[q], PS[bu], ALU.mult, ["sg%d" % q, "ps%d" % bu], ["aT"])
            for b4 in range(4):
                r0 = 512 * c + 128 * b4
                for hf in range(2):
                    b = 6 + hf
                    for f in range(NFF):
                        mm(PS[b], aT[:, f, 128 * b4:128 * (b4 + 1)], wd_sb[:, f, 512 * hf:512 * (hf + 1)], f == 0, f == NFF - 1,
                           ["aT", "wd_sb"], ["ps%d" % b])
                    tt("dve", xm[b4][:, 512 * hf:512 * (hf + 1)], xm[b4][:, 512 * hf:512 * (hf + 1)], PS[b], ALU.add,
                       ["xm%d" % b4, "ps%d" % b], ["xm%d" % b4])
                if l < L - 1:
                    simple_dma(x1[r0:r0 + 128, :], xm[b4], "xm%d" % b4, reads=["xm%d" % b4], writes=["D:x%d:%d" % (l + 1, c)])
                else:
                    act(junk, xm[b4], AF.Square, ["xm%d" % b4], ["junk", "ssv"], accum_out=ssv[:, 4 + b4:5 + b4])
                    rsqrt(ssv[:, 4 + b4:5 + b4], D, ssv[:, 4 + b4:5 + b4], ["ssv"], "ssv")
                    stt("dve", xm[b4], xm[b4], ssv[:, 4 + b4:5 + b4], gL, ALU.mult, ALU.mult, ["xm%d" % b4, "ssv", "gL"], ["xm%d" % b4])
                    ro = out_row0(c) + 128 * b4
                    simple_dma(y_out[ro:ro + 128, :], xm[b4], "xm%d" % b4, reads=["xm%d" % b4], writes=["D:y:%d" % c])
            if ci + 1 < len(chunks):
                load(ci + 1)

    phases = []
    for l in range(L):
        phases += [("A", l), ("Bna", l), ("Bd", l), ("C1", l), ("C2", l)]
    for name, l in phases:
        {"A": phase_A, "Bna": phase_B_na, "Bd": phase_B_dense, "C1": phase_C1, "C2": phase_C2}[name](l)
        if stop_after == (name, l):
            break
    P.final_wait("sp")
    cc = P.emit(nc, st)
    st.close()
    return nc, (len(P.ops), {k: len(v) for k, v in P.streams.items()}, cc)


def rope_perm(d):
    half = d // 2
    return np.concatenate([np.arange(half, d), np.arange(0, half)])


def rope_tables(pos, d):
    half = d // 2
    inv = (10000.0 ** (-(np.arange(half, dtype=np.float32) * 2.0) / d)).astype(np.float32)
    ang = pos.astype(np.float32)[None, :] * inv[:, None]
    cos = np.cos(ang).astype(np.float32)
    sin = np.sin(ang).astype(np.float32)
    return np.concatenate([cos, cos], 0), np.concatenate([-sin, sin], 0)


def na_bias_tile(rpb_l, rows, tq, tk):
    out = np.full((128, 4, 128), NEG, np.float32)
    nb = rows // 2
    if tk is None or tk < 0 or tk >= nb:
        return out.reshape(128, 512)
    qi = np.arange(128)
    r = 2 * tq + qi // 64
    cq = qi % 64
    ki = np.arange(128)
    kr = 2 * tk + ki // 64
    ck = ki % 64
    kr_n = min(8, rows)
    rs = np.clip(r - kr_n // 2, 0, rows - kr_n)
    cs = np.clip(cq - 8, 0, GRID_W - 16)
    valid = ((kr[:, None] >= rs[None, :]) & (kr[:, None] < rs[None, :] + kr_n)
             & (ck[:, None] >= cs[None, :]) & (ck[:, None] < cs[None, :] + 16))
    dr = np.clip(kr[:, None] - r[None, :] + 7, 0, 14)
    dc = np.clip(ck[:, None] - cq[None, :] + 15, 0, 30)
    vals = rpb_l[:, dr, dc]
    out = np.where(valid[:, None, :], vals.transpose(1, 0, 2), np.float32(NEG)).astype(np.float32)
    return np.ascontiguousarray(out[:, list(HS), :]).reshape(128, 512)


def prep_weights(inp):
    f = lambda a: np.asarray(a, np.float32)
    w_in = f(inp["w_in"])
    Ls = w_in.shape[0]
    segs = np.cumsum([0, 256, 256, 256, 384, 256, 32, 384, 128, 128])
    o_qa, o_ka, o_va, o_cq, o_ckv, o_kpe, o_qc, o_kc, o_vc = segs[:9]
    p32 = rope_perm(32)
    p_ax = np.concatenate([rope_perm(32), 32 + rope_perm(32)])
    win = np.zeros((Ls, D, NIN), np.float32)
    win[:, :, C_QA:C_QA + 256] = w_in[:, :, o_qa:o_qa + 256]
    win[:, :, C_KA:C_KA + 256] = w_in[:, :, o_ka:o_ka + 256]
    win[:, :, C_CQ:C_CQ + 384] = w_in[:, :, o_cq:o_cq + 384]
    win[:, :, C_CKV:C_CKV + 256] = w_in[:, :, o_ckv:o_ckv + 256]
    win[:, :, C_KPA + 64:C_KPA + 96] = w_in[:, :, o_kpe:o_kpe + 32]
    win[:, :, C_KPB + 64:C_KPB + 96] = w_in[:, :, o_kpe + p32]
    qc_sw = np.concatenate([o_qc + 64 * h + p_ax for h in range(6)])
    kc_sw = np.concatenate([o_kc + 64 * h + p_ax for h in range(2)])
    win[:, :, C_QCA:C_QCA + 384] = w_in[:, :, o_qc:o_qc + 384]
    win[:, :, C_QCB:C_QCB + 384] = w_in[:, :, qc_sw]
    win[:, :, C_KCA:C_KCA + 128] = w_in[:, :, o_kc:o_kc + 128]
    win[:, :, C_KCB:C_KCB + 128] = w_in[:, :, kc_sw]
    win[:, :, C_V:C_V + 256] = w_in[:, :, o_va:o_va + 256]
    win[:, :, C_V + 256:C_V + 384] = w_in[:, :, o_vc:o_vc + 128]
    w_uq = f(inp["w_uq"])
    wuq = np.zeros((Ls, 384, 1152), np.float32)
    for h in range(6):
        wuq[:, :, 192 * h:192 * h + 96] = w_uq[:, :, 96 * h:96 * h + 96]
        wuq[:, :, 192 * h + 96:192 * h + 160] = w_uq[:, :, 96 * h:96 * h + 64]
        wuq[:, :, 192 * h + 160:192 * h + 192] = w_uq[:, :, 96 * h + 64 + p32]
    w_ukv = f(inp["w_ukv"])
    wukv = np.zeros((Ls, 256, 768), np.float32)
    for h in range(6):
        wukv[:, :, 64 * h:64 * h + 64] = w_ukv[:, :, 128 * h:128 * h + 64]
        wukv[:, :, 384 + 64 * h:384 + 64 * h + 64] = w_ukv[:, :, 128 * h + 64:128 * h + 128]
    gsm = np.zeros((Ls, 128, 17), np.float32)
    qan = f(inp["q_a_norm"])
    kvn = f(inp["kv_a_norm"])
    qn = f(inp["q_norm_c"])
    kn = f(inp["k_norm_c"])
    gout = np.concatenate([f(inp["g_out_a"]), f(inp["g_out_b"]), f(inp["g_out_c"])], 1)
    for l in range(Ls):
        gsm[l, :, 0:3] = qan[l].reshape(3, 128).T
        gsm[l, :, 3:5] = kvn[l].reshape(2, 128).T
        gsm[l, :, 5] = np.tile(qn[l], 2)
        gsm[l, :, 6] = np.tile(qn[l][p_ax], 2)
        gsm[l, :, 7] = np.tile(kn[l], 2)
        gsm[l, :, 8] = np.tile(kn[l][p_ax], 2)
        gsm[l, :, 9:17] = gout[l].reshape(8, 128).T
    gbc = np.stack([f(inp["attn_norm"]), f(inp["ffn_norm"])], 1)
    bd = np.zeros((128, 128), np.float32)
    bd[:64, :64] = 1
    bd[64:, 64:] = 1
    return dict(win=win, wuq=wuq, wukv=wukv, wout=f(inp["w_out"]), wg=f(inp["w_gate"]), wu=f(inp["w_up"]),
                wd=f(inp["w_down"]), gsm=gsm, gbc=np.ascontiguousarray(gbc), gfin=f(inp["final_norm"]),
                ident=np.eye(128, dtype=np.float32).astype(ml_dtypes.bfloat16), bdones=bd.astype(ml_dtypes.bfloat16))


def prep_core_tables(rpb, slots, true_pos):
    pos = np.concatenate(true_pos)
    row = pos // GRID_W
    col = pos % GRID_W
    cr, sr = rope_tables(row, 32)
    cc, sc = rope_tables(col, 32)
    cosC = np.concatenate([cr, cc], 0)
    sinC = np.concatenate([sr, sc], 0)
    ropeC = np.stack([np.tile(cosC, (2, 1)), np.tile(sinC, (2, 1))], 0).astype(np.float32)
    cb, sb = rope_tables(pos, 32)
    ropeB = np.stack([cb, sb], 0).astype(np.float32)
    Ls = rpb.shape[0]
    nab_int = np.zeros((Ls, 128, 5, 512), np.float32)
    sp_tiles = [[] for _ in range(Ls)]
    for l in range(Ls):
        for di, d in enumerate((-2, -1, 0, 1, 2)):
            nab_int[l, :, di, :] = na_bias_tile(rpb[l], 64, 8, 8 + d)
        for si, (S, nact) in enumerate(slots):
            nb = S // 128
            rows = S // GRID_W
            tb = true_pos[si][::128] // 128
            for i in na_specials(S, nact):
                for d in NA_OFFS:
                    tq = int(tb[i])
                    tk = tq + d
                    j = (i + d) % nb
                    if 0 <= tk < nb:
                        assert int(tb[j]) == tk
                    sp_tiles[l].append(na_bias_tile(rpb[l], rows, tq, tk))
    nab_sp = np.stack([np.stack(t, 0) for t in sp_tiles], 0)
    return ropeC, ropeB, nab_int, nab_sp


_CACHE = {}


def run_slots(slots, per_core_x, per_core_pos, inp, debug=False, stop_after=None, trace=False):
    key = (tuple(slots), debug, stop_after)
    if key not in _CACHE:
        _CACHE[key] = build_program(slots, debug=debug, stop_after=stop_after)
    nc, _ = _CACHE[key]
    W = prep_weights(inp)
    rpb = np.asarray(inp["rpb"], np.float32)
    in_maps = []
    for x, pos in zip(per_core_x, per_core_pos):
        ropeC, ropeB, nab_int, nab_sp = prep_core_tables(rpb, slots, pos)
        m = dict(W)
        m.update(x_in=np.ascontiguousarray(x, dtype=np.float32), ropeC=ropeC, ropeB=ropeB, nab_int=nab_int, nab_sp=nab_sp)
        in_maps.append(m)
    res = run_bass_kernel_spmd(nc, in_maps, core_ids=list(range(len(in_maps))), **({"trace": True} if trace else {}))
    return res


def kernel(**inp):
    xp = np.asarray(inp["x_prompt"], np.float32)
    xs = np.asarray(inp["x_sample"], np.float32)
    B, S, _ = xp.shape
    Bs, Ss, _ = xs.shape
    assert Bs == 2 * 8 and B * 2 == 8
    H = S // 2
    slots = [(Ss, Ss), (Ss, Ss), (S, H)]
    per_x, per_pos = [], []
    for c in range(8):
        p, h = c // 2, c % 2
        xl = np.concatenate([xs[2 * c], xs[2 * c + 1], np.roll(xp[p], -h * H, axis=0)], 0)
        per_x.append(xl)
        per_pos.append([np.arange(Ss), np.arange(Ss), (np.arange(S) + h * H) % S])
    res = run_slots(slots, per_x, per_pos, inp)
    yp = np.zeros_like(xp)
    ys = np.zeros_like(xs)
    for c in range(8):
        y = res.results[c]["y_out"]
        p, h = c // 2, c % 2
        ys[2 * c] = y[0:Ss]
        ys[2 * c + 1] = y[Ss:2 * Ss]
        yp[p, h * H:(h + 1) * H] = y[2 * Ss:2 * Ss + H]
    return yp, ys
```

```python
import contextlib
import math

import ml_dtypes
import numpy as np

import concourse.bass as bass
import concourse.mybir as mybir
from concourse.bass_utils import run_bass_kernel_spmd

F32 = mybir.dt.float32
BF16 = mybir.dt.bfloat16
ALU = mybir.AluOpType
AF = mybir.ActivationFunctionType
AX = mybir.AxisListType

D = 1024
L = 2
GRID_W = 64
DFF = 2816
NFF = DFF // 128
EPS = 1e-6
NEG = -30000.0
NIN = 2752
C_QA, C_KA, C_CQ, C_CKV, C_KPA, C_KPB, C_QCA, C_QCB, C_KCA, C_KCB, C_V = (
    0, 256, 512, 896, 1152, 1248, 1344, 1728, 2112, 2240, 2368)
NA_OFFS = (-3, -2, -1, 0, 1, 2, 3)
HS = (0, 2, 1, 3)

SAME_ENGINE_SYNC = {"pe": False, "act": True, "dve": True, "pool": True, "sp": False}


class Op:
    __slots__ = ("stream", "fn", "deps", "flag", "count", "semkey", "nparts", "waits")

    def __init__(self, stream, fn, semkey=None, nparts=0):
        self.stream = stream
        self.fn = fn
        self.deps = ()
        self.flag = False
        self.count = None
        self.semkey = semkey
        self.nparts = nparts
        self.waits = None


class Prog:
    def __init__(self):
        self.ops = []
        self.slots = {}
        self.streams = {s: [] for s in ("pe", "act", "dve", "pool", "sp")}
        self.last = {}

    def _slot(self, name):
        s = self.slots.get(name)
        if s is None:
            s = [None, {}]
            self.slots[name] = s
        return s

    def op(self, stream, fn, reads=(), writes=(), semkey=None, nparts=0, extra_deps=()):
        o = Op(stream, fn, semkey, nparts)
        deps = set(extra_deps)
        okey = semkey if semkey is not None else stream
        for r in reads:
            s = self._slot(r)
            if s[0] is not None:
                deps.add(s[0])
            if r.startswith("ps"):
                for k, rd in s[1].items():
                    if k != okey:
                        deps.add(rd)
        for w in writes:
            s = self._slot(w)
            if s[0] is not None:
                deps.add(s[0])
            for rd in s[1].values():
                deps.add(rd)
        for r in reads:
            self._slot(r)[1][okey] = o
        for w in writes:
            s = self._slot(w)
            s[0] = o
            s[1] = {}
        deps.discard(o)
        o.deps = tuple(deps)
        self.ops.append(o)
        self.streams[stream].append(o)
        if fn is not None:
            self.last[okey] = o
        return o

    def dma(self, stream, fn, semkey, nparts, reads=(), writes=()):
        return self.op(stream, fn, reads, writes, semkey=semkey, nparts=nparts)

    def barrier(self):
        lasts = list(self.last.values())
        for s in self.streams:
            self.op(s, None, extra_deps=lasts)
        self.slots = {k: v for k, v in self.slots.items() if k.startswith("D:")}

    def resolve(self):
        known = {s: {} for s in self.streams}
        seq = {}
        cnt = {}
        for o in self.ops:
            key = o.semkey if o.semkey is not None else o.stream
            cnt[key] = cnt.get(key, 0) + 1
            seq[o] = (key, cnt[key])
        for o in self.ops:
            need = {}
            kn = known[o.stream]
            for d in o.deps:
                key, n = seq[d]
                if d.semkey is None and d.stream == o.stream and not SAME_ENGINE_SYNC[o.stream]:
                    continue
                if kn.get(key, 0) >= n:
                    continue
                if key not in need or seq[need[key]][1] < n:
                    need[key] = d
            o.waits = list(need.values())
            for key, d in need.items():
                kn[key] = seq[d][1]
                d.flag = True
            o.deps = ()
        ccount = {}
        for o in self.ops:
            if o.semkey is not None:
                ccount[o.semkey] = ccount.get(o.semkey, 0) + 16 * o.nparts
                o.count = ccount[o.semkey]
            elif o.flag:
                ccount[o.stream] = ccount.get(o.stream, 0) + 1
                o.count = ccount[o.stream]
        return ccount

    def emit(self, nc, stack):
        ccount = self.resolve()
        sems = {}
        for i, key in enumerate(ccount):
            sems[key] = stack.enter_context(nc.semaphore("s%d" % i))
        engmap = {"pe": "tensor", "act": "scalar", "dve": "vector", "pool": "gpsimd", "sp": "sync"}
        block = stack.enter_context(nc.Block())
        for sname, ops in self.streams.items():
            if not ops:
                continue

            def body(e, ops=ops, sname=sname):
                for o in ops:
                    for d in o.waits:
                        key = d.semkey if d.semkey is not None else d.stream
                        e.wait_ge(sems[key], d.count)
                    if o.fn is None:
                        continue
                    if o.semkey is not None:
                        o.fn(e, sems[o.semkey])
                    else:
                        ins = o.fn(e)
                        if o.flag:
                            ins.then_inc(sems[sname], 1)

            getattr(block, engmap[sname])(body)
        return ccount

    def final_wait(self, stream="sp"):
        self.op(stream, None, extra_deps=list(self.last.values()))


class Arena:
    def __init__(self, base_ap, nwords):
        self.base = base_ap
        self.nwords = nwords
        self.off = 0

    def reset(self):
        self.off = 0

    def alloc(self, shape, dt):
        n = 1
        for s in shape[1:]:
            n *= s
        nb = n * (2 if dt == BF16 else 4)
        n4 = (nb + 3) // 4
        n4 = (n4 + 7) // 8 * 8
        assert self.off + n4 <= self.nwords, ("SBUF arena overflow", self.off + n4, self.nwords)
        a = self.base[:, self.off:self.off + n4]
        self.off += n4
        if dt == BF16:
            a = a.bitcast(BF16)
        a = a[:, 0:n]
        if len(shape) == 3:
            a = a.rearrange("p (a b) -> p a b", b=shape[2])
        elif len(shape) == 4:
            a = a.rearrange("p (a b c) -> p a b c", b=shape[2], c=shape[3])
        return a


def na_specials(S, nact):
    nb = S // 128
    half = nb // 2
    if nact == S:
        sp = {0, 1, nb - 2, nb - 1}
    else:
        sp = {0, 1, nb - 2, nb - 1, half - 2, half - 1, half, half + 1}
    return sorted(x for x in sp if 0 <= x < nb)


def build_program(slots, debug=False, stop_after=None):
    nc = bass.Bass("TRN2", target_bir_lowering=False)
    T = sum(S for S, _ in slots)
    TA = sum(a for _, a in slots)
    NCH = T // 512
    slot_tok0 = np.cumsum([0] + [S for S, _ in slots]).tolist()
    slot_out0 = np.cumsum([0] + [a for _, a in slots]).tolist()
    nsp_tot = sum(len(na_specials(S, a)) for S, a in slots)

    def din(name, shape, dt=F32):
        return nc.dram_tensor(name, list(shape), dt, kind="ExternalInput").ap()

    skind = "ExternalOutput" if debug else "Internal"

    def dscr(name, shape, dt):
        return nc.dram_tensor(name, list(shape), dt, kind=skind).ap()

    x_in = din("x_in", [T, D])
    y_out = nc.dram_tensor("y_out", [TA, D], F32, kind="ExternalOutput").ap()
    win = din("win", [L, D, NIN])
    wuq = din("wuq", [L, 384, 1152])
    wukv = din("wukv", [L, 256, 768])
    wout = din("wout", [L, D, D])
    wg = din("wg", [L, D, DFF])
    wu = din("wu", [L, D, DFF])
    wd = din("wd", [L, DFF, D])
    gsm = din("gsm", [L, 128, 17])
    gbc = din("gbc", [L, 2, D])
    gfin = din("gfin", [D])
    ropeC = din("ropeC", [2, 128, T])
    ropeB = din("ropeB", [2, 32, T])
    nab_int = din("nab_int", [L, 128, 5, 512])
    nab_sp = din("nab_sp", [L, nsp_tot * 7, 128, 512])
    ident_d = din("ident", [128, 128], BF16)
    bdones_d = din("bdones", [128, 128], BF16)

    x1 = dscr("x1", [T, D], F32)
    xmid = dscr("xmid", [T, D], F32)
    qaT = dscr("qaT", [256, T], BF16)
    kaT = dscr("kaT", [256, T], BF16)
    va = dscr("va", [T, 4, 65], BF16)
    qbT = dscr("qbT", [6, 96, T], BF16)
    kbT = dscr("kbT", [6, 96, T], BF16)
    vb = dscr("vb", [T, 6, 65], BF16)
    qcT = dscr("qcT", [384, T], BF16)
    kcT = dscr("kcT", [128, T], BF16)
    vc = dscr("vc", [T, 2, 65], BF16)
    oT = dscr("oT", [D, T], F32)

    P = Prog()
    st = contextlib.ExitStack()
    AW = 51200
    arena_t = st.enter_context(nc.sbuf_tensor("arena", [128, AW], F32))
    A = Arena(arena_t[:], AW)
    psall_t = st.enter_context(nc.psum_tensor("psall", [128, 4096], F32))
    psall = psall_t[:]
    PS = [psall[:, 512 * i:512 * (i + 1)] for i in range(8)]
    PSB = [psall[:, 512 * i:512 * (i + 1)].bitcast(BF16) for i in range(8)]

    dq = ["sp"]

    def dma(fn, key, n, reads=(), writes=(), q=None):
        return P.dma(q or "sp", fn, "q_" + key, n, reads, writes)

    def simple_dma(out, in_, key, reads=(), writes=(), q=None):
        return dma(lambda e, s: e.dma_start(out=out, in_=in_).then_inc(s, 16), key, 1, reads, writes, q)

    def mm(out, lhsT, rhs, start, stop, reads, writes):
        return P.op("pe", lambda e: e.matmul(out, lhsT=lhsT, rhs=rhs, start=start, stop=stop), reads, writes)

    def act(out, in_, func, reads, writes, **kw):
        return P.op("act", lambda e: e.activation(out=out, in_=in_, func=func, **kw), reads, writes)

    def vcopy(eng, out, in_, reads, writes):
        if eng == "act":
            return act(out, in_, AF.Copy, reads, writes)
        return P.op(eng, lambda e: e.tensor_copy(out=out, in_=in_), reads, writes)

    def tt(eng, out, in0, in1, op, reads, writes):
        return P.op(eng, lambda e: e.tensor_tensor(out=out, in0=in0, in1=in1, op=op), reads, writes)

    def stt(eng, out, in0, scalar, in1, op0, op1, reads, writes):
        return P.op(eng, lambda e: e.scalar_tensor_tensor(out=out, in0=in0, scalar=scalar, in1=in1, op0=op0, op1=op1),
                    reads, writes)

    consts = A.alloc([128, 8], F32)
    ident = A.alloc([128, 128], BF16)
    bdones = A.alloc([128, 128], BF16)
    ones_bf = A.alloc([128, 128], BF16)
    sel_b = A.alloc([128, 64], BF16)
    gfin_b = None
    GLOBAL_OFF = None

    P.op("dve", lambda e: e.memset(consts[:, 0:1], EPS), writes=["consts"])
    P.op("dve", lambda e: e.memset(consts[:, 1:2], 1.0), writes=["consts"])
    P.op("dve", lambda e: e.memset(ones_bf, 1.0), writes=["ones_bf"])
    P.op("dve", lambda e: e.memset(sel_b, 0.0), writes=["sel_b"])
    P.op("dve", lambda e: e.memset(sel_b[64:65, :], 1.0), writes=["sel_b"])
    simple_dma(ident, ident_d, "c0", writes=["ident"])
    simple_dma(bdones, bdones_d, "c1", writes=["bdones"])
    GLOBAL_OFF = A.off

    def eps_col(ap):
        b = ap.base_partition()
        return consts[b:b + ap.shape[0], 0:1]

    def rsqrt(ap, n, src, reads, slot):
        act(ap, src, AF.Sqrt, list(reads) + ["consts"], [slot], scale=1.0 / n, bias=eps_col(ap))
        P.op("dve", lambda e: e.reciprocal(out=ap, in_=ap), [slot], [slot])

    def chunk_slot(c):
        t = 512 * c
        for i in range(len(slots)):
            if slot_tok0[i] <= t < slot_tok0[i + 1]:
                return i
        raise AssertionError

    def chunk_active(c, l):
        if l == 0:
            return True
        i = chunk_slot(c)
        return 512 * c - slot_tok0[i] < slots[i][1]

    def out_row0(c):
        i = chunk_slot(c)
        return slot_out0[i] + 512 * c - slot_tok0[i]

    def phase_A(l):
        P.barrier()
        A.off = GLOBAL_OFF
        x_src = x_in if l == 0 else x1
        win_sb = A.alloc([128, 8, NIN], BF16)
        wuq_sb = A.alloc([128, 3, 1152], BF16)
        wukv_sb = A.alloc([128, 2, 768], BF16)
        gsm_sb = A.alloc([128, 17], F32)
        gA = A.alloc([128, D], F32)
        xa = [A.alloc([128, 4, D], F32) for _ in range(2)]
        junk = A.alloc([128, D], F32)
        ss = [A.alloc([128, 4], F32) for _ in range(2)]
        hb = [A.alloc([128, D], BF16) for _ in range(2)]
        hT = [A.alloc([128, 8, 512], BF16) for _ in range(2)]
        rC = [A.alloc([128, 2, 512], F32) for _ in range(2)]
        rB = [A.alloc([128, 2, 512], F32) for _ in range(2)]
        NSTG = 8
        stg = [A.alloc([128, 512], BF16) for _ in range(NSTG)]
        cqf = A.alloc([128, 5, 512], F32)
        sqb = [A.alloc([128, 512], BF16) for _ in range(2)]
        cn = A.alloc([128, 5, 512], BF16)
        rs = [A.alloc([128, 512], F32) for _ in range(2)]
        tA = [A.alloc([128, 512], F32) for _ in range(2)]
        tB = [A.alloc([128, 512], F32) for _ in range(2)]
        vst = A.alloc([128, 4, 6, 65], BF16)
        vsb = A.alloc([128, 4, 6, 65], BF16)
        kpe_s = A.alloc([128, 512], BF16)

        for k in range(8):
            dma(lambda e, s, k=k: e.dma_start(out=win_sb[:, k, :], in_=win[l, 128 * k:128 * (k + 1), :]).then_inc(s, 16),
                "w_in%d" % k, 1, writes=["win_sb"], q="pool")
        dma(lambda e, s: e.dma_start(out=wuq_sb, in_=wuq[l].rearrange("(k p) n -> p k n", p=128)).then_inc(s, 16),
            "w_uq", 1, writes=["wuq_sb"], q="pool")
        dma(lambda e, s: e.dma_start(out=wukv_sb, in_=wukv[l].rearrange("(k p) n -> p k n", p=128)).then_inc(s, 16),
            "w_ukv", 1, writes=["wukv_sb"], q="pool")
        simple_dma(gsm_sb, gsm[l], "g0", writes=["gsm_sb"])
        simple_dma(gA, gbc[l, 0].partition_broadcast(128), "g1", writes=["gA"])
        P.op("pool", lambda e: e.memset(vst, 1.0), writes=["vst"])
        P.op("pool", lambda e: e.memset(vsb, 1.0), writes=["vsb"])

        bank_rr = [0]

        def nbank():
            b = 2 + bank_rr[0] % 4
            bank_rr[0] += 1
            return b

        stg_rr = [0]

        def nstg():
            i = stg_rr[0] % NSTG
            stg_rr[0] += 1
            return i

        ev_rr = [0]

        def ev_eng():
            ev_rr[0] += 1
            return "act" if ev_rr[0] % 2 else "dve"

        def front(c):
            i = c % 2
            t0 = 512 * c
            dma(lambda e, s: e.dma_start(out=xa[i], in_=x_src[t0:t0 + 512, :].rearrange("(b p) d -> p b d", p=128)).then_inc(s, 16),
                "xa%d" % i, 1, reads=["D:x%d:%d" % (l, c)], writes=["xa%d" % i])
            dma(lambda e, s: e.dma_start(out=rC[i], in_=ropeC[:, :, t0:t0 + 512].rearrange("a p t -> p a t")).then_inc(s, 16),
                "rC%d" % i, 1, writes=["rC%d" % i])
            dma(lambda e, s: e.dma_start(out=rB[i][64:96], in_=ropeB[:, :, t0:t0 + 512].rearrange("a p t -> p a t")).then_inc(s, 16),
                "rB%d" % i, 1, writes=["rB%d" % i])
            for b in range(4):
                act(junk, xa[i][:, b, :], AF.Square, ["xa%d" % i], ["junk", "ss%d" % i], accum_out=ss[i][:, b:b + 1])
            rsqrt(ss[i], D, ss[i], ["ss%d" % i], "ss%d" % i)
            for b in range(4):
                j = b % 2
                stt("dve", hb[j], xa[i][:, b, :], ss[i][:, b:b + 1], gA, ALU.mult, ALU.mult,
                    ["xa%d" % i, "ss%d" % i, "gA"], ["hb%d" % j])
                for k in range(8):
                    P.op("pe", lambda e, k=k, j=j: e.transpose(out=PSB[j][:, 128 * k:128 * (k + 1)], in_=hb[j][:, 128 * k:128 * (k + 1)], identity=ident),
                         ["hb%d" % j, "ident"], ["ps%d" % j])
                vcopy("act" if b % 2 else "dve", hT[i][:, :, 128 * b:128 * (b + 1)],
                      PSB[j].rearrange("p (k t) -> p k t", t=128), ["ps%d" % j], ["hT%d" % i])

        def fm(i, col0, M, wsb=None, nk=8, rhs_fn=None, rslot=None):
            wsb = win_sb if wsb is None else wsb
            b = nbank()
            for k in range(nk):
                rhs = hT[i][:, k, :] if rhs_fn is None else rhs_fn(k)
                mm(PS[b][0:M, :], wsb[:, k, col0:col0 + M], rhs, k == 0, k == nk - 1,
                   (list(rslot) if rslot else ["hT%d" % i]) + ["win_sb", "wuq_sb", "wukv_sb"], ["ps%d" % b])
            return b

        def store_fm(ap_sb, dst, key_i, c, name, npart=128):
            simple_dma(dst, ap_sb, "stg%d" % key_i, reads=["stg%d" % key_i], writes=["D:%s:%d" % (name, c)])

        def back(c):
            i = c % 2
            t0 = 512 * c
            ts = slice(t0, t0 + 512)
            hs = "hT%d" % i
            for name, col0, dst in (("qaT", C_QA, qaT), ("kaT", C_KA, kaT)):
                for j in range(2):
                    b = fm(i, col0 + 128 * j, 128)
                    si = nstg()
                    vcopy(ev_eng(), stg[si], PS[b], ["ps%d" % b], ["stg%d" % si])
                    store_fm(stg[si], dst[128 * j:128 * (j + 1), ts], si, c, name)
            if CUT == 3:
                return
            for (col0, nchk, off, gcol, nfeat, nb_) in ((C_CQ, 3, 0, 0, 384, 6), (C_CKV, 2, 3, 3, 256, 7)):
                for j in range(nchk):
                    b = fm(i, col0 + 128 * j, 128)
                    q = j % 2
                    act(sqb[q], PS[b], AF.Square, ["ps%d" % b], ["sqb%d" % q])
                    vcopy("dve", cqf[:, off + j, :], PS[b], ["ps%d" % b], ["cqf%d" % (off + j)])
                    mm(PS[nb_], ones_bf, sqb[q], j == 0, j == nchk - 1, ["sqb%d" % q, "ones_bf"], ["ps%d" % nb_])
                r = (off // 3) % 2
                rsqrt(rs[r], nfeat, PS[nb_], ["ps%d" % nb_], "rs%d" % r)
                for j in range(nchk):
                    stt("dve", cn[:, off + j, :], cqf[:, off + j, :], gsm_sb[:, gcol + j:gcol + j + 1], rs[r], ALU.mult, ALU.mult,
                        ["cqf%d" % (off + j), "gsm_sb", "rs%d" % r], ["cn%d" % (off + j)])
            if CUT == 4:
                return
            bA = fm(i, C_KPA, 96)
            bB = fm(i, C_KPB, 96)
            tt("dve", tA[0][64:96], PS[bA][64:96], rB[i][64:96, 0, :], ALU.mult, ["ps%d" % bA, "rB%d" % i], ["tA0"])
            tt("dve", tB[0][64:96], PS[bB][64:96], rB[i][64:96, 1, :], ALU.mult, ["ps%d" % bB, "rB%d" % i], ["tB0"])
            tt("dve", kpe_s[64:96], tA[0][64:96], tB[0][64:96], ALU.add, ["tA0", "tB0"], ["kpe_s"])
            dma(lambda e, s: [e.dma_start(out=kbT[h, 64:96, ts], in_=kpe_s[64:96]).then_inc(s, 16) for h in range(6)],
                "kpe", 6, reads=["kpe_s"], writes=["D:kbTp:%d" % c])
            if CUT == 5:
                return
            for (colA, colB, gc, name, dst, row0) in ([(C_QCA + 128 * j, C_QCB + 128 * j, 5, "qcT", qcT, 128 * j) for j in range(3)]
                                                      + [(C_KCA, C_KCB, 7, "kcT", kcT, 0)]):
                bA = fm(i, colA, 128)
                bB = fm(i, colB, 128)
                q = ev_rr[0] % 2
                ev_rr[0] += 1
                act(sqb[q], PS[bA], AF.Square, ["ps%d" % bA], ["sqb%d" % q])
                act(tA[q], PS[bA], AF.Copy, ["ps%d" % bA, "gsm_sb"], ["tA%d" % q], scale=gsm_sb[:, gc:gc + 1])
                act(tB[q], PS[bB], AF.Copy, ["ps%d" % bB, "gsm_sb"], ["tB%d" % q], scale=gsm_sb[:, gc + 1:gc + 2])
                nb_ = 6 + q
                mm(PS[nb_], bdones, sqb[q], True, True, ["sqb%d" % q, "bdones"], ["ps%d" % nb_])
                rsqrt(rs[q], 64, PS[nb_], ["ps%d" % nb_], "rs%d" % q)
                tt("pool", tB[q], tB[q], rC[i][:, 1, :], ALU.mult, ["tB%d" % q, "rC%d" % i], ["tB%d" % q])
                tt("dve", tA[q], tA[q], rC[i][:, 0, :], ALU.mult, ["tA%d" % q, "rC%d" % i], ["tA%d" % q])
                tt("dve", tA[q], tA[q], tB[q], ALU.add, ["tA%d" % q, "tB%d" % q], ["tA%d" % q])
                si = nstg()
                tt("dve", stg[si], tA[q], rs[q], ALU.mult, ["tA%d" % q, "rs%d" % q], ["stg%d" % si])
                store_fm(stg[si], dst[row0:row0 + 128, ts], si, c, name)
            if CUT == 6:
                return
            for h in range(6):
                bA = fm(i, 192 * h, 96, wsb=wuq_sb, nk=3, rhs_fn=lambda k: cn[:, k, :], rslot=("cn0", "cn1", "cn2"))
                bB = fm(i, 192 * h + 96, 96, wsb=wuq_sb, nk=3, rhs_fn=lambda k: cn[:, k, :], rslot=("cn0", "cn1", "cn2"))
                si = nstg()
                q = h % 2
                vcopy("act", stg[si][0:64], PS[bA][0:64], ["ps%d" % bA], ["stg%d" % si])
                tt("dve", tA[q][64:96], PS[bA][64:96], rB[i][64:96, 0, :], ALU.mult, ["ps%d" % bA, "rB%d" % i], ["tA%d" % q])
                tt("dve", tB[q][64:96], PS[bB][64:96], rB[i][64:96, 1, :], ALU.mult, ["ps%d" % bB, "rB%d" % i], ["tB%d" % q])
                tt("dve", stg[si][64:96], tA[q][64:96], tB[q][64:96], ALU.add, ["tA%d" % q, "tB%d" % q], ["stg%d" % si])
                simple_dma(qbT[h, :, ts], stg[si][0:96], "stg%d" % si, reads=["stg%d" % si], writes=["D:qbT:%d" % c])
            if CUT == 7:
                return
            for h in range(6):
                b = fm(i, 64 * h, 64, wsb=wukv_sb, nk=2, rhs_fn=lambda k: cn[:, 3 + k, :], rslot=("cn3", "cn4"))
                si = nstg()
                vcopy(ev_eng(), stg[si][0:64], PS[b][0:64], ["ps%d" % b], ["stg%d" % si])
                simple_dma(kbT[h, 0:64, ts], stg[si][0:64], "stg%d" % si, reads=["stg%d" % si], writes=["D:kbTn:%d" % c])
            if CUT == 8:
                return
            for b4 in range(4):
                b = nbank()
                for k in range(8):
                    mm(PS[b][:, 0:384], hT[i][:, k, 128 * b4:128 * (b4 + 1)], win_sb[:, k, C_V:C_V + 384], k == 0, k == 7,
                       [hs, "win_sb"], ["ps%d" % b])
                vcopy(ev_eng(), vst[:, b4, :, 0:64], PS[b][:, 0:384].rearrange("p (h d) -> p h d", d=64), ["ps%d" % b], ["vst"])
                b = nbank()
                for k in range(2):
                    mm(PS[b][:, 0:384], cn[:, 3 + k, 128 * b4:128 * (b4 + 1)], wukv_sb[:, k, 384:768], k == 0, k == 1,
                       ["cn3", "cn4", "wukv_sb"], ["ps%d" % b])
                vcopy(ev_eng(), vsb[:, b4, :, 0:64], PS[b][:, 0:384].rearrange("p (h d) -> p h d", d=64), ["ps%d" % b], ["vsb"])
            simple_dma(va[ts].rearrange("(b p) h d -> p b h d", p=128), vst[:, :, 0:4, :], "vst_a", reads=["vst"], writes=["D:va:%d" % c])
            simple_dma(vc[ts].rearrange("(b p) h d -> p b h d", p=128), vst[:, :, 4:6, :], "vst_c", reads=["vst"], writes=["D:vc:%d" % c])
            simple_dma(vb[ts].rearrange("(b p) h d -> p b h d", p=128), vsb, "vsb", reads=["vsb"], writes=["D:vb:%d" % c])

        import os
        CUT = int(os.environ.get("K_CUT", "0"))
        if CUT == 1:
            return
        front(0)
        if CUT == 2:
            return
        for c in range(NCH):
            if c + 1 < NCH:
                front(c + 1)
            back(c)
            if CUT:
                return

    def attn_epilogue(acc_bank, bc_bank, ebuf, row0, t0, ntok, c_list, tag):
        osb, rc, hi, lo = ebuf
        vcopy("dve", osb[0:65, 0:ntok], PS[acc_bank][0:65, 0:ntok], ["ps%d" % acc_bank], [tag + "osb"])
        P.op("dve", lambda e: e.reciprocal(out=rc[64:65, 0:ntok], in_=osb[64:65, 0:ntok]), [tag + "osb"], [tag + "rc"])
        vcopy("dve", hi[64:65, 0:ntok], rc[64:65, 0:ntok], [tag + "rc"], [tag + "hi"])
        tt("dve", lo[64:65, 0:ntok], rc[64:65, 0:ntok], hi[64:65, 0:ntok], ALU.subtract, [tag + "rc", tag + "hi"], [tag + "lo"])
        mm(PS[bc_bank][0:64, 0:ntok], sel_b, hi[:, 0:ntok], True, False, [tag + "hi", "sel_b"], ["ps%d" % bc_bank])
        mm(PS[bc_bank][0:64, 0:ntok], sel_b, lo[:, 0:ntok], False, True, [tag + "lo", "sel_b"], ["ps%d" % bc_bank])
        tt("dve", osb[0:64, 0:ntok], osb[0:64, 0:ntok], PS[bc_bank][0:64, 0:ntok], ALU.mult,
           [tag + "osb", "ps%d" % bc_bank], [tag + "osb"])
        simple_dma(oT[row0:row0 + 64, t0:t0 + ntok], osb[0:64, 0:ntok], tag + "osb", reads=[tag + "osb"],
                   writes=["D:oT%d:%d" % (row0, c) for c in c_list])

    def phase_B_dense(l):
        P.barrier()
        A.off = GLOBAL_OFF
        SM = max(S for S, _ in slots)
        NB = SM // 128
        kbuf = [A.alloc([128, SM], BF16) for _ in range(2)]
        vbuf = [A.alloc([128, NB, 65], BF16) for _ in range(2)]
        qbuf = [A.alloc([128, SM], BF16) for _ in range(2)]
        NPT = 4
        pt = [A.alloc([128, 1024], BF16) for _ in range(NPT)]
        ebufs = [(A.alloc([128, 512], F32), A.alloc([128, 512], F32), A.alloc([128, 512], BF16), A.alloc([128, 512], BF16))
                 for _ in range(2)]
        for i_, eb_ in enumerate(ebufs):
            P.op("pool", lambda e, eb_=eb_: e.memset(eb_[2], 0.0), writes=["e%dhi" % i_])
            P.op("pool", lambda e, eb_=eb_: e.memset(eb_[3], 0.0), writes=["e%dlo" % i_])
        work = []
        for si, (S, nact) in enumerate(slots):
            na = S if l == 0 else nact
            for h in range(6):
                work.append((si, "b", h, [h], na))
            for kv in range(2):
                work.append((si, "c", kv, [3 * kv, 3 * kv + 1, 3 * kv + 2], na))
        qjobs = []
        for wi, (si, kind, kv, qhs, na) in enumerate(work):
            for qh in qhs:
                qjobs.append((wi, qh))
        for i_ in range(2):
            P.op("pool", lambda e, i_=i_: e.memset(kbuf[i_], 0.0), writes=["kbuf%d" % i_])
            P.op("pool", lambda e, i_=i_: e.memset(qbuf[i_], 0.0), writes=["qbuf%d" % i_])

        def load_kv(wi):
            si, kind, kv, qhs, na = work[wi]
            S = slots[si][0]
            t0 = slot_tok0[si]
            i = wi % 2
            cl = range(t0 // 512, (t0 + S) // 512)
            if kind == "b":
                simple_dma(kbuf[i][0:96, 0:S], kbT[kv, :, t0:t0 + S], "kb%d" % i,
                           reads=["D:kbTn:%d" % c for c in cl] + ["D:kbTp:%d" % c for c in cl], writes=["kbuf%d" % i])
                simple_dma(vbuf[i][:, 0:S // 128, :], vb[t0:t0 + S, kv, :].rearrange("(b p) d -> p b d", p=128), "vb%d" % i,
                           reads=["D:vb:%d" % c for c in cl], writes=["vbuf%d" % i])
            else:
                P.op("pool", lambda e: e.memset(kbuf[i][64:128, 0:S], 0.0), writes=["kbuf%d" % i])
                simple_dma(kbuf[i][0:64, 0:S], kcT[64 * kv:64 * kv + 64, t0:t0 + S], "kb%d" % i,
                           reads=["D:kcT:%d" % c for c in cl], writes=["kbuf%d" % i])
                simple_dma(vbuf[i][:, 0:S // 128, :], vc[t0:t0 + S, kv, :].rearrange("(b p) d -> p b d", p=128), "vb%d" % i,
                           reads=["D:vc:%d" % c for c in cl], writes=["vbuf%d" % i])

        def load_q(qi):
            wi, qh = qjobs[qi]
            si, kind, kv, qhs, na = work[wi]
            t0 = slot_tok0[si]
            i = qi % 2
            cl = range(t0 // 512, (t0 + na) // 512)
            if kind == "b":
                simple_dma(qbuf[i][0:96, 0:na], qbT[qh, :, t0:t0 + na], "qb%d" % i,
                           reads=["D:qbT:%d" % c for c in cl], writes=["qbuf%d" % i])
            else:
                simple_dma(qbuf[i][0:64, 0:na], qcT[64 * qh:64 * qh + 64, t0:t0 + na], "qb%d" % i,
                           reads=["D:qcT:%d" % c for c in cl], writes=["qbuf%d" % i])

        load_kv(0)
        load_q(0)
        it = [0]
        nacc = [0]
        for qi, (wi, qh) in enumerate(qjobs):
            si, kind, kv, qhs, na = work[wi]
            S = slots[si][0]
            t0 = slot_tok0[si]
            if qi + 1 < len(qjobs):
                nwi = qjobs[qi + 1][0]
                if nwi != wi:
                    load_kv(nwi)
                load_q(qi + 1)
            ki = wi % 2
            qb_ = qi % 2
            dk = 96 if kind == "b" else 128
            scale = 1.0 / math.sqrt(96 if kind == "b" else 64)
            row0 = (256 + 64 * qh) if kind == "b" else (640 + 64 * qh)
            nkb = S // 128
            npair = nkb // 2
            for qc in range(na // 512):
                ab = 6
                bb = 7
                eb = ebufs[nacc[0] % 2]
                etag = "e%d" % (nacc[0] % 2)
                nacc[0] += 1
                pend = []

                def qk(kp):
                    n = it[0]
                    it[0] += 1
                    sp_ = n % 3
                    pi = n % NPT
                    for u in range(2):
                        kb = 2 * kp + u
                        mm(PS[2 * sp_ + u], kbuf[ki][0:dk, 128 * kb:128 * (kb + 1)], qbuf[qb_][0:dk, 512 * qc:512 * (qc + 1)], True, True,
                           ["kbuf%d" % ki, "qbuf%d" % qb_], ["ps%d" % (2 * sp_ + u)])
                    act(pt[pi], psall[:, 1024 * sp_:1024 * (sp_ + 1)], AF.Exp, ["ps%d" % (2 * sp_), "ps%d" % (2 * sp_ + 1)],
                        ["pt%d" % pi], scale=scale)
                    return pi

                def pv(kp, pi):
                    for u in range(2):
                        kb = 2 * kp + u
                        mm(PS[ab][0:65, :], vbuf[ki][:, kb, :], pt[pi][:, 512 * u:512 * (u + 1)], kb == 0, kb == nkb - 1,
                           ["vbuf%d" % ki, "pt%d" % pi], ["ps%d" % ab])

                LOOK = 2
                for kp in range(npair + LOOK):
                    if kp < npair:
                        pend.append(qk(kp))
                    if kp >= LOOK:
                        pv(kp - LOOK, pend[kp - LOOK])
                tq = t0 + 512 * qc
                attn_epilogue(ab, bb, eb, row0, tq, 512, [tq // 512], etag)

    def phase_B_na(l):
        import os
        NACUT = int(os.environ.get("K_NACUT", "0"))
        P.barrier()
        A.off = GLOBAL_OFF
        SM = max(S for S, _ in slots)
        NB = SM // 128
        qz = A.alloc([128, 4, SM], BF16)
        ka_sb = A.alloc([128, 2, SM], BF16)
        P.op("pool", lambda e: e.memset(qz, 0.0), writes=["qz"])
        va_sb = A.alloc([128, NB, 4 * 65], BF16)
        bint = A.alloc([128, 5, 512], F32)
        bsp = [A.alloc([128, 512], F32) for _ in range(3)]
        sbias = [A.alloc([128, 512], F32) for _ in range(2)]
        pt = [A.alloc([128, 512], BF16) for _ in range(3)]
        ebufs = [(A.alloc([128, 512], F32), A.alloc([128, 512], F32), A.alloc([128, 512], BF16), A.alloc([128, 512], BF16))
                 for _ in range(2)]
        for i_, eb_ in enumerate(ebufs):
            P.op("pool", lambda e, eb_=eb_: e.memset(eb_[2], 0.0), writes=["e%dhi" % i_])
            P.op("pool", lambda e, eb_=eb_: e.memset(eb_[3], 0.0), writes=["e%dlo" % i_])
        simple_dma(bint, nab_int[l], "bint", writes=["bint"])
        sp_base = 0
        it = [0]
        spn = [0]
        nacc = [0]
        for si, (S, nact) in enumerate(slots):
            na = S if l == 0 else nact
            t0 = slot_tok0[si]
            nb = S // 128
            cl = range(t0 // 512, (t0 + S) // 512)
            specials = na_specials(S, nact)
            dma(lambda e, s_, t0=t0, S=S: [e.dma_start(out=qz[64 * (hd % 2):64 * (hd % 2) + 64, hd, 0:S],
                                                       in_=qaT[64 * hd:64 * hd + 64, t0:t0 + S]).then_inc(s_, 16) for hd in range(4)],
                "qz", 4, reads=["D:qaT:%d" % c for c in cl], writes=["qz"])
            simple_dma(ka_sb[:, :, 0:S], kaT[:, t0:t0 + S].rearrange("(j p) t -> p j t", p=128), "ka_sb",
                       reads=["D:kaT:%d" % c for c in cl], writes=["ka_sb"])
            simple_dma(va_sb[:, 0:nb, :], va[t0:t0 + S].rearrange("(b p) h d -> p b (h d)", p=128), "va_sb",
                       reads=["D:va:%d" % c for c in cl], writes=["va_sb"])
            for g in range(na // 512):
                for qi4 in range(4):
                    i = 4 * g + qi4
                    if i in specials:
                        sidx = sp_base + specials.index(i)
                        offs = [(d, ("sp", sidx * 7 + di)) for di, d in enumerate(NA_OFFS)]
                    else:
                        offs = [(d, ("int", d + 2)) for d in (-2, -1, 0, 1, 2)]
                    for oi, (d, (kind, bidx)) in enumerate(offs):
                        j = (i + d) % nb
                        n = it[0]
                        it[0] += 1
                        bk = (2 * (n % 2), 2 * (n % 2) + 1)
                        for c4 in range(4):
                            hd = HS[c4]
                            mm(PS[bk[c4 // 2]][:, 128 * (c4 % 2):128 * (c4 % 2 + 1)], ka_sb[:, hd // 2, 128 * j:128 * (j + 1)],
                               qz[:, hd, 128 * i:128 * (i + 1)], True, True, ["ka_sb", "qz"], ["ps%d" % bk[c4 // 2]])
                        if kind == "sp":
                            bi = spn[0] % 3
                            spn[0] += 1
                            simple_dma(bsp[bi], nab_sp[l, bidx], "bsp%d" % bi, writes=["bsp%d" % bi])
                            bias_ap, bias_slot = bsp[bi], "bsp%d" % bi
                        else:
                            bias_ap, bias_slot = bint[:, bidx, :], "bint"
                        sbi = n % 2
                        for hf in range(2):
                            stt("dve", sbias[sbi][:, 256 * hf:256 * (hf + 1)], PS[bk[hf]][:, 0:256], 0.125,
                                bias_ap[:, 256 * hf:256 * (hf + 1)], ALU.mult, ALU.add,
                                ["ps%d" % bk[hf], bias_slot], ["sbias%d" % sbi])
                        pi = n % 3
                        act(pt[pi], sbias[sbi], AF.Exp, ["sbias%d" % sbi], ["pt%d" % pi])
                        if NACUT == 2:
                            continue
                        for c4 in range(4):
                            hd = HS[c4]
                            mm(PS[4 + c4][0:65, 128 * qi4:128 * (qi4 + 1)], va_sb[:, j, 65 * hd:65 * (hd + 1)],
                               pt[pi][:, 128 * c4:128 * (c4 + 1)], oi == 0, oi == len(offs) - 1,
                               ["va_sb", "pt%d" % pi], ["ps%d" % (4 + c4)])
                tq = t0 + 512 * g
                if NACUT in (2, 3):
                    continue
                for c4 in range(4):
                    eb = ebufs[nacc[0] % 2]
                    etag = "e%d" % (nacc[0] % 2)
                    bb = 2 * (nacc[0] % 2)
                    nacc[0] += 1
                    attn_epilogue(4 + c4, bb, eb, 64 * HS[c4], tq, 512, [tq // 512], etag)
            sp_base += len(specials)

    def phase_C1(l):
        P.barrier()
        A.off = GLOBAL_OFF
        x_src = x_in if l == 0 else x1
        wo_sb = A.alloc([128, 8, D], BF16)
        gsm_sb = A.alloc([128, 17], F32)
        ot = [A.alloc([128, 8, 512], F32) for _ in range(2)]
        sqb = [A.alloc([128, 512], BF16) for _ in range(3)]
        rs = [A.alloc([128, 512], F32) for _ in range(3)]
        mT = A.alloc([128, 8, 512], BF16)
        xa = [A.alloc([128, D], F32) for _ in range(4)]
        dma(lambda e, s: e.dma_start(out=wo_sb, in_=wout[l].rearrange("(k p) n -> p k n", p=128)).then_inc(s, 16),
            "w_o", 1, writes=["wo_sb"], q="pool")
        simple_dma(gsm_sb, gsm[l], "g0", writes=["gsm_sb"])
        chunks = [c for c in range(NCH) if chunk_active(c, l)]
        groups = ((0, 2, 256), (2, 5, 384), (5, 8, 384))

        def load(ci):
            c = chunks[ci]
            i = ci % 2
            simple_dma(ot[i], oT[:, 512 * c:512 * (c + 1)].rearrange("(j p) t -> p j t", p=128), "ot%d" % i,
                       reads=["D:oT%d:%d" % (r, c) for r in range(0, D, 64)], writes=["ot%d" % i])

        load(0)
        xn = [0]
        for ci, c in enumerate(chunks):
            i = ci % 2
            if ci + 1 < len(chunks):
                load(ci + 1)
            for gi, (j0, j1, nf) in enumerate(groups):
                for j in range(j0, j1):
                    q = j % 3
                    act(sqb[q], ot[i][:, j, :], AF.Square, ["ot%d" % i], ["sqb%d" % q])
                    mm(PS[gi], ones_bf, sqb[q], j == j0, j == j1 - 1, ["sqb%d" % q, "ones_bf"], ["ps%d" % gi])
                rsqrt(rs[gi], nf, PS[gi], ["ps%d" % gi], "rs%d" % gi)
                for j in range(j0, j1):
                    stt("dve", mT[:, j, :], ot[i][:, j, :], gsm_sb[:, 9 + j:10 + j], rs[gi], ALU.mult, ALU.mult,
                        ["ot%d" % i, "gsm_sb", "rs%d" % gi], ["mT"])
            for b4 in range(4):
                xi = xn[0] % 4
                xn[0] += 1
                r0 = 512 * c + 128 * b4
                simple_dma(xa[xi], x_src[r0:r0 + 128, :], "xa%d" % xi, reads=["D:x%d:%d" % (l, c)], writes=["xa%d" % xi])
                for hf in range(2):
                    b = 4 + (2 * b4 + hf) % 4
                    for j in range(8):
                        mm(PS[b], mT[:, j, 128 * b4:128 * (b4 + 1)], wo_sb[:, j, 512 * hf:512 * (hf + 1)], j == 0, j == 7,
                           ["mT", "wo_sb"], ["ps%d" % b])
                    tt("dve", xa[xi][:, 512 * hf:512 * (hf + 1)], xa[xi][:, 512 * hf:512 * (hf + 1)], PS[b], ALU.add,
                       ["xa%d" % xi, "ps%d" % b], ["xa%d" % xi])
                simple_dma(xmid[r0:r0 + 128, :], xa[xi], "xa%d" % xi, reads=["xa%d" % xi], writes=["D:xmid:%d" % c])

    def phase_C2(l):
        P.barrier()
        A.off = GLOBAL_OFF
        wg_sb = A.alloc([128, 8, DFF], BF16)
        wu_sb = A.alloc([128, 8, DFF], BF16)
        wd_sb = A.alloc([128, NFF, D], BF16)
        gF = A.alloc([128, D], F32)
        gL = A.alloc([128, D], F32)
        xm = [A.alloc([128, D], F32) for _ in range(4)]
        junk = A.alloc([128, D], BF16)
        ssv = A.alloc([128, 8], F32)
        hb = [A.alloc([128, D], BF16) for _ in range(2)]
        hT = A.alloc([128, 8, 512], BF16)
        aT = A.alloc([128, NFF, 512], BF16)
        sg = [A.alloc([128, 512], F32) for _ in range(2)]
        for k in range(8):
            dma(lambda e, s, k=k: e.dma_start(out=wg_sb[:, k, :], in_=wg[l, 128 * k:128 * (k + 1), :]).then_inc(s, 16),
                "w_g%d" % k, 1, writes=["wg_sb"], q="pool")
            dma(lambda e, s, k=k: e.dma_start(out=wu_sb[:, k, :], in_=wu[l, 128 * k:128 * (k + 1), :]).then_inc(s, 16),
                "w_u%d" % k, 1, writes=["wu_sb"], q="pool")
        for f0 in range(0, NFF, 4):
            f1 = min(NFF, f0 + 4)
            dma(lambda e, s, f0=f0, f1=f1: e.dma_start(out=wd_sb[:, f0:f1, :], in_=wd[l, 128 * f0:128 * f1, :].rearrange("(k p) n -> p k n", p=128)).then_inc(s, 16),
                "w_d%d" % f0, 1, writes=["wd_sb"], q="pool")
        simple_dma(gF, gbc[l, 1].partition_broadcast(128), "g1", writes=["gF"])
        if l == L - 1:
            simple_dma(gL, gfin.partition_broadcast(128), "g2", writes=["gL"])
        chunks = [c for c in range(NCH) if chunk_active(c, l)]

        def load(ci):
            c = chunks[ci]
            for b4 in range(4):
                r0 = 512 * c + 128 * b4
                simple_dma(xm[b4], xmid[r0:r0 + 128, :], "xm%d" % b4, reads=["D:xmid:%d" % c], writes=["xm%d" % b4])

        load(0)
        for ci, c in enumerate(chunks):
            for b4 in range(4):
                j = b4 % 2
                act(junk, xm[b4], AF.Square, ["xm%d" % b4], ["junk", "ssv"], accum_out=ssv[:, b4:b4 + 1])
                rsqrt(ssv[:, b4:b4 + 1], D, ssv[:, b4:b4 + 1], ["ssv"], "ssv")
                stt("dve", hb[j], xm[b4], ssv[:, b4:b4 + 1], gF, ALU.mult, ALU.mult, ["xm%d" % b4, "ssv", "gF"], ["hb%d" % j])
                for k in range(8):
                    P.op("pe", lambda e, k=k, j=j: e.transpose(out=PSB[j][:, 128 * k:128 * (k + 1)], in_=hb[j][:, 128 * k:128 * (k + 1)], identity=ident),
                         ["hb%d" % j, "ident"], ["ps%d" % j])
                vcopy("act" if b4 % 2 else "dve", hT[:, :, 128 * b4:128 * (b4 + 1)],
                      PSB[j].rearrange("p (k t) -> p k t", t=128), ["ps%d" % j], ["hT"])
            for f in range(NFF):
                bg = 2 + f % 2
                bu = 4 + f % 2
                for k in range(8):
                    mm(PS[bg], wg_sb[:, k, 128 * f:128 * (f + 1)], hT[:, k, :], k == 0, k == 7, ["wg_sb", "hT"], ["ps%d" % bg])
                for k in range(8):
                    mm(PS[bu], wu_sb[:, k, 128 * f:128 * (f + 1)], hT[:, k, :], k == 0, k == 7, ["wu_sb", "hT"], ["ps%d" % bu])
                q = f % 2
                act(sg[q], PS[bg], AF.Silu, ["ps%d" % bg], ["sg%d" % q])
                tt("dve", aT[:, f, :], sg[q], PS[bu], ALU.mult, ["sg%d" % q, "ps%d" % bu], ["aT"])
            for b4 in range(4):
                r0 = 512 * c + 128 * b4
                for hf in range(2):
                    b = 6 + hf
                    for f in range(NFF):
                        mm(PS[b], aT[:, f, 128 * b4:128 * (b4 + 1)], wd_sb[:, f, 512 * hf:512 * (hf + 1)], f == 0, f == NFF - 1,
                           ["aT", "wd_sb"], ["ps%d" % b])
                    tt("dve", xm[b4][:, 512 * hf:512 * (hf + 1)], xm[b4][:, 512 * hf:512 * (hf + 1)], PS[b], ALU.add,
                       ["xm%d" % b4, "ps%d" % b], ["xm%d" % b4])
                if l < L - 1:
                    simple_dma(x1[r0:r0 + 128, :], xm[b4], "xm%d" % b4, reads=["xm%d" % b4], writes=["D:x%d:%d" % (l + 1, c)])
                else:
                    act(junk, xm[b4], AF.Square, ["xm%d" % b4], ["junk", "ssv"], accum_out=ssv[:, 4 + b4:5 + b4])
                    rsqrt(ssv[:, 4 + b4:5 + b4], D, ssv[:, 4 + b4:5 + b4], ["ssv"], "ssv")
                    stt("dve", xm[b4], xm[b4], ssv[:, 4 + b4:5 + b4], gL, ALU.mult, ALU.mult, ["xm%d" % b4, "ssv", "gL"], ["xm%d" % b4])
                    ro = out_row0(c) + 128 * b4
                    simple_dma(y_out[ro:ro + 128, :], xm[b4], "xm%d" % b4, reads=["xm%d" % b4], writes=["D:y:%d" % c])
            if ci + 1 < len(chunks):
                load(ci + 1)

    phases = []
    for l in range(L):
        phases += [("A", l), ("Bna", l), ("Bd", l), ("C1", l), ("C2", l)]
    for name, l in phases:
        {"A": phase_A, "Bna": phase_B_na, "Bd": phase_B_dense, "C1": phase_C1, "C2": phase_C2}[name](l)
        if stop_after == (name, l):
            break
    P.final_wait("sp")
    cc = P.emit(nc, st)
    st.close()
    return nc, (len(P.ops), {k: len(v) for k, v in P.streams.items()}, cc)


def rope_perm(d):
    half = d // 2
    return np.concatenate([np.arange(half, d), np.arange(0, half)])


def rope_tables(pos, d):
    half = d // 2
    inv = (10000.0 ** (-(np.arange(half, dtype=np.float32) * 2.0) / d)).astype(np.float32)
    ang = pos.astype(np.float32)[None, :] * inv[:, None]
    cos = np.cos(ang).astype(np.float32)
    sin = np.sin(ang).astype(np.float32)
    return np.concatenate([cos, cos], 0), np.concatenate([-sin, sin], 0)


def na_bias_tile(rpb_l, rows, tq, tk):
    out = np.full((128, 4, 128), NEG, np.float32)
    nb = rows // 2
    if tk is None or tk < 0 or tk >= nb:
        return out.reshape(128, 512)
    qi = np.arange(128)
    r = 2 * tq + qi // 64
    cq = qi % 64
    ki = np.arange(128)
    kr = 2 * tk + ki // 64
    ck = ki % 64
    kr_n = min(8, rows)
    rs = np.clip(r - kr_n // 2, 0, rows - kr_n)
    cs = np.clip(cq - 8, 0, GRID_W - 16)
    valid = ((kr[:, None] >= rs[None, :]) & (kr[:, None] < rs[None, :] + kr_n)
             & (ck[:, None] >= cs[None, :]) & (ck[:, None] < cs[None, :] + 16))
    dr = np.clip(kr[:, None] - r[None, :] + 7, 0, 14)
    dc = np.clip(ck[:, None] - cq[None, :] + 15, 0, 30)
    vals = rpb_l[:, dr, dc]
    out = np.where(valid[:, None, :], vals.transpose(1, 0, 2), np.float32(NEG)).astype(np.float32)
    return np.ascontiguousarray(out[:, list(HS), :]).reshape(128, 512)


def prep_weights(inp):
    f = lambda a: np.asarray(a, np.float32)
    w_in = f(inp["w_in"])
    Ls = w_in.shape[0]
    segs = np.cumsum([0, 256, 256, 256, 384, 256, 32, 384, 128, 128])
    o_qa, o_ka, o_va, o_cq, o_ckv, o_kpe, o_qc, o_kc, o_vc = segs[:9]
    p32 = rope_perm(32)
    p_ax = np.concatenate([rope_perm(32), 32 + rope_perm(32)])
    win = np.zeros((Ls, D, NIN), np.float32)
    win[:, :, C_QA:C_QA + 256] = w_in[:, :, o_qa:o_qa + 256]
    win[:, :, C_KA:C_KA + 256] = w_in[:, :, o_ka:o_ka + 256]
    win[:, :, C_CQ:C_CQ + 384] = w_in[:, :, o_cq:o_cq + 384]
    win[:, :, C_CKV:C_CKV + 256] = w_in[:, :, o_ckv:o_ckv + 256]
    win[:, :, C_KPA + 64:C_KPA + 96] = w_in[:, :, o_kpe:o_kpe + 32]
    win[:, :, C_KPB + 64:C_KPB + 96] = w_in[:, :, o_kpe + p32]
    qc_sw = np.concatenate([o_qc + 64 * h + p_ax for h in range(6)])
    kc_sw = np.concatenate([o_kc + 64 * h + p_ax for h in range(2)])
    win[:, :, C_QCA:C_QCA + 384] = w_in[:, :, o_qc:o_qc + 384]
    win[:, :, C_QCB:C_QCB + 384] = w_in[:, :, qc_sw]
    win[:, :, C_KCA:C_KCA + 128] = w_in[:, :, o_kc:o_kc + 128]
    win[:, :, C_KCB:C_KCB + 128] = w_in[:, :, kc_sw]
    win[:, :, C_V:C_V + 256] = w_in[:, :, o_va:o_va + 256]
    win[:, :, C_V + 256:C_V + 384] = w_in[:, :, o_vc:o_vc + 128]
    w_uq = f(inp["w_uq"])
    wuq = np.zeros((Ls, 384, 1152), np.float32)
    for h in range(6):
        wuq[:, :, 192 * h:192 * h + 96] = w_uq[:, :, 96 * h:96 * h + 96]
        wuq[:, :, 192 * h + 96:192 * h + 160] = w_uq[:, :, 96 * h:96 * h + 64]
        wuq[:, :, 192 * h + 160:192 * h + 192] = w_uq[:, :, 96 * h + 64 + p32]
    w_ukv = f(inp["w_ukv"])
    wukv = np.zeros((Ls, 256, 768), np.float32)
    for h in range(6):
        wukv[:, :, 64 * h:64 * h + 64] = w_ukv[:, :, 128 * h:128 * h + 64]
        wukv[:, :, 384 + 64 * h:384 + 64 * h + 64] = w_ukv[:, :, 128 * h + 64:128 * h + 128]
    gsm = np.zeros((Ls, 128, 17), np.float32)
    qan = f(inp["q_a_norm"])
    kvn = f(inp["kv_a_norm"])
    qn = f(inp["q_norm_c"])
    kn = f(inp["k_norm_c"])
    gout = np.concatenate([f(inp["g_out_a"]), f(inp["g_out_b"]), f(inp["g_out_c"])], 1)
    for l in range(Ls):
        gsm[l, :, 0:3] = qan[l].reshape(3, 128).T
        gsm[l, :, 3:5] = kvn[l].reshape(2, 128).T
        gsm[l, :, 5] = np.tile(qn[l], 2)
        gsm[l, :, 6] = np.tile(qn[l][p_ax], 2)
        gsm[l, :, 7] = np.tile(kn[l], 2)
        gsm[l, :, 8] = np.tile(kn[l][p_ax], 2)
        gsm[l, :, 9:17] = gout[l].reshape(8, 128).T
    gbc = np.stack([f(inp["attn_norm"]), f(inp["ffn_norm"])], 1)
    bd = np.zeros((128, 128), np.float32)
    bd[:64, :64] = 1
    bd[64:, 64:] = 1
    return dict(win=win, wuq=wuq, wukv=wukv, wout=f(inp["w_out"]), wg=f(inp["w_gate"]), wu=f(inp["w_up"]),
                wd=f(inp["w_down"]), gsm=gsm, gbc=np.ascontiguousarray(gbc), gfin=f(inp["final_norm"]),
                ident=np.eye(128, dtype=np.float32).astype(ml_dtypes.bfloat16), bdones=bd.astype(ml_dtypes.bfloat16))


def prep_core_tables(rpb, slots, true_pos):
    pos = np.concatenate(true_pos)
    row = pos // GRID_W
    col = pos % GRID_W
    cr, sr = rope_tables(row, 32)
    cc, sc = rope_tables(col, 32)
    cosC = np.concatenate([cr, cc], 0)
    sinC = np.concatenate([sr, sc], 0)
    ropeC = np.stack([np.tile(cosC, (2, 1)), np.tile(sinC, (2, 1))], 0).astype(np.float32)
    cb, sb = rope_tables(pos, 32)
    ropeB = np.stack([cb, sb], 0).astype(np.float32)
    Ls = rpb.shape[0]
    nab_int = np.zeros((Ls, 128, 5, 512), np.float32)
    sp_tiles = [[] for _ in range(Ls)]
    for l in range(Ls):
        for di, d in enumerate((-2, -1, 0, 1, 2)):
            nab_int[l, :, di, :] = na_bias_tile(rpb[l], 64, 8, 8 + d)
        for si, (S, nact) in enumerate(slots):
            nb = S // 128
            rows = S // GRID_W
            tb = true_pos[si][::128] // 128
            for i in na_specials(S, nact):
                for d in NA_OFFS:
                    tq = int(tb[i])
                    tk = tq + d
                    j = (i + d) % nb
                    if 0 <= tk < nb:
                        assert int(tb[j]) == tk
                    sp_tiles[l].append(na_bias_tile(rpb[l], rows, tq, tk))
    nab_sp = np.stack([np.stack(t, 0) for t in sp_tiles], 0)
    return ropeC, ropeB, nab_int, nab_sp


_CACHE = {}


def run_slots(slots, per_core_x, per_core_pos, inp, debug=False, stop_after=None, trace=False):
    key = (tuple(slots), debug, stop_after)
    if key not in _CACHE:
        _CACHE[key] = build_program(slots, debug=debug, stop_after=stop_after)
    nc, _ = _CACHE[key]
    W = prep_weights(inp)
    rpb = np.asarray(inp["rpb"], np.float32)
    in_maps = []
    for x, pos in zip(per_core_x, per_core_pos):
        ropeC, ropeB, nab_int, nab_sp = prep_core_tables(rpb, slots, pos)
        m = dict(W)
        m.update(x_in=np.ascontiguousarray(x, dtype=np.float32), ropeC=ropeC, ropeB=ropeB, nab_int=nab_int, nab_sp=nab_sp)
        in_maps.append(m)
    res = run_bass_kernel_spmd(nc, in_maps, core_ids=list(range(len(in_maps))), **({"trace": True} if trace else {}))
    return res


def kernel(**inp):
    xp = np.asarray(inp["x_prompt"], np.float32)
    xs = np.asarray(inp["x_sample"], np.float32)
    B, S, _ = xp.shape
    Bs, Ss, _ = xs.shape
    assert Bs == 2 * 8 and B * 2 == 8
    H = S // 2
    slots = [(Ss, Ss), (Ss, Ss), (S, H)]
    per_x, per_pos = [], []
    for c in range(8):
        p, h = c // 2, c % 2
        xl = np.concatenate([xs[2 * c], xs[2 * c + 1], np.roll(xp[p], -h * H, axis=0)], 0)
        per_x.append(xl)
        per_pos.append([np.arange(Ss), np.arange(Ss), (np.arange(S) + h * H) % S])
    res = run_slots(slots, per_x, per_pos, inp)
    yp = np.zeros_like(xp)
    ys = np.zeros_like(xs)
    for c in range(8):
        y = res.results[c]["y_out"]
        p, h = c // 2, c % 2
        ys[2 * c] = y[0:Ss]
        ys[2 * c + 1] = y[Ss:2 * Ss]
        yp[p, h * H:(h + 1) * H] = y[2 * Ss:2 * Ss + H]
    return yp, ys
```

```python
import contextlib
import math

import ml_dtypes
import numpy as np

import concourse.bass as bass
import concourse.mybir as mybir
from concourse.bass_utils import run_bass_kernel_spmd

F32 = mybir.dt.float32
BF16 = mybir.dt.bfloat16
ALU = mybir.AluOpType
AF = mybir.ActivationFunctionType
AX = mybir.AxisListType

D = 1024
L = 2
GRID_W = 64
DFF = 2816
NFF = DFF // 128
EPS = 1e-6
NEG = -30000.0
NIN = 2752
C_QA, C_KA, C_CQ, C_CKV, C_KPA, C_KPB, C_QCA, C_QCB, C_KCA, C_KCB, C_V = (
    0, 256, 512, 896, 1152, 1248, 1344, 1728, 2112, 2240, 2368)
NA_OFFS = (-3, -2, -1, 0, 1, 2, 3)
HS = (0, 2, 1, 3)

SAME_ENGINE_SYNC = {"pe": False, "act": True, "dve": True, "pool": True, "sp": False}


class Op:
    __slots__ = ("stream", "fn", "deps", "flag", "count", "semkey", "nparts", "waits")

    def __init__(self, stream, fn, semkey=None, nparts=0):
        self.stream = stream
        self.fn = fn
        self.deps = ()
        self.flag = False
        self.count = None
        self.semkey = semkey
        self.nparts = nparts
        self.waits = None


class Prog:
    def __init__(self):
        self.ops = []
        self.slots = {}
        self.streams = {s: [] for s in ("pe", "act", "dve", "pool", "sp")}
        self.last = {}

    def _slot(self, name):
        s = self.slots.get(name)
        if s is None:
            s = [None, {}]
            self.slots[name] = s
        return s

    def op(self, stream, fn, reads=(), writes=(), semkey=None, nparts=0, extra_deps=()):
        o = Op(stream, fn, semkey, nparts)
        deps = set(extra_deps)
        okey = semkey if semkey is not None else stream
        for r in reads:
            s = self._slot(r)
            if s[0] is not None:
                deps.add(s[0])
            if r.startswith("ps"):
                for k, rd in s[1].items():
                    if k != okey:
                        deps.add(rd)
        for w in writes:
            s = self._slot(w)
            if s[0] is not None:
                deps.add(s[0])
            for rd in s[1].values():
                deps.add(rd)
        for r in reads:
            self._slot(r)[1][okey] = o
        for w in writes:
            s = self._slot(w)
            s[0] = o
            s[1] = {}
        deps.discard(o)
        o.deps = tuple(deps)
        self.ops.append(o)
        self.streams[stream].append(o)
        if fn is not None:
            self.last[okey] = o
        return o

    def dma(self, stream, fn, semkey, nparts, reads=(), writes=()):
        return self.op(stream, fn, reads, writes, semkey=semkey, nparts=nparts)

    def barrier(self):
        lasts = list(self.last.values())
        for s in self.streams:
            self.op(s, None, extra_deps=lasts)
        self.slots = {k: v for k, v in self.slots.items() if k.startswith("D:")}

    def resolve(self):
        known = {s: {} for s in self.streams}
        seq = {}
        cnt = {}
        for o in self.ops:
            key = o.semkey if o.semkey is not None else o.stream
            cnt[key] = cnt.get(key, 0) + 1
            seq[o] = (key, cnt[key])
        for o in self.ops:
            need = {}
            kn = known[o.stream]
            for d in o.deps:
                key, n = seq[d]
                if d.semkey is None and d.stream == o.stream and not SAME_ENGINE_SYNC[o.stream]:
                    continue
                if kn.get(key, 0) >= n:
                    continue
                if key not in need or seq[need[key]][1] < n:
                    need[key] = d
            o.waits = list(need.values())
            for key, d in need.items():
                kn[key] = seq[d][1]
                d.flag = True
            o.deps = ()
        ccount = {}
        for o in self.ops:
            if o.semkey is not None:
                ccount[o.semkey] = ccount.get(o.semkey, 0) + 16 * o.nparts
                o.count = ccount[o.semkey]
            elif o.flag:
                ccount[o.stream] = ccount.get(o.stream, 0) + 1
                o.count = ccount[o.stream]
        return ccount

    def emit(self, nc, stack):
        ccount = self.resolve()
        sems = {}
        for i, key in enumerate(ccount):
            sems[key] = stack.enter_context(nc.semaphore("s%d" % i))
        engmap = {"pe": "tensor", "act": "scalar", "dve": "vector", "pool": "gpsimd", "sp": "sync"}
        block = stack.enter_context(nc.Block())
        for sname, ops in self.streams.items():
            if not ops:
                continue

            def body(e, ops=ops, sname=sname):
                for o in ops:
                    for d in o.waits:
                        key = d.semkey if d.semkey is not None else d.stream
                        e.wait_ge(sems[key], d.count)
                    if o.fn is None:
                        continue
                    if o.semkey is not None:
                        o.fn(e, sems[o.semkey])
                    else:
                        ins = o.fn(e)
                        if o.flag:
                            ins.then_inc(sems[sname], 1)

            getattr(block, engmap[sname])(body)
        return ccount

    def final_wait(self, stream="sp"):
        self.op(stream, None, extra_deps=list(self.last.values()))


class Arena:
    def __init__(self, base_ap, nwords):
        self.base = base_ap
        self.nwords = nwords
        self.off = 0

    def reset(self):
        self.off = 0

    def alloc(self, shape, dt):
        n = 1
        for s in shape[1:]:
            n *= s
        nb = n * (2 if dt == BF16 else 4)
        n4 = (nb + 3) // 4
        n4 = (n4 + 7) // 8 * 8
        assert self.off + n4 <= self.nwords, ("SBUF arena overflow", self.off + n4, self.nwords)
        a = self.base[:, self.off:self.off + n4]
        self.off += n4
        if dt == BF16:
            a = a.bitcast(BF16)
        a = a[:, 0:n]
        if len(shape) == 3:
            a = a.rearrange("p (a b) -> p a b", b=shape[2])
        elif len(shape) == 4:
            a = a.rearrange("p (a b c) -> p a b c", b=shape[2], c=shape[3])
        return a


def na_specials(S, nact):
    nb = S // 128
    half = nb // 2
    if nact == S:
        sp = {0, 1, nb - 2, nb - 1}
    else:
        sp = {0, 1, nb - 2, nb - 1, half - 2, half - 1, half, half + 1}
    return sorted(x for x in sp if 0 <= x < nb)


def build_program(slots, debug=False, stop_after=None):
    nc = bass.Bass("TRN2", target_bir_lowering=False)
    T = sum(S for S, _ in slots)
    TA = sum(a for _, a in slots)
    NCH = T // 512
    slot_tok0 = np.cumsum([0] + [S for S, _ in slots]).tolist()
    slot_out0 = np.cumsum([0] + [a for _, a in slots]).tolist()
    nsp_tot = sum(len(na_specials(S, a)) for S, a in slots)

    def din(name, shape, dt=F32):
        return nc.dram_tensor(name, list(shape), dt, kind="ExternalInput").ap()

    skind = "ExternalOutput" if debug else "Internal"

    def dscr(name, shape, dt):
        return nc.dram_tensor(name, list(shape), dt, kind=skind).ap()

    x_in = din("x_in", [T, D])
    y_out = nc.dram_tensor("y_out", [TA, D], F32, kind="ExternalOutput").ap()
    win = din("win", [L, D, NIN])
    wuq = din("wuq", [L, 384, 1152])
    wukv = din("wukv", [L, 256, 768])
    wout = din("wout", [L, D, D])
    wg = din("wg", [L, D, DFF])
    wu = din("wu", [L, D, DFF])
    wd = din("wd", [L, DFF, D])
    gsm = din("gsm", [L, 128, 17])
    gbc = din("gbc", [L, 2, D])
    gfin = din("gfin", [D])
    ropeC = din("ropeC", [2, 128, T])
    ropeB = din("ropeB", [2, 32, T])
    nab_int = din("nab_int", [L, 128, 5, 512])
    nab_sp = din("nab_sp", [L, nsp_tot * 7, 128, 512])
    ident_d = din("ident", [128, 128], BF16)
    bdones_d = din("bdones", [128, 128], BF16)

    x1 = dscr("x1", [T, D], F32)
    xmid = dscr("xmid", [T, D], F32)
    qaT = dscr("qaT", [256, T], BF16)
    kaT = dscr("kaT", [256, T], BF16)
    va = dscr("va", [T, 4, 65], BF16)
    qbT = dscr("qbT", [6, 96, T], BF16)
    kbT = dscr("kbT", [6, 96, T], BF16)
    vb = dscr("vb", [T, 6, 65], BF16)
    qcT = dscr("qcT", [384, T], BF16)
    kcT = dscr("kcT", [128, T], BF16)
    vc = dscr("vc", [T, 2, 65], BF16)
    oT = dscr("oT", [D, T], F32)

    P = Prog()
    st = contextlib.ExitStack()
    AW = 51200
    arena_t = st.enter_context(nc.sbuf_tensor("arena", [128, AW], F32))
    A = Arena(arena_t[:], AW)
    psall_t = st.enter_context(nc.psum_tensor("psall", [128, 4096], F32))
    psall = psall_t[:]
    PS = [psall[:, 512 * i:512 * (i + 1)] for i in range(8)]
    PSB = [psall[:, 512 * i:512 * (i + 1)].bitcast(BF16) for i in range(8)]

    dq = ["sp"]

    def dma(fn, key, n, reads=(), writes=(), q=None):
        return P.dma(q or "sp", fn, "q_" + key, n, reads, writes)

    def simple_dma(out, in_, key, reads=(), writes=(), q=None):
        return dma(lambda e, s: e.dma_start(out=out, in_=in_).then_inc(s, 16), key, 1, reads, writes, q)

    def mm(out, lhsT, rhs, start, stop, reads, writes):
        return P.op("pe", lambda e: e.matmul(out, lhsT=lhsT, rhs=rhs, start=start, stop=stop), reads, writes)

    def act(out, in_, func, reads, writes, **kw):
        return P.op("act", lambda e: e.activation(out=out, in_=in_, func=func, **kw), reads, writes)

    def vcopy(eng, out, in_, reads, writes):
        if eng == "act":
            return act(out, in_, AF.Copy, reads, writes)
        return P.op(eng, lambda e: e.tensor_copy(out=out, in_=in_), reads, writes)

    def tt(eng, out, in0, in1, op, reads, writes):
        return P.op(eng, lambda e: e.tensor_tensor(out=out, in0=in0, in1=in1, op=op), reads, writes)

    def stt(eng, out, in0, scalar, in1, op0, op1, reads, writes):
        return P.op(eng, lambda e: e.scalar_tensor_tensor(out=out, in0=in0, scalar=scalar, in1=in1, op0=op0, op1=op1),
                    reads, writes)

    consts = A.alloc([128, 8], F32)
    ident = A.alloc([128, 128], BF16)
    bdones = A.alloc([128, 128], BF16)
    ones_bf = A.alloc([128, 128], BF16)
    sel_b = A.alloc([128, 64], BF16)
    gfin_b = None
    GLOBAL_OFF = None

    P.op("dve", lambda e: e.memset(consts[:, 0:1], EPS), writes=["consts"])
    P.op("dve", lambda e: e.memset(consts[:, 1:2], 1.0), writes=["consts"])
    P.op("dve", lambda e: e.memset(ones_bf, 1.0), writes=["ones_bf"])
    P.op("dve", lambda e: e.memset(sel_b, 0.0), writes=["sel_b"])
    P.op("dve", lambda e: e.memset(sel_b[64:65, :], 1.0), writes=["sel_b"])
    simple_dma(ident, ident_d, "c0", writes=["ident"])
    simple_dma(bdones, bdones_d, "c1", writes=["bdones"])
    GLOBAL_OFF = A.off

    def eps_col(ap):
        b = ap.base_partition()
        return consts[b:b + ap.shape[0], 0:1]

    def rsqrt(ap, n, src, reads, slot):
        act(ap, src, AF.Sqrt, list(reads) + ["consts"], [slot], scale=1.0 / n, bias=eps_col(ap))
        P.op("dve", lambda e: e.reciprocal(out=ap, in_=ap), [slot], [slot])

    def chunk_slot(c):
        t = 512 * c
        for i in range(len(slots)):
            if slot_tok0[i] <= t < slot_tok0[i + 1]:
                return i
        raise AssertionError

    def chunk_active(c, l):
        if l == 0:
            return True
        i = chunk_slot(c)
        return 512 * c - slot_tok0[i] < slots[i][1]

    def out_row0(c):
        i = chunk_slot(c)
        return slot_out0[i] + 512 * c - slot_tok0[i]

    def phase_A(l):
        P.barrier()
        A.off = GLOBAL_OFF
        x_src = x_in if l == 0 else x1
        win_sb = A.alloc([128, 8, NIN], BF16)
        wuq_sb = A.alloc([128, 3, 1152], BF16)
        wukv_sb = A.alloc([128, 2, 768], BF16)
        gsm_sb = A.alloc([128, 17], F32)
        gA = A.alloc([128, D], F32)
        xa = [A.alloc([128, 4, D], F32) for _ in range(2)]
        junk = A.alloc([128, D], F32)
        ss = [A.alloc([128, 4], F32) for _ in range(2)]
        hb = [A.alloc([128, D], BF16) for _ in range(2)]
        hT = [A.alloc([128, 8, 512], BF16) for _ in range(2)]
        rC = [A.alloc([128, 2, 512], F32) for _ in range(2)]
        rB = [A.alloc([128, 2, 512], F32) for _ in range(2)]
        NSTG = 8
        stg = [A.alloc([128, 512], BF16) for _ in range(NSTG)]
        cqf = A.alloc([128, 5, 512], F32)
        sqb = [A.alloc([128, 512], BF16) for _ in range(2)]
        cn = A.alloc([128, 5, 512], BF16)
        rs = [A.alloc([128, 512], F32) for _ in range(2)]
        tA = [A.alloc([128, 512], F32) for _ in range(2)]
        tB = [A.alloc([128, 512], F32) for _ in range(2)]
        vst = A.alloc([128, 4, 6, 65], BF16)
        vsb = A.alloc([128, 4, 6, 65], BF16)
        kpe_s = A.alloc([128, 512], BF16)

        for k in range(8):
            dma(lambda e, s, k=k: e.dma_start(out=win_sb[:, k, :], in_=win[l, 128 * k:128 * (k + 1), :]).then_inc(s, 16),
                "w_in%d" % k, 1, writes=["win_sb"], q="pool")
        dma(lambda e, s: e.dma_start(out=wuq_sb, in_=wuq[l].rearrange("(k p) n -> p k n", p=128)).then_inc(s, 16),
            "w_uq", 1, writes=["wuq_sb"], q="pool")
        dma(lambda e, s: e.dma_start(out=wukv_sb, in_=wukv[l].rearrange("(k p) n -> p k n", p=128)).then_inc(s, 16),
            "w_ukv", 1, writes=["wukv_sb"], q="pool")
        simple_dma(gsm_sb, gsm[l], "g0", writes=["gsm_sb"])
        simple_dma(gA, gbc[l, 0].partition_broadcast(128), "g1", writes=["gA"])
        P.op("pool", lambda e: e.memset(vst, 1.0), writes=["vst"])
        P.op("pool", lambda e: e.memset(vsb, 1.0), writes=["vsb"])

        bank_rr = [0]

        def nbank():
            b = 2 + bank_rr[0] % 4
            bank_rr[0] += 1
            return b

        stg_rr = [0]

        def nstg():
            i = stg_rr[0] % NSTG
            stg_rr[0] += 1
            return i

        ev_rr = [0]

        def ev_eng():
            ev_rr[0] += 1
            return "act" if ev_rr[0] % 2 else "dve"

        def front(c):
            i = c % 2
            t0 = 512 * c
            dma(lambda e, s: e.dma_start(out=xa[i], in_=x_src[t0:t0 + 512, :].rearrange("(b p) d -> p b d", p=128)).then_inc(s, 16),
                "xa%d" % i, 1, reads=["D:x%d:%d" % (l, c)], writes=["xa%d" % i])
            dma(lambda e, s: e.dma_start(out=rC[i], in_=ropeC[:, :, t0:t0 + 512].rearrange("a p t -> p a t")).then_inc(s, 16),
                "rC%d" % i, 1, writes=["rC%d" % i])
            dma(lambda e, s: e.dma_start(out=rB[i][64:96], in_=ropeB[:, :, t0:t0 + 512].rearrange("a p t -> p a t")).then_inc(s, 16),
                "rB%d" % i, 1, writes=["rB%d" % i])
            for b in range(4):
                act(junk, xa[i][:, b, :], AF.Square, ["xa%d" % i], ["junk", "ss%d" % i], accum_out=ss[i][:, b:b + 1])
            rsqrt(ss[i], D, ss[i], ["ss%d" % i], "ss%d" % i)
            for b in range(4):
                j = b % 2
                stt("dve", hb[j], xa[i][:, b, :], ss[i][:, b:b + 1], gA, ALU.mult, ALU.mult,
                    ["xa%d" % i, "ss%d" % i, "gA"], ["hb%d" % j])
                for k in range(8):
                    P.op("pe", lambda e, k=k, j=j: e.transpose(out=PSB[j][:, 128 * k:128 * (k + 1)], in_=hb[j][:, 128 * k:128 * (k + 1)], identity=ident),
                         ["hb%d" % j, "ident"], ["ps%d" % j])
                vcopy("act" if b % 2 else "dve", hT[i][:, :, 128 * b:128 * (b + 1)],
                      PSB[j].rearrange("p (k t) -> p k t", t=128), ["ps%d" % j], ["hT%d" % i])

        def fm(i, col0, M, wsb=None, nk=8, rhs_fn=None, rslot=None):
            wsb = win_sb if wsb is None else wsb
            b = nbank()
            for k in range(nk):
                rhs = hT[i][:, k, :] if rhs_fn is None else rhs_fn(k)
                mm(PS[b][0:M, :], wsb[:, k, col0:col0 + M], rhs, k == 0, k == nk - 1,
                   (list(rslot) if rslot else ["hT%d" % i]) + ["win_sb", "wuq_sb", "wukv_sb"], ["ps%d" % b])
            return b

        def store_fm(ap_sb, dst, key_i, c, name, npart=128):
            simple_dma(dst, ap_sb, "stg%d" % key_i, reads=["stg%d" % key_i], writes=["D:%s:%d" % (name, c)])

        def back(c):
            i = c % 2
            t0 = 512 * c
            ts = slice(t0, t0 + 512)
            hs = "hT%d" % i
            for name, col0, dst in (("qaT", C_QA, qaT), ("kaT", C_KA, kaT)):
                for j in range(2):
                    b = fm(i, col0 + 128 * j, 128)
                    si = nstg()
                    vcopy(ev_eng(), stg[si], PS[b], ["ps%d" % b], ["stg%d" % si])
                    store_fm(stg[si], dst[128 * j:128 * (j + 1), ts], si, c, name)
            if CUT == 3:
                return
            for (col0, nchk, off, gcol, nfeat, nb_) in ((C_CQ, 3, 0, 0, 384, 6), (C_CKV, 2, 3, 3, 256, 7)):
                for j in range(nchk):
                    b = fm(i, col0 + 128 * j, 128)
                    q = j % 2
                    act(sqb[q], PS[b], AF.Square, ["ps%d" % b], ["sqb%d" % q])
                    vcopy("dve", cqf[:, off + j, :], PS[b], ["ps%d" % b], ["cqf%d" % (off + j)])
                    mm(PS[nb_], ones_bf, sqb[q], j == 0, j == nchk - 1, ["sqb%d" % q, "ones_bf"], ["ps%d" % nb_])
                r = (off // 3) % 2
                rsqrt(rs[r], nfeat, PS[nb_], ["ps%d" % nb_], "rs%d" % r)
                for j in range(nchk):
                    stt("dve", cn[:, off + j, :], cqf[:, off + j, :], gsm_sb[:, gcol + j:gcol + j + 1], rs[r], ALU.mult, ALU.mult,
                        ["cqf%d" % (off + j), "gsm_sb", "rs%d" % r], ["cn%d" % (off + j)])
            if CUT == 4:
                return
            bA = fm(i, C_KPA, 96)
            bB = fm(i, C_KPB, 96)
            tt("dve", tA[0][64:96], PS[bA][64:96], rB[i][64:96, 0, :], ALU.mult, ["ps%d" % bA, "rB%d" % i], ["tA0"])
            tt("dve", tB[0][64:96], PS[bB][64:96], rB[i][64:96, 1, :], ALU.mult, ["ps%d" % bB, "rB%d" % i], ["tB0"])
            tt("dve", kpe_s[64:96], tA[0][64:96], tB[0][64:96], ALU.add, ["tA0", "tB0"], ["kpe_s"])
            dma(lambda e, s: [e.dma_start(out=kbT[h, 64:96, ts], in_=kpe_s[64:96]).then_inc(s, 16) for h in range(6)],
                "kpe", 6, reads=["kpe_s"], writes=["D:kbTp:%d" % c])
            if CUT == 5:
                return
            for (colA, colB, gc, name, dst, row0) in ([(C_QCA + 128 * j, C_QCB + 128 * j, 5, "qcT", qcT, 128 * j) for j in range(3)]
                                                      + [(C_KCA, C_KCB, 7, "kcT", kcT, 0)]):
                bA = fm(i, colA, 128)
                bB = fm(i, colB, 128)
                q = ev_rr[0] % 2
                ev_rr[0] += 1
                act(sqb[q], PS[bA], AF.Square, ["ps%d" % bA], ["sqb%d" % q])
                act(tA[q], PS[bA], AF.Copy, ["ps%d" % bA, "gsm_sb"], ["tA%d" % q], scale=gsm_sb[:, gc:gc + 1])
                act(tB[q], PS[bB], AF.Copy, ["ps%d" % bB, "gsm_sb"], ["tB%d" % q], scale=gsm_sb[:, gc + 1:gc + 2])
                nb_ = 6 + q
                mm(PS[nb_], bdones, sqb[q], True, True, ["sqb%d" % q, "bdones"], ["ps%d" % nb_])
                rsqrt(rs[q], 64, PS[nb_], ["ps%d" % nb_], "rs%d" % q)
                tt("pool", tB[q], tB[q], rC[i][:, 1, :], ALU.mult, ["tB%d" % q, "rC%d" % i], ["tB%d" % q])
                tt("dve", tA[q], tA[q], rC[i][:, 0, :], ALU.mult, ["tA%d" % q, "rC%d" % i], ["tA%d" % q])
                tt("dve", tA[q], tA[q], tB[q], ALU.add, ["tA%d" % q, "tB%d" % q], ["tA%d" % q])
                si = nstg()
                tt("dve", stg[si], tA[q], rs[q], ALU.mult, ["tA%d" % q, "rs%d" % q], ["stg%d" % si])
                store_fm(stg[si], dst[row0:row0 + 128, ts], si, c, name)
            if CUT == 6:
                return
            for h in range(6):
                bA = fm(i, 192 * h, 96, wsb=wuq_sb, nk=3, rhs_fn=lambda k: cn[:, k, :], rslot=("cn0", "cn1", "cn2"))
                bB = fm(i, 192 * h + 96, 96, wsb=wuq_sb, nk=3, rhs_fn=lambda k: cn[:, k, :], rslot=("cn0", "cn1", "cn2"))
                si = nstg()
                q = h % 2
                vcopy("act", stg[si][0:64], PS[bA][0:64], ["ps%d" % bA], ["stg%d" % si])
                tt("dve", tA[q][64:96], PS[bA][64:96], rB[i][64:96, 0, :], ALU.mult, ["ps%d" % bA, "rB%d" % i], ["tA%d" % q])
                tt("dve", tB[q][64:96], PS[bB][64:96], rB[i][64:96, 1, :], ALU.mult, ["ps%d" % bB, "rB%d" % i], ["tB%d" % q])
                tt("dve", stg[si][64:96], tA[q][64:96], tB[q][64:96], ALU.add, ["tA%d" % q, "tB%d" % q], ["stg%d" % si])
                simple_dma(qbT[h, :, ts], stg[si][0:96], "stg%d" % si, reads=["stg%d" % si], writes=["D:qbT:%d" % c])
            if CUT == 7:
                return
            for h in range(6):
                b = fm(i, 64 * h, 64, wsb=wukv_sb, nk=2, rhs_fn=lambda k: cn[:, 3 + k, :], rslot=("cn3", "cn4"))
                si = nstg()
                vcopy(ev_eng(), stg[si][0:64], PS[b][0:64], ["ps%d" % b], ["stg%d" % si])
                simple_dma(kbT[h, 0:64, ts], stg[si][0:64], "stg%d" % si, reads=["stg%d" % si], writes=["D:kbTn:%d" % c])
            if CUT == 8:
                return
            for b4 in range(4):
                b = nbank()
                for k in range(8):
                    mm(PS[b][:, 0:384], hT[i][:, k, 128 * b4:128 * (b4 + 1)], win_sb[:, k, C_V:C_V + 384], k == 0, k == 7,
                       [hs, "win_sb"], ["ps%d" % b])
                vcopy(ev_eng(), vst[:, b4, :, 0:64], PS[b][:, 0:384].rearrange("p (h d) -> p h d", d=64), ["ps%d" % b], ["vst"])
                b = nbank()
                for k in range(2):
                    mm(PS[b][:, 0:384], cn[:, 3 + k, 128 * b4:128 * (b4 + 1)], wukv_sb[:, k, 384:768], k == 0, k == 1,
                       ["cn3", "cn4", "wukv_sb"], ["ps%d" % b])
                vcopy(ev_eng(), vsb[:, b4, :, 0:64], PS[b][:, 0:384].rearrange("p (h d) -> p h d", d=64), ["ps%d" % b], ["vsb"])
            simple_dma(va[ts].rearrange("(b p) h d -> p b h d", p=128), vst[:, :, 0:4, :], "vst_a", reads=["vst"], writes=["D:va:%d" % c])
            simple_dma(vc[ts].rearrange("(b p) h d -> p b h d", p=128), vst[:, :, 4:6, :], "vst_c", reads=["vst"], writes=["D:vc:%d" % c])
            simple_dma(vb[ts].rearrange("(b p) h d -> p b h d", p=128), vsb, "vsb", reads=["vsb"], writes=["D:vb:%d" % c])

        import os
        CUT = int(os.environ.get("K_CUT", "0"))
        if CUT == 1:
            return
        front(0)
        if CUT == 2:
            return
        for c in range(NCH):
            if c + 1 < NCH:
                front(c + 1)
            back(c)
            if CUT:
                return

    def attn_epilogue(acc_bank, bc_bank, ebuf, row0, t0, ntok, c_list, tag):
        osb, rc, hi, lo = ebuf

        def part1():
            vcopy("dve", osb[0:65, 0:ntok], PS[acc_bank][0:65, 0:ntok], ["ps%d" % acc_bank], [tag + "osb"])
            P.op("dve", lambda e: e.reciprocal(out=rc[64:65, 0:ntok], in_=osb[64:65, 0:ntok]), [tag + "osb"], [tag + "rc"])
            vcopy("dve", hi[64:65, 0:ntok], rc[64:65, 0:ntok], [tag + "rc"], [tag + "hi"])
            tt("dve", lo[64:65, 0:ntok], rc[64:65, 0:ntok], hi[64:65, 0:ntok], ALU.subtract, [tag + "rc", tag + "hi"], [tag + "lo"])

        def part2():
            mm(PS[bc_bank][0:64, 0:ntok], sel_b, hi[:, 0:ntok], True, False, [tag + "hi", "sel_b"], ["ps%d" % bc_bank])
            mm(PS[bc_bank][0:64, 0:ntok], sel_b, lo[:, 0:ntok], False, True, [tag + "lo", "sel_b"], ["ps%d" % bc_bank])
            tt("dve", osb[0:64, 0:ntok], osb[0:64, 0:ntok], PS[bc_bank][0:64, 0:ntok], ALU.mult,
               [tag + "osb", "ps%d" % bc_bank], [tag + "osb"])
            simple_dma(oT[row0:row0 + 64, t0:t0 + ntok], osb[0:64, 0:ntok], tag + "osb", reads=[tag + "osb"],
                       writes=["D:oT%d:%d" % (row0, c) for c in c_list])

        return part1, part2

    def emit_sorted(items):
        items.sort(key=lambda t: (t[0], t[1]))
        for _, _, fn in items:
            fn()

    def phase_B_dense(l):
        P.barrier()
        A.off = GLOBAL_OFF
        SM = max(S for S, _ in slots)
        NB = SM // 128
        kbuf = [A.alloc([128, SM], BF16) for _ in range(2)]
        vbuf = [A.alloc([128, NB, 65], BF16) for _ in range(2)]
        qbuf = [A.alloc([128, SM], BF16) for _ in range(2)]
        NPT = 4
        pt = [A.alloc([128, 1024], BF16) for _ in range(NPT)]
        ebufs = [(A.alloc([128, 512], F32), A.alloc([128, 512], F32), A.alloc([128, 512], BF16), A.alloc([128, 512], BF16))
                 for _ in range(2)]
        for i_, eb_ in enumerate(ebufs):
            P.op("pool", lambda e, eb_=eb_: e.memset(eb_[2], 0.0), writes=["e%dhi" % i_])
            P.op("pool", lambda e, eb_=eb_: e.memset(eb_[3], 0.0), writes=["e%dlo" % i_])
        work = []
        for si, (S, nact) in enumerate(slots):
            na = S if l == 0 else nact
            for h in range(6):
                work.append((si, "b", h, [h], na))
            for kv in range(2):
                work.append((si, "c", kv, [3 * kv, 3 * kv + 1, 3 * kv + 2], na))
        qjobs = []
        for wi, (si, kind, kv, qhs, na) in enumerate(work):
            for qh in qhs:
                qjobs.append((wi, qh))
        for i_ in range(2):
            P.op("pool", lambda e, i_=i_: e.memset(kbuf[i_], 0.0), writes=["kbuf%d" % i_])
            P.op("pool", lambda e, i_=i_: e.memset(qbuf[i_], 0.0), writes=["qbuf%d" % i_])

        def load_kv(wi):
            si, kind, kv, qhs, na = work[wi]
            S = slots[si][0]
            t0 = slot_tok0[si]
            i = wi % 2
            cl = range(t0 // 512, (t0 + S) // 512)
            if kind == "b":
                simple_dma(kbuf[i][0:96, 0:S], kbT[kv, :, t0:t0 + S], "kb%d" % i,
                           reads=["D:kbTn:%d" % c for c in cl] + ["D:kbTp:%d" % c for c in cl], writes=["kbuf%d" % i])
                simple_dma(vbuf[i][:, 0:S // 128, :], vb[t0:t0 + S, kv, :].rearrange("(b p) d -> p b d", p=128), "vb%d" % i,
                           reads=["D:vb:%d" % c for c in cl], writes=["vbuf%d" % i])
            else:
                P.op("pool", lambda e: e.memset(kbuf[i][64:128, 0:S], 0.0), writes=["kbuf%d" % i])
                simple_dma(kbuf[i][0:64, 0:S], kcT[64 * kv:64 * kv + 64, t0:t0 + S], "kb%d" % i,
                           reads=["D:kcT:%d" % c for c in cl], writes=["kbuf%d" % i])
                simple_dma(vbuf[i][:, 0:S // 128, :], vc[t0:t0 + S, kv, :].rearrange("(b p) d -> p b d", p=128), "vb%d" % i,
                           reads=["D:vc:%d" % c for c in cl], writes=["vbuf%d" % i])

        def load_q(qi):
            wi, qh = qjobs[qi]
            si, kind, kv, qhs, na = work[wi]
            t0 = slot_tok0[si]
            i = qi % 2
            cl = range(t0 // 512, (t0 + na) // 512)
            if kind == "b":
                simple_dma(qbuf[i][0:96, 0:na], qbT[qh, :, t0:t0 + na], "qb%d" % i,
                           reads=["D:qbT:%d" % c for c in cl], writes=["qbuf%d" % i])
            else:
                simple_dma(qbuf[i][0:64, 0:na], qcT[64 * qh:64 * qh + 64, t0:t0 + na], "qb%d" % i,
                           reads=["D:qcT:%d" % c for c in cl], writes=["qbuf%d" % i])

        items = []
        seqn = [0]

        def add(pos, fn):
            items.append((pos, seqn[0], fn))
            seqn[0] += 1

        LOOK = 2
        n = 0
        nacc = 0
        add(-2.0, lambda: load_kv(0))
        add(-2.0, lambda: load_q(0))
        for qi, (wi, qh) in enumerate(qjobs):
            si, kind, kv, qhs, na = work[wi]
            S = slots[si][0]
            t0 = slot_tok0[si]
            if qi + 1 < len(qjobs):
                nwi = qjobs[qi + 1][0]
                if nwi != wi:
                    add(n - 1 + LOOK + 0.55, lambda nwi=nwi: load_kv(nwi))
                add(n - 0.5, lambda qi=qi: load_q(qi + 1))
            ki = wi % 2
            qb_ = qi % 2
            dk = 96 if kind == "b" else 128
            scale = 1.0 / math.sqrt(96 if kind == "b" else 64)
            row0 = (256 + 64 * qh) if kind == "b" else (640 + 64 * qh)
            nkb = S // 128
            npair = nkb // 2
            for qc in range(na // 512):
                ab, bb = 6, 7
                eb = ebufs[nacc % 2]
                etag = "e%d" % (nacc % 2)
                nacc += 1
                for kp in range(npair):
                    sp_ = n % 3
                    pi = n % NPT

                    def front(kp=kp, sp_=sp_, pi=pi, ki=ki, qb_=qb_, qc=qc, dk=dk, scale=scale):
                        for u in range(2):
                            kb = 2 * kp + u
                            mm(PS[2 * sp_ + u], kbuf[ki][0:dk, 128 * kb:128 * (kb + 1)], qbuf[qb_][0:dk, 512 * qc:512 * (qc + 1)], True, True,
                               ["kbuf%d" % ki, "qbuf%d" % qb_], ["ps%d" % (2 * sp_ + u)])
                        act(pt[pi], psall[:, 1024 * sp_:1024 * (sp_ + 1)], AF.Exp, ["ps%d" % (2 * sp_), "ps%d" % (2 * sp_ + 1)],
                            ["pt%d" % pi], scale=scale)

                    def back(kp=kp, pi=pi, ki=ki, nkb=nkb, ab=ab):
                        for u in range(2):
                            kb = 2 * kp + u
                            mm(PS[ab][0:65, :], vbuf[ki][:, kb, :], pt[pi][:, 512 * u:512 * (u + 1)], kb == 0, kb == nkb - 1,
                               ["vbuf%d" % ki, "pt%d" % pi], ["ps%d" % ab])

                    add(n, front)
                    add(n + LOOK + 0.5, back)
                    n += 1
                tq = t0 + 512 * qc
                p1, p2 = attn_epilogue(ab, bb, eb, row0, tq, 512, [tq // 512], etag)
                add(n - 1 + LOOK + 0.6, p1)
                add(n - 1 + LOOK + 4.7, p2)
        emit_sorted(items)

    def phase_B_na(l):
        import os
        NACUT = int(os.environ.get("K_NACUT", "0"))
        P.barrier()
        A.off = GLOBAL_OFF
        SM = max(S for S, _ in slots)
        NB = SM // 128
        qz = A.alloc([128, 4, SM], BF16)
        ka_sb = A.alloc([128, 2, SM], BF16)
        P.op("pool", lambda e: e.memset(qz, 0.0), writes=["qz"])
        va_sb = A.alloc([128, NB, 4 * 65], BF16)
        bint = A.alloc([128, 5, 512], F32)
        bsp = [A.alloc([128, 512], F32) for _ in range(3)]
        sbias = [A.alloc([128, 512], F32) for _ in range(3)]
        pt = [A.alloc([128, 512], BF16) for _ in range(3)]
        ebufs = [(A.alloc([128, 512], F32), A.alloc([128, 512], F32), A.alloc([128, 512], BF16), A.alloc([128, 512], BF16))
                 for _ in range(2)]
        for i_, eb_ in enumerate(ebufs):
            P.op("pool", lambda e, eb_=eb_: e.memset(eb_[2], 0.0), writes=["e%dhi" % i_])
            P.op("pool", lambda e, eb_=eb_: e.memset(eb_[3], 0.0), writes=["e%dlo" % i_])
        simple_dma(bint, nab_int[l], "bint", writes=["bint"])
        sp_base = 0
        items = []
        seqn = [0]

        def add(pos, fn):
            items.append((pos, seqn[0], fn))
            seqn[0] += 1

        LOOK = 2
        n = 0
        spn = 0
        nacc = 0
        for si, (S, nact) in enumerate(slots):
            na = S if l == 0 else nact
            t0 = slot_tok0[si]
            nb = S // 128
            cl = list(range(t0 // 512, (t0 + S) // 512))
            specials = na_specials(S, nact)

            def loads(t0=t0, S=S, nb=nb, cl=cl):
                dma(lambda e, s_: [e.dma_start(out=qz[64 * (hd % 2):64 * (hd % 2) + 64, hd, 0:S],
                                               in_=qaT[64 * hd:64 * hd + 64, t0:t0 + S]).then_inc(s_, 16) for hd in range(4)],
                    "qz", 4, reads=["D:qaT:%d" % c for c in cl], writes=["qz"])
                simple_dma(ka_sb[:, :, 0:S], kaT[:, t0:t0 + S].rearrange("(j p) t -> p j t", p=128), "ka_sb",
                           reads=["D:kaT:%d" % c for c in cl], writes=["ka_sb"])
                simple_dma(va_sb[:, 0:nb, :], va[t0:t0 + S].rearrange("(b p) h d -> p b (h d)", p=128), "va_sb",
                           reads=["D:va:%d" % c for c in cl], writes=["va_sb"])

            add(n + LOOK + 5.0 if si else -1.0, loads)
            n += (LOOK + 6) if si else 0
            for g in range(na // 512):
                for qi4 in range(4):
                    i = 4 * g + qi4
                    if i in specials:
                        sidx = sp_base + specials.index(i)
                        offs = [(d, ("sp", sidx * 7 + di)) for di, d in enumerate(NA_OFFS)]
                    else:
                        offs = [(d, ("int", d + 2)) for d in (-2, -1, 0, 1, 2)]
                    for oi, (d, (kind, bidx)) in enumerate(offs):
                        j = (i + d) % nb
                        sb_ = n % 3
                        bi = None
                        if kind == "sp":
                            bi = spn % 3
                            spn += 1

                        def front(i=i, j=j, sb_=sb_, kind=kind, bidx=bidx, bi=bi):
                            for c4 in range(4):
                                hd = HS[c4]
                                mm(PS[sb_][:, 128 * c4:128 * (c4 + 1)], ka_sb[:, hd // 2, 128 * j:128 * (j + 1)],
                                   qz[:, hd, 128 * i:128 * (i + 1)], True, True, ["ka_sb", "qz"], ["ps%d" % sb_])
                            if kind == "sp":
                                simple_dma(bsp[bi], nab_sp[l, bidx], "bsp%d" % bi, writes=["bsp%d" % bi])
                                bias_ap, bias_slot = bsp[bi], "bsp%d" % bi
                            else:
                                bias_ap, bias_slot = bint[:, bidx, :], "bint"
                            stt("dve", sbias[sb_], PS[sb_], 0.125, bias_ap, ALU.mult, ALU.add, ["ps%d" % sb_, bias_slot], ["sbias%d" % sb_])
                            act(pt[sb_], sbias[sb_], AF.Exp, ["sbias%d" % sb_], ["pt%d" % sb_])

                        def back(j=j, sb_=sb_, qi4=qi4, first=(oi == 0), last=(oi == len(offs) - 1)):
                            for c4 in range(4):
                                hd = HS[c4]
                                mm(PS[4 + c4][0:65, 128 * qi4:128 * (qi4 + 1)], va_sb[:, j, 65 * hd:65 * (hd + 1)],
                                   pt[sb_][:, 128 * c4:128 * (c4 + 1)], first, last, ["va_sb", "pt%d" % sb_], ["ps%d" % (4 + c4)])

                        add(n, front)
                        add(n + LOOK + 0.5, back)
                        n += 1
                tq = t0 + 512 * g
                for c4 in range(4):
                    eb = ebufs[nacc % 2]
                    etag = "e%d" % (nacc % 2)
                    nacc += 1
                    p1, p2 = attn_epilogue(4 + c4, 3, eb, 64 * HS[c4], tq, 512, [tq // 512], etag)
                    add(n - 1 + LOOK + 0.6 + 0.01 * c4, p1)
                    add(n - 1 + LOOK + 0.6 + 0.01 * c4 + 0.005, p2)
            sp_base += len(specials)
        emit_sorted(items)

    def phase_C1(l):
        P.barrier()
        A.off = GLOBAL_OFF
        x_src = x_in if l == 0 else x1
        wo_sb = A.alloc([128, 8, D], BF16)
        gsm_sb = A.alloc([128, 17], F32)
        ot = [A.alloc([128, 8, 512], F32) for _ in range(2)]
        sqb = [A.alloc([128, 512], BF16) for _ in range(3)]
        rs = [A.alloc([128, 512], F32) for _ in range(3)]
        mT = A.alloc([128, 8, 512], BF16)
        xa = [A.alloc([128, D], F32) for _ in range(4)]
        dma(lambda e, s: e.dma_start(out=wo_sb, in_=wout[l].rearrange("(k p) n -> p k n", p=128)).then_inc(s, 16),
            "w_o", 1, writes=["wo_sb"], q="pool")
        simple_dma(gsm_sb, gsm[l], "g0", writes=["gsm_sb"])
        chunks = [c for c in range(NCH) if chunk_active(c, l)]
        groups = ((0, 2, 256), (2, 5, 384), (5, 8, 384))

        def load(ci):
            c = chunks[ci]
            i = ci % 2
            simple_dma(ot[i], oT[:, 512 * c:512 * (c + 1)].rearrange("(j p) t -> p j t", p=128), "ot%d" % i,
                       reads=["D:oT%d:%d" % (r, c) for r in range(0, D, 64)], writes=["ot%d" % i])

        load(0)
        xn = [0]
        for ci, c in enumerate(chunks):
            i = ci % 2
            if ci + 1 < len(chunks):
                load(ci + 1)
            for gi, (j0, j1, nf) in enumerate(groups):
                for j in range(j0, j1):
                    q = j % 3
                    act(sqb[q], ot[i][:, j, :], AF.Square, ["ot%d" % i], ["sqb%d" % q])
                    mm(PS[gi], ones_bf, sqb[q], j == j0, j == j1 - 1, ["sqb%d" % q, "ones_bf"], ["ps%d" % gi])
                rsqrt(rs[gi], nf, PS[gi], ["ps%d" % gi], "rs%d" % gi)
                for j in range(j0, j1):
                    stt("dve", mT[:, j, :], ot[i][:, j, :], gsm_sb[:, 9 + j:10 + j], rs[gi], ALU.mult, ALU.mult,
                        ["ot%d" % i, "gsm_sb", "rs%d" % gi], ["mT"])
            for b4 in range(4):
                xi = xn[0] % 4
                xn[0] += 1
                r0 = 512 * c + 128 * b4
                simple_dma(xa[xi], x_src[r0:r0 + 128, :], "xa%d" % xi, reads=["D:x%d:%d" % (l, c)], writes=["xa%d" % xi])
                for hf in range(2):
                    b = 4 + (2 * b4 + hf) % 4
                    for j in range(8):
                        mm(PS[b], mT[:, j, 128 * b4:128 * (b4 + 1)], wo_sb[:, j, 512 * hf:512 * (hf + 1)], j == 0, j == 7,
                           ["mT", "wo_sb"], ["ps%d" % b])
                    tt("dve", xa[xi][:, 512 * hf:512 * (hf + 1)], xa[xi][:, 512 * hf:512 * (hf + 1)], PS[b], ALU.add,
                       ["xa%d" % xi, "ps%d" % b], ["xa%d" % xi])
                simple_dma(xmid[r0:r0 + 128, :], xa[xi], "xa%d" % xi, reads=["xa%d" % xi], writes=["D:xmid:%d" % c])

    def phase_C2(l):
        P.barrier()
        A.off = GLOBAL_OFF
        wg_sb = A.alloc([128, 8, DFF], BF16)
        wu_sb = A.alloc([128, 8, DFF], BF16)
        wd_sb = A.alloc([128, NFF, D], BF16)
        gF = A.alloc([128, D], F32)
        gL = A.alloc([128, D], F32)
        xm = [A.alloc([128, D], F32) for _ in range(4)]
        junk = A.alloc([128, D], BF16)
        ssv = A.alloc([128, 8], F32)
        hb = [A.alloc([128, D], BF16) for _ in range(2)]
        hT = A.alloc([128, 8, 512], BF16)
        aT = A.alloc([128, NFF, 512], BF16)
        sg = [A.alloc([128, 512], F32) for _ in range(2)]
        for k in range(8):
            dma(lambda e, s, k=k: e.dma_start(out=wg_sb[:, k, :], in_=wg[l, 128 * k:128 * (k + 1), :]).then_inc(s, 16),
                "w_g%d" % k, 1, writes=["wg_sb"], q="pool")
            dma(lambda e, s, k=k: e.dma_start(out=wu_sb[:, k, :], in_=wu[l, 128 * k:128 * (k + 1), :]).then_inc(s, 16),
                "w_u%d" % k, 1, writes=["wu_sb"], q="pool")
        for f0 in range(0, NFF, 4):
            f1 = min(NFF, f0 + 4)
            dma(lambda e, s, f0=f0, f1=f1: e.dma_start(out=wd_sb[:, f0:f1, :], in_=wd[l, 128 * f0:128 * f1, :].rearrange("(k p) n -> p k n", p=128)).then_inc(s, 16),
                "w_d%d" % f0, 1, writes=["wd_sb"], q="pool")
        simple_dma(gF, gbc[l, 1].partition_broadcast(128), "g1", writes=["gF"])
        if l == L - 1:
            simple_dma(gL, gfin.partition_broadcast(128), "g2", writes=["gL"])
        chunks = [c for c in range(NCH) if chunk_active(c, l)]

        def load(ci):
            c = chunks[ci]
            for b4 in range(4):
                r0 = 512 * c + 128 * b4
                simple_dma(xm[b4], xmid[r0:r0 + 128, :], "xm%d" % b4, reads=["D:xmid:%d" % c], writes=["xm%d" % b4])

        load(0)
        for ci, c in enumerate(chunks):
            for b4 in range(4):
                j = b4 % 2
                act(junk, xm[b4], AF.Square, ["xm%d" % b4], ["junk", "ssv"], accum_out=ssv[:, b4:b4 + 1])
                rsqrt(ssv[:, b4:b4 + 1], D, ssv[:, b4:b4 + 1], ["ssv"], "ssv")
                stt("dve", hb[j], xm[b4], ssv[:, b4:b4 + 1], gF, ALU.mult, ALU.mult, ["xm%d" % b4, "ssv", "gF"], ["hb%d" % j])
                for k in range(8):
                    P.op("pe", lambda e, k=k, j=j: e.transpose(out=PSB[j][:, 128 * k:128 * (k + 1)], in_=hb[j][:, 128 * k:128 * (k + 1)], identity=ident),
                         ["hb%d" % j, "ident"], ["ps%d" % j])
                vcopy("act" if b4 % 2 else "dve", hT[:, :, 128 * b4:128 * (b4 + 1)],
                      PSB[j].rearrange("p (k t) -> p k t", t=128), ["ps%d" % j], ["hT"])
            for f in range(NFF):
                bg = 2 + f % 2
                bu = 4 + f % 2
                for k in range(8):
                    mm(PS[bg], wg_sb[:, k, 128 * f:128 * (f + 1)], hT[:, k, :], k == 0, k == 7, ["wg_sb", "hT"], ["ps%d" % bg])
                for k in range(8):
                    mm(PS[bu], wu_sb[:, k, 128 * f:128 * (f + 1)], hT[:, k, :], k == 0, k == 7, ["wu_sb", "hT"], ["ps%d" % bu])
                q = f % 2
                act(sg[q], PS[bg], AF.Silu, ["ps%d" % bg], ["sg%d" % q])
                tt("dve", aT[:, f, :], sg[q], PS[bu], ALU.mult, ["sg%d" % q, "ps%d" % bu], ["aT"])
            for b4 in range(4):
                r0 = 512 * c + 128 * b4
                for hf in range(2):
                    b = 6 + hf
                    for f in range(NFF):
                        mm(PS[b], aT[:, f, 128 * b4:128 * (b4 + 1)], wd_sb[:, f, 512 * hf:512 * (hf + 1)], f == 0, f == NFF - 1,
                           ["aT", "wd_sb"], ["ps%d" % b])
                    tt("dve", xm[b4][:, 512 * hf:512 * (hf + 1)], xm[b4][:, 512 * hf:512 * (hf + 1)], PS[b], ALU.add,
                       ["xm%d" % b4, "ps%d" % b], ["xm%d" % b4])
                if l < L - 1:
                    simple_dma(x1[r0:r0 + 128, :], xm[b4], "xm%d" % b4, reads=["xm%d" % b4], writes=["D:x%d:%d" % (l + 1, c)])
                else:
                    act(junk, xm[b4], AF.Square, ["xm%d" % b4], ["junk", "ssv"], accum_out=ssv[:, 4 + b4:5 + b4])
                    rsqrt(ssv[:, 4 + b4:5 + b4], D, ssv[:, 4 + b4:5 + b4], ["ssv"], "ssv")
                    stt("dve", xm[b4], xm[b4], ssv[:, 4 + b4:5 + b4], gL, ALU.mult, ALU.mult, ["xm%d" % b4, "ssv", "gL"], ["xm%d" % b4])
                    ro = out_row0(c) + 128 * b4
                    simple_dma(y_out[ro:ro + 128, :], xm[b4], "xm%d" % b4, reads=["xm%d" % b4], writes=["D:y:%d" % c])
            if ci + 1 < len(chunks):
                load(ci + 1)

    phases = []
    for l in range(L):
        phases += [("A", l), ("Bna", l), ("Bd", l), ("C1", l), ("C2", l)]
    for name, l in phases:
        {"A": phase_A, "Bna": phase_B_na, "Bd": phase_B_dense, "C1": phase_C1, "C2": phase_C2}[name](l)
        if stop_after == (name, l):
            break
    P.final_wait("sp")
    cc = P.emit(nc, st)
    st.close()
    return nc, (len(P.ops), {k: len(v) for k, v in P.streams.items()}, cc)


def rope_perm(d):
    half = d // 2
    return np.concatenate([np.arange(half, d), np.arange(0, half)])


def rope_tables(pos, d):
    half = d // 2
    inv = (10000.0 ** (-(np.arange(half, dtype=np.float32) * 2.0) / d)).astype(np.float32)
    ang = pos.astype(np.float32)[None, :] * inv[:, None]
    cos = np.cos(ang).astype(np.float32)
    sin = np.sin(ang).astype(np.float32)
    return np.concatenate([cos, cos], 0), np.concatenate([-sin, sin], 0)


def na_bias_tile(rpb_l, rows, tq, tk):
    out = np.full((128, 4, 128), NEG, np.float32)
    nb = rows // 2
    if tk is None or tk < 0 or tk >= nb:
        return out.reshape(128, 512)
    qi = np.arange(128)
    r = 2 * tq + qi // 64
    cq = qi % 64
    ki = np.arange(128)
    kr = 2 * tk + ki // 64
    ck = ki % 64
    kr_n = min(8, rows)
    rs = np.clip(r - kr_n // 2, 0, rows - kr_n)
    cs = np.clip(cq - 8, 0, GRID_W - 16)
    valid = ((kr[:, None] >= rs[None, :]) & (kr[:, None] < rs[None, :] + kr_n)
             & (ck[:, None] >= cs[None, :]) & (ck[:, None] < cs[None, :] + 16))
    dr = np.clip(kr[:, None] - r[None, :] + 7, 0, 14)
    dc = np.clip(ck[:, None] - cq[None, :] + 15, 0, 30)
    vals = rpb_l[:, dr, dc]
    out = np.where(valid[:, None, :], vals.transpose(1, 0, 2), np.float32(NEG)).astype(np.float32)
    return np.ascontiguousarray(out[:, list(HS), :]).reshape(128, 512)


def prep_weights(inp):
    f = lambda a: np.asarray(a, np.float32)
    w_in = f(inp["w_in"])
    Ls = w_in.shape[0]
    segs = np.cumsum([0, 256, 256, 256, 384, 256, 32, 384, 128, 128])
    o_qa, o_ka, o_va, o_cq, o_ckv, o_kpe, o_qc, o_kc, o_vc = segs[:9]
    p32 = rope_perm(32)
    p_ax = np.concatenate([rope_perm(32), 32 + rope_perm(32)])
    win = np.zeros((Ls, D, NIN), np.float32)
    win[:, :, C_QA:C_QA + 256] = w_in[:, :, o_qa:o_qa + 256]
    win[:, :, C_KA:C_KA + 256] = w_in[:, :, o_ka:o_ka + 256]
    win[:, :, C_CQ:C_CQ + 384] = w_in[:, :, o_cq:o_cq + 384]
    win[:, :, C_CKV:C_CKV + 256] = w_in[:, :, o_ckv:o_ckv + 256]
    win[:, :, C_KPA + 64:C_KPA + 96] = w_in[:, :, o_kpe:o_kpe + 32]
    win[:, :, C_KPB + 64:C_KPB + 96] = w_in[:, :, o_kpe + p32]
    qc_sw = np.concatenate([o_qc + 64 * h + p_ax for h in range(6)])
    kc_sw = np.concatenate([o_kc + 64 * h + p_ax for h in range(2)])
    win[:, :, C_QCA:C_QCA + 384] = w_in[:, :, o_qc:o_qc + 384]
    win[:, :, C_QCB:C_QCB + 384] = w_in[:, :, qc_sw]
    win[:, :, C_KCA:C_KCA + 128] = w_in[:, :, o_kc:o_kc + 128]
    win[:, :, C_KCB:C_KCB + 128] = w_in[:, :, kc_sw]
    win[:, :, C_V:C_V + 256] = w_in[:, :, o_va:o_va + 256]
    win[:, :, C_V + 256:C_V + 384] = w_in[:, :, o_vc:o_vc + 128]
    w_uq = f(inp["w_uq"])
    wuq = np.zeros((Ls, 384, 1152), np.float32)
    for h in range(6):
        wuq[:, :, 192 * h:192 * h + 96] = w_uq[:, :, 96 * h:96 * h + 96]
        wuq[:, :, 192 * h + 96:192 * h + 160] = w_uq[:, :, 96 * h:96 * h + 64]
        wuq[:, :, 192 * h + 160:192 * h + 192] = w_uq[:, :, 96 * h + 64 + p32]
    w_ukv = f(inp["w_ukv"])
    wukv = np.zeros((Ls, 256, 768), np.float32)
    for h in range(6):
        wukv[:, :, 64 * h:64 * h + 64] = w_ukv[:, :, 128 * h:128 * h + 64]
        wukv[:, :, 384 + 64 * h:384 + 64 * h + 64] = w_ukv[:, :, 128 * h + 64:128 * h + 128]
    gsm = np.zeros((Ls, 128, 17), np.float32)
    qan = f(inp["q_a_norm"])
    kvn = f(inp["kv_a_norm"])
    qn = f(inp["q_norm_c"])
    kn = f(inp["k_norm_c"])
    gout = np.concatenate([f(inp["g_out_a"]), f(inp["g_out_b"]), f(inp["g_out_c"])], 1)
    for l in range(Ls):
        gsm[l, :, 0:3] = qan[l].reshape(3, 128).T
        gsm[l, :, 3:5] = kvn[l].reshape(2, 128).T
        gsm[l, :, 5] = np.tile(qn[l], 2)
        gsm[l, :, 6] = np.tile(qn[l][p_ax], 2)
        gsm[l, :, 7] = np.tile(kn[l], 2)
        gsm[l, :, 8] = np.tile(kn[l][p_ax], 2)
        gsm[l, :, 9:17] = gout[l].reshape(8, 128).T
    gbc = np.stack([f(inp["attn_norm"]), f(inp["ffn_norm"])], 1)
    bd = np.zeros((128, 128), np.float32)
    bd[:64, :64] = 1
    bd[64:, 64:] = 1
    return dict(win=win, wuq=wuq, wukv=wukv, wout=f(inp["w_out"]), wg=f(inp["w_gate"]), wu=f(inp["w_up"]),
                wd=f(inp["w_down"]), gsm=gsm, gbc=np.ascontiguousarray(gbc), gfin=f(inp["final_norm"]),
                ident=np.eye(128, dtype=np.float32).astype(ml_dtypes.bfloat16), bdones=bd.astype(ml_dtypes.bfloat16))


def prep_core_tables(rpb, slots, true_pos):
    pos = np.concatenate(true_pos)
    row = pos // GRID_W
    col = pos % GRID_W
    cr, sr = rope_tables(row, 32)
    cc, sc = rope_tables(col, 32)
    cosC = np.concatenate([cr, cc], 0)
    sinC = np.concatenate([sr, sc], 0)
    ropeC = np.stack([np.tile(cosC, (2, 1)), np.tile(sinC, (2, 1))], 0).astype(np.float32)
    cb, sb = rope_tables(pos, 32)
    ropeB = np.stack([cb, sb], 0).astype(np.float32)
    Ls = rpb.shape[0]
    nab_int = np.zeros((Ls, 128, 5, 512), np.float32)
    sp_tiles = [[] for _ in range(Ls)]
    for l in range(Ls):
        for di, d in enumerate((-2, -1, 0, 1, 2)):
            nab_int[l, :, di, :] = na_bias_tile(rpb[l], 64, 8, 8 + d)
        for si, (S, nact) in enumerate(slots):
            nb = S // 128
            rows = S // GRID_W
            tb = true_pos[si][::128] // 128
            for i in na_specials(S, nact):
                for d in NA_OFFS:
                    tq = int(tb[i])
                    tk = tq + d
                    j = (i + d) % nb
                    if 0 <= tk < nb:
                        assert int(tb[j]) == tk
                    sp_tiles[l].append(na_bias_tile(rpb[l], rows, tq, tk))
    nab_sp = np.stack([np.stack(t, 0) for t in sp_tiles], 0)
    return ropeC, ropeB, nab_int, nab_sp


_CACHE = {}


def run_slots(slots, per_core_x, per_core_pos, inp, debug=False, stop_after=None, trace=False):
    key = (tuple(slots), debug, stop_after)
    if key not in _CACHE:
        _CACHE[key] = build_program(slots, debug=debug, stop_after=stop_after)
    nc, _ = _CACHE[key]
    W = prep_weights(inp)
    rpb = np.asarray(inp["rpb"], np.float32)
    in_maps = []
    for x, pos in zip(per_core_x, per_core_pos):
        ropeC, ropeB, nab_int, nab_sp = prep_core_tables(rpb, slots, pos)
        m = dict(W)
        m.update(x_in=np.ascontiguousarray(x, dtype=np.float32), ropeC=ropeC, ropeB=ropeB, nab_int=nab_int, nab_sp=nab_sp)
        in_maps.append(m)
    res = run_bass_kernel_spmd(nc, in_maps, core_ids=list(range(len(in_maps))), **({"trace": True} if trace else {}))
    return res


def kernel(**inp):
    xp = np.asarray(inp["x_prompt"], np.float32)
    xs = np.asarray(inp["x_sample"], np.float32)
    B, S, _ = xp.shape
    Bs, Ss, _ = xs.shape
    assert Bs == 2 * 8 and B * 2 == 8
    H = S // 2
    slots = [(Ss, Ss), (Ss, Ss), (S, H)]
    per_x, per_pos = [], []
    for c in range(8):
        p, h = c // 2, c % 2
        xl = np.concatenate([xs[2 * c], xs[2 * c + 1], np.roll(xp[p], -h * H, axis=0)], 0)
        per_x.append(xl)
        per_pos.append([np.arange(Ss), np.arange(Ss), (np.arange(S) + h * H) % S])
    res = run_slots(slots, per_x, per_pos, inp)
    yp = np.zeros_like(xp)
    ys = np.zeros_like(xs)
    for c in range(8):
        y = res.results[c]["y_out"]
        p, h = c // 2, c % 2
        ys[2 * c] = y[0:Ss]
        ys[2 * c + 1] = y[Ss:2 * Ss]
        yp[p, h * H:(h + 1) * H] = y[2 * Ss:2 * Ss + H]
    return yp, ys
```

```python
import contextlib
import math

import ml_dtypes
import numpy as np

import concourse.bass as bass
import concourse.mybir as mybir
from concourse.bass_utils import run_bass_kernel_spmd

F32 = mybir.dt.float32
BF16 = mybir.dt.bfloat16
ALU = mybir.AluOpType
AF = mybir.ActivationFunctionType
AX = mybir.AxisListType

D = 1024
L = 2
GRID_W = 64
DFF = 2816
NFF = DFF // 128
EPS = 1e-6
NEG = -30000.0
NIN = 2752
C_QA, C_KA, C_CQ, C_CKV, C_KPA, C_KPB, C_QCA, C_QCB, C_KCA, C_KCB, C_V = (
    0, 256, 512, 896, 1152, 1248, 1344, 1728, 2112, 2240, 2368)
NA_OFFS = (-3, -2, -1, 0, 1, 2, 3)
HS = (0, 2, 1, 3)

SAME_ENGINE_SYNC = {"pe": False, "act": True, "dve": True, "pool": True, "sp": False}


class Op:
    __slots__ = ("stream", "fn", "deps", "flag", "count", "semkey", "nparts", "waits")

    def __init__(self, stream, fn, semkey=None, nparts=0):
        self.stream = stream
        self.fn = fn
        self.deps = ()
        self.flag = False
        self.count = None
        self.semkey = semkey
        self.nparts = nparts
        self.waits = None


class Prog:
    def __init__(self):
        self.ops = []
        self.slots = {}
        self.streams = {s: [] for s in ("pe", "act", "dve", "pool", "sp")}
        self.last = {}

    def _slot(self, name):
        s = self.slots.get(name)
        if s is None:
            s = [None, {}]
            self.slots[name] = s
        return s

    def op(self, stream, fn, reads=(), writes=(), semkey=None, nparts=0, extra_deps=()):
        o = Op(stream, fn, semkey, nparts)
        deps = set(extra_deps)
        okey = semkey if semkey is not None else stream
        for r in reads:
            s = self._slot(r)
            if s[0] is not None:
                deps.add(s[0])
            if r.startswith("ps"):
                for k, rd in s[1].items():
                    if k != okey:
                        deps.add(rd)
        for w in writes:
            s = self._slot(w)
            if s[0] is not None:
                deps.add(s[0])
            for rd in s[1].values():
                deps.add(rd)
        for r in reads:
            self._slot(r)[1][okey] = o
        for w in writes:
            s = self._slot(w)
            s[0] = o
            s[1] = {}
        deps.discard(o)
        o.deps = tuple(deps)
        self.ops.append(o)
        self.streams[stream].append(o)
        if fn is not None:
            self.last[okey] = o
        return o

    def dma(self, stream, fn, semkey, nparts, reads=(), writes=()):
        return self.op(stream, fn, reads, writes, semkey=semkey, nparts=nparts)

    def barrier(self):
        lasts = list(self.last.values())
        for s in self.streams:
            self.op(s, None, extra_deps=lasts)
        self.slots = {k: v for k, v in self.slots.items() if k.startswith("D:")}

    def resolve(self):
        known = {s: {} for s in self.streams}
        seq = {}
        cnt = {}
        for o in self.ops:
            key = o.semkey if o.semkey is not None else o.stream
            cnt[key] = cnt.get(key, 0) + 1
            seq[o] = (key, cnt[key])
        for o in self.ops:
            need = {}
            kn = known[o.stream]
            for d in o.deps:
                key, n = seq[d]
                if d.semkey is None and d.stream == o.stream and not SAME_ENGINE_SYNC[o.stream]:
                    continue
                if kn.get(key, 0) >= n:
                    continue
                if key not in need or seq[need[key]][1] < n:
                    need[key] = d
            o.waits = list(need.values())
            for key, d in need.items():
                kn[key] = seq[d][1]
                d.flag = True
            o.deps = ()
        ccount = {}
        for o in self.ops:
            if o.semkey is not None:
                ccount[o.semkey] = ccount.get(o.semkey, 0) + 16 * o.nparts
                o.count = ccount[o.semkey]
            elif o.flag:
                ccount[o.stream] = ccount.get(o.stream, 0) + 1
                o.count = ccount[o.stream]
        return ccount

    def emit(self, nc, stack):
        ccount = self.resolve()
        sems = {}
        for i, key in enumerate(ccount):
            sems[key] = stack.enter_context(nc.semaphore("s%d" % i))
        engmap = {"pe": "tensor", "act": "scalar", "dve": "vector", "pool": "gpsimd", "sp": "sync"}
        block = stack.enter_context(nc.Block())
        for sname, ops in self.streams.items():
            if not ops:
                continue

            def body(e, ops=ops, sname=sname):
                for o in ops:
                    for d in o.waits:
                        key = d.semkey if d.semkey is not None else d.stream
                        e.wait_ge(sems[key], d.count)
                    if o.fn is None:
                        continue
                    if o.semkey is not None:
                        o.fn(e, sems[o.semkey])
                    else:
                        ins = o.fn(e)
                        if o.flag:
                            ins.then_inc(sems[sname], 1)

            getattr(block, engmap[sname])(body)
        return ccount

    def final_wait(self, stream="sp"):
        self.op(stream, None, extra_deps=list(self.last.values()))


class Arena:
    def __init__(self, base_ap, nwords):
        self.base = base_ap
        self.nwords = nwords
        self.off = 0

    def reset(self):
        self.off = 0

    def alloc(self, shape, dt):
        n = 1
        for s in shape[1:]:
            n *= s
        nb = n * (2 if dt == BF16 else 4)
        n4 = (nb + 3) // 4
        n4 = (n4 + 7) // 8 * 8
        assert self.off + n4 <= self.nwords, ("SBUF arena overflow", self.off + n4, self.nwords)
        a = self.base[:, self.off:self.off + n4]
        self.off += n4
        if dt == BF16:
            a = a.bitcast(BF16)
        a = a[:, 0:n]
        if len(shape) == 3:
            a = a.rearrange("p (a b) -> p a b", b=shape[2])
        elif len(shape) == 4:
            a = a.rearrange("p (a b c) -> p a b c", b=shape[2], c=shape[3])
        return a


def na_specials(S, nact):
    nb = S // 128
    half = nb // 2
    if nact == S:
        sp = {0, 1, nb - 2, nb - 1}
    else:
        sp = {0, 1, nb - 2, nb - 1, half - 2, half - 1, half, half + 1}
    return sorted(x for x in sp if 0 <= x < nb)


def build_program(slots, debug=False, stop_after=None):
    nc = bass.Bass("TRN2", target_bir_lowering=False)
    T = sum(S for S, _ in slots)
    TA = sum(a for _, a in slots)
    NCH = T // 512
    slot_tok0 = np.cumsum([0] + [S for S, _ in slots]).tolist()
    slot_out0 = np.cumsum([0] + [a for _, a in slots]).tolist()
    nsp_tot = sum(len(na_specials(S, a)) for S, a in slots)

    def din(name, shape, dt=F32):
        return nc.dram_tensor(name, list(shape), dt, kind="ExternalInput").ap()

    skind = "ExternalOutput" if debug else "Internal"

    def dscr(name, shape, dt):
        return nc.dram_tensor(name, list(shape), dt, kind=skind).ap()

    x_in = din("x_in", [T, D])
    y_out = nc.dram_tensor("y_out", [TA, D], F32, kind="ExternalOutput").ap()
    win = din("win", [L, D, NIN])
    wuq = din("wuq", [L, 384, 1152])
    wukv = din("wukv", [L, 256, 768])
    wout = din("wout", [L, D, D])
    wg = din("wg", [L, D, DFF])
    wu = din("wu", [L, D, DFF])
    wd = din("wd", [L, DFF, D])
    gsm = din("gsm", [L, 128, 17])
    gbc = din("gbc", [L, 2, D])
    gfin = din("gfin", [D])
    ropeC = din("ropeC", [2, 128, T])
    ropeB = din("ropeB", [2, 32, T])
    nab_int = din("nab_int", [L, 128, 5, 512])
    nab_sp = din("nab_sp", [L, nsp_tot * 7, 128, 512])
    ident_d = din("ident", [128, 128], BF16)
    bdones_d = din("bdones", [128, 128], BF16)

    x1 = dscr("x1", [T, D], F32)
    xmid = dscr("xmid", [T, D], F32)
    qaT = dscr("qaT", [256, T], BF16)
    kaT = dscr("kaT", [256, T], BF16)
    va = dscr("va", [T, 4, 65], BF16)
    qbT = dscr("qbT", [6, 96, T], BF16)
    kbT = dscr("kbT", [6, 96, T], BF16)
    vb = dscr("vb", [T, 6, 65], BF16)
    qcT = dscr("qcT", [384, T], BF16)
    kcT = dscr("kcT", [128, T], BF16)
    vc = dscr("vc", [T, 2, 65], BF16)
    oT = dscr("oT", [D, T], F32)

    P = Prog()
    st = contextlib.ExitStack()
    AW = 51200
    arena_t = st.enter_context(nc.sbuf_tensor("arena", [128, AW], F32))
    A = Arena(arena_t[:], AW)
    psall_t = st.enter_context(nc.psum_tensor("psall", [128, 4096], F32))
    psall = psall_t[:]
    PS = [psall[:, 512 * i:512 * (i + 1)] for i in range(8)]
    PSB = [psall[:, 512 * i:512 * (i + 1)].bitcast(BF16) for i in range(8)]

    dq = ["sp"]

    def dma(fn, key, n, reads=(), writes=(), q=None):
        return P.dma(q or "sp", fn, "q_" + key, n, reads, writes)

    def simple_dma(out, in_, key, reads=(), writes=(), q=None):
        return dma(lambda e, s: e.dma_start(out=out, in_=in_).then_inc(s, 16), key, 1, reads, writes, q)

    def mm(out, lhsT, rhs, start, stop, reads, writes):
        return P.op("pe", lambda e: e.matmul(out, lhsT=lhsT, rhs=rhs, start=start, stop=stop), reads, writes)

    def act(out, in_, func, reads, writes, **kw):
        return P.op("act", lambda e: e.activation(out=out, in_=in_, func=func, **kw), reads, writes)

    def vcopy(eng, out, in_, reads, writes):
        if eng == "act":
            return act(out, in_, AF.Copy, reads, writes)
        return P.op(eng, lambda e: e.tensor_copy(out=out, in_=in_), reads, writes)

    def tt(eng, out, in0, in1, op, reads, writes):
        return P.op(eng, lambda e: e.tensor_tensor(out=out, in0=in0, in1=in1, op=op), reads, writes)

    def stt(eng, out, in0, scalar, in1, op0, op1, reads, writes):
        return P.op(eng, lambda e: e.scalar_tensor_tensor(out=out, in0=in0, scalar=scalar, in1=in1, op0=op0, op1=op1),
                    reads, writes)

    consts = A.alloc([128, 8], F32)
    ident = A.alloc([128, 128], BF16)
    bdones = A.alloc([128, 128], BF16)
    ones_bf = A.alloc([128, 128], BF16)
    sel_b = A.alloc([128, 64], BF16)
    gfin_b = None
    GLOBAL_OFF = None

    P.op("dve", lambda e: e.memset(consts[:, 0:1], EPS), writes=["consts"])
    P.op("dve", lambda e: e.memset(consts[:, 1:2], 1.0), writes=["consts"])
    P.op("dve", lambda e: e.memset(ones_bf, 1.0), writes=["ones_bf"])
    P.op("dve", lambda e: e.memset(sel_b, 0.0), writes=["sel_b"])
    P.op("dve", lambda e: e.memset(sel_b[64:65, :], 1.0), writes=["sel_b"])
    simple_dma(ident, ident_d, "c0", writes=["ident"])
    simple_dma(bdones, bdones_d, "c1", writes=["bdones"])
    GLOBAL_OFF = A.off

    def eps_col(ap):
        b = ap.base_partition()
        return consts[b:b + ap.shape[0], 0:1]

    def rsqrt(ap, n, src, reads, slot):
        act(ap, src, AF.Sqrt, list(reads) + ["consts"], [slot], scale=1.0 / n, bias=eps_col(ap))
        P.op("dve", lambda e: e.reciprocal(out=ap, in_=ap), [slot], [slot])

    def chunk_slot(c):
        t = 512 * c
        for i in range(len(slots)):
            if slot_tok0[i] <= t < slot_tok0[i + 1]:
                return i
        raise AssertionError

    def chunk_active(c, l):
        if l == 0:
            return True
        i = chunk_slot(c)
        return 512 * c - slot_tok0[i] < slots[i][1]

    def out_row0(c):
        i = chunk_slot(c)
        return slot_out0[i] + 512 * c - slot_tok0[i]

    def phase_A(l):
        P.barrier()
        A.off = GLOBAL_OFF
        x_src = x_in if l == 0 else x1
        win_sb = A.alloc([128, 8, NIN], BF16)
        wuq_sb = A.alloc([128, 3, 1152], BF16)
        wukv_sb = A.alloc([128, 2, 768], BF16)
        gsm_sb = A.alloc([128, 17], F32)
        gA = A.alloc([128, D], F32)
        xa = [A.alloc([128, 4, D], F32) for _ in range(2)]
        junk = A.alloc([128, D], F32)
        ss = [A.alloc([128, 4], F32) for _ in range(2)]
        hb = [A.alloc([128, D], BF16) for _ in range(2)]
        hT = [A.alloc([128, 8, 512], BF16) for _ in range(2)]
        rC = [A.alloc([128, 2, 512], F32) for _ in range(2)]
        rB = [A.alloc([128, 2, 512], F32) for _ in range(2)]
        NSTG = 8
        stg = [A.alloc([128, 512], BF16) for _ in range(NSTG)]
        cqf = A.alloc([128, 5, 512], F32)
        sqb = [A.alloc([128, 512], BF16) for _ in range(2)]
        cn = A.alloc([128, 5, 512], BF16)
        rs = [A.alloc([128, 512], F32) for _ in range(2)]
        tA = [A.alloc([128, 512], F32) for _ in range(2)]
        tB = [A.alloc([128, 512], F32) for _ in range(2)]
        vst = A.alloc([128, 4, 6, 65], BF16)
        vsb = A.alloc([128, 4, 6, 65], BF16)
        kpe_s = A.alloc([128, 512], BF16)

        for k in range(8):
            dma(lambda e, s, k=k: e.dma_start(out=win_sb[:, k, :], in_=win[l, 128 * k:128 * (k + 1), :]).then_inc(s, 16),
                "w_in%d" % k, 1, writes=["win_sb"], q="pool")
        dma(lambda e, s: e.dma_start(out=wuq_sb, in_=wuq[l].rearrange("(k p) n -> p k n", p=128)).then_inc(s, 16),
            "w_uq", 1, writes=["wuq_sb"], q="pool")
        dma(lambda e, s: e.dma_start(out=wukv_sb, in_=wukv[l].rearrange("(k p) n -> p k n", p=128)).then_inc(s, 16),
            "w_ukv", 1, writes=["wukv_sb"], q="pool")
        simple_dma(gsm_sb, gsm[l], "g0", writes=["gsm_sb"])
        simple_dma(gA, gbc[l, 0].partition_broadcast(128), "g1", writes=["gA"])
        P.op("pool", lambda e: e.memset(vst, 1.0), writes=["vst"])
        P.op("pool", lambda e: e.memset(vsb, 1.0), writes=["vsb"])

        bank_rr = [0]

        def nbank():
            b = 2 + bank_rr[0] % 4
            bank_rr[0] += 1
            return b

        stg_rr = [0]

        def nstg():
            i = stg_rr[0] % NSTG
            stg_rr[0] += 1
            return i

        ev_rr = [0]

        def ev_eng():
            ev_rr[0] += 1
            return "act" if ev_rr[0] % 2 else "dve"

        def front_load(c):
            i = c % 2
            t0 = 512 * c
            dma(lambda e, s: e.dma_start(out=xa[i], in_=x_src[t0:t0 + 512, :].rearrange("(b p) d -> p b d", p=128)).then_inc(s, 16),
                "xa%d" % i, 1, reads=["D:x%d:%d" % (l, c)], writes=["xa%d" % i])
            dma(lambda e, s: e.dma_start(out=rC[i], in_=ropeC[:, :, t0:t0 + 512].rearrange("a p t -> p a t")).then_inc(s, 16),
                "rC%d" % i, 1, writes=["rC%d" % i])
            dma(lambda e, s: e.dma_start(out=rB[i][64:96], in_=ropeB[:, :, t0:t0 + 512].rearrange("a p t -> p a t")).then_inc(s, 16),
                "rB%d" % i, 1, writes=["rB%d" % i])
            for b in range(4):
                act(junk, xa[i][:, b, :], AF.Square, ["xa%d" % i], ["junk", "ss%d" % i], accum_out=ss[i][:, b:b + 1])
            rsqrt(ss[i], D, ss[i], ["ss%d" % i], "ss%d" % i)

        def front_block(c, b):
            i = c % 2
            j = b % 2
            stt("dve", hb[j], xa[i][:, b, :], ss[i][:, b:b + 1], gA, ALU.mult, ALU.mult,
                ["xa%d" % i, "ss%d" % i, "gA"], ["hb%d" % j])
            for k in range(8):
                P.op("pe", lambda e, k=k, j=j: e.transpose(out=PSB[j][:, 128 * k:128 * (k + 1)], in_=hb[j][:, 128 * k:128 * (k + 1)], identity=ident),
                     ["hb%d" % j, "ident"], ["ps%d" % j])
            vcopy("act" if b % 2 else "dve", hT[i][:, :, 128 * b:128 * (b + 1)],
                  PSB[j].rearrange("p (k t) -> p k t", t=128), ["ps%d" % j], ["hT%d" % i])

        pending = []

        def hook():
            if pending:
                pending.pop(0)()

        def fm(i, col0, M, wsb=None, nk=8, rhs_fn=None, rslot=None):
            wsb = win_sb if wsb is None else wsb
            b = nbank()
            for k in range(nk):
                rhs = hT[i][:, k, :] if rhs_fn is None else rhs_fn(k)
                mm(PS[b][0:M, :], wsb[:, k, col0:col0 + M], rhs, k == 0, k == nk - 1,
                   (list(rslot) if rslot else ["hT%d" % i]) + ["win_sb", "wuq_sb", "wukv_sb"], ["ps%d" % b])
            return b

        def store_fm(ap_sb, dst, key_i, c, name, npart=128):
            simple_dma(dst, ap_sb, "stg%d" % key_i, reads=["stg%d" % key_i], writes=["D:%s:%d" % (name, c)])

        def back(c):
            i = c % 2
            t0 = 512 * c
            ts = slice(t0, t0 + 512)
            hs = "hT%d" % i
            for name, col0, dst in (("qaT", C_QA, qaT), ("kaT", C_KA, kaT)):
                for j in range(2):
                    b = fm(i, col0 + 128 * j, 128)
                    si = nstg()
                    vcopy(ev_eng(), stg[si], PS[b], ["ps%d" % b], ["stg%d" % si])
                    store_fm(stg[si], dst[128 * j:128 * (j + 1), ts], si, c, name)
            hook()
            if CUT == 3:
                return
            for (col0, nchk, off, gcol, nfeat, nb_) in ((C_CQ, 3, 0, 0, 384, 6), (C_CKV, 2, 3, 3, 256, 7)):
                for j in range(nchk):
                    b = fm(i, col0 + 128 * j, 128)
                    q = j % 2
                    act(sqb[q], PS[b], AF.Square, ["ps%d" % b], ["sqb%d" % q])
                    vcopy("dve", cqf[:, off + j, :], PS[b], ["ps%d" % b], ["cqf%d" % (off + j)])
                    mm(PS[nb_], ones_bf, sqb[q], j == 0, j == nchk - 1, ["sqb%d" % q, "ones_bf"], ["ps%d" % nb_])
                r = (off // 3) % 2
                rsqrt(rs[r], nfeat, PS[nb_], ["ps%d" % nb_], "rs%d" % r)
                for j in range(nchk):
                    stt("dve", cn[:, off + j, :], cqf[:, off + j, :], gsm_sb[:, gcol + j:gcol + j + 1], rs[r], ALU.mult, ALU.mult,
                        ["cqf%d" % (off + j), "gsm_sb", "rs%d" % r], ["cn%d" % (off + j)])
            hook()
            if CUT == 4:
                return
            bA = fm(i, C_KPA, 96)
            bB = fm(i, C_KPB, 96)
            tt("dve", tA[0][64:96], PS[bA][64:96], rB[i][64:96, 0, :], ALU.mult, ["ps%d" % bA, "rB%d" % i], ["tA0"])
            tt("dve", tB[0][64:96], PS[bB][64:96], rB[i][64:96, 1, :], ALU.mult, ["ps%d" % bB, "rB%d" % i], ["tB0"])
            tt("dve", kpe_s[64:96], tA[0][64:96], tB[0][64:96], ALU.add, ["tA0", "tB0"], ["kpe_s"])
            dma(lambda e, s: [e.dma_start(out=kbT[h, 64:96, ts], in_=kpe_s[64:96]).then_inc(s, 16) for h in range(6)],
                "kpe", 6, reads=["kpe_s"], writes=["D:kbTp:%d" % c])
            if CUT == 5:
                return
            for (colA, colB, gc, name, dst, row0) in ([(C_QCA + 128 * j, C_QCB + 128 * j, 5, "qcT", qcT, 128 * j) for j in range(3)]
                                                      + [(C_KCA, C_KCB, 7, "kcT", kcT, 0)]):
                bA = fm(i, colA, 128)
                bB = fm(i, colB, 128)
                q = ev_rr[0] % 2
                ev_rr[0] += 1
                act(sqb[q], PS[bA], AF.Square, ["ps%d" % bA], ["sqb%d" % q])
                act(tA[q], PS[bA], AF.Copy, ["ps%d" % bA, "gsm_sb"], ["tA%d" % q], scale=gsm_sb[:, gc:gc + 1])
                act(tB[q], PS[bB], AF.Copy, ["ps%d" % bB, "gsm_sb"], ["tB%d" % q], scale=gsm_sb[:, gc + 1:gc + 2])
                nb_ = 6 + q
                mm(PS[nb_], bdones, sqb[q], True, True, ["sqb%d" % q, "bdones"], ["ps%d" % nb_])
                rsqrt(rs[q], 64, PS[nb_], ["ps%d" % nb_], "rs%d" % q)
                tt("pool", tB[q], tB[q], rC[i][:, 1, :], ALU.mult, ["tB%d" % q, "rC%d" % i], ["tB%d" % q])
                tt("dve", tA[q], tA[q], rC[i][:, 0, :], ALU.mult, ["tA%d" % q, "rC%d" % i], ["tA%d" % q])
                tt("dve", tA[q], tA[q], tB[q], ALU.add, ["tA%d" % q, "tB%d" % q], ["tA%d" % q])
                si = nstg()
                tt("dve", stg[si], tA[q], rs[q], ALU.mult, ["tA%d" % q, "rs%d" % q], ["stg%d" % si])
                store_fm(stg[si], dst[row0:row0 + 128, ts], si, c, name)
            hook()
            if CUT == 6:
                return
            for h in range(6):
                bA = fm(i, 192 * h, 96, wsb=wuq_sb, nk=3, rhs_fn=lambda k: cn[:, k, :], rslot=("cn0", "cn1", "cn2"))
                bB = fm(i, 192 * h + 96, 96, wsb=wuq_sb, nk=3, rhs_fn=lambda k: cn[:, k, :], rslot=("cn0", "cn1", "cn2"))
                si = nstg()
                q = h % 2
                vcopy("act", stg[si][0:64], PS[bA][0:64], ["ps%d" % bA], ["stg%d" % si])
                tt("dve", tA[q][64:96], PS[bA][64:96], rB[i][64:96, 0, :], ALU.mult, ["ps%d" % bA, "rB%d" % i], ["tA%d" % q])
                tt("dve", tB[q][64:96], PS[bB][64:96], rB[i][64:96, 1, :], ALU.mult, ["ps%d" % bB, "rB%d" % i], ["tB%d" % q])
                tt("dve", stg[si][64:96], tA[q][64:96], tB[q][64:96], ALU.add, ["tA%d" % q, "tB%d" % q], ["stg%d" % si])
                simple_dma(qbT[h, :, ts], stg[si][0:96], "stg%d" % si, reads=["stg%d" % si], writes=["D:qbT:%d" % c])
            hook()
            if CUT == 7:
                return
            for h in range(6):
                b = fm(i, 64 * h, 64, wsb=wukv_sb, nk=2, rhs_fn=lambda k: cn[:, 3 + k, :], rslot=("cn3", "cn4"))
                si = nstg()
                vcopy(ev_eng(), stg[si][0:64], PS[b][0:64], ["ps%d" % b], ["stg%d" % si])
                simple_dma(kbT[h, 0:64, ts], stg[si][0:64], "stg%d" % si, reads=["stg%d" % si], writes=["D:kbTn:%d" % c])
            if CUT == 8:
                return
            for b4 in range(4):
                b = nbank()
                for k in range(8):
                    mm(PS[b][:, 0:384], hT[i][:, k, 128 * b4:128 * (b4 + 1)], win_sb[:, k, C_V:C_V + 384], k == 0, k == 7,
                       [hs, "win_sb"], ["ps%d" % b])
                vcopy(ev_eng(), vst[:, b4, :, 0:64], PS[b][:, 0:384].rearrange("p (h d) -> p h d", d=64), ["ps%d" % b], ["vst"])
                b = nbank()
                for k in range(2):
                    mm(PS[b][:, 0:384], cn[:, 3 + k, 128 * b4:128 * (b4 + 1)], wukv_sb[:, k, 384:768], k == 0, k == 1,
                       ["cn3", "cn4", "wukv_sb"], ["ps%d" % b])
                vcopy(ev_eng(), vsb[:, b4, :, 0:64], PS[b][:, 0:384].rearrange("p (h d) -> p h d", d=64), ["ps%d" % b], ["vsb"])
            simple_dma(va[ts].rearrange("(b p) h d -> p b h d", p=128), vst[:, :, 0:4, :], "vst_a", reads=["vst"], writes=["D:va:%d" % c])
            simple_dma(vc[ts].rearrange("(b p) h d -> p b h d", p=128), vst[:, :, 4:6, :], "vst_c", reads=["vst"], writes=["D:vc:%d" % c])
            simple_dma(vb[ts].rearrange("(b p) h d -> p b h d", p=128), vsb, "vsb", reads=["vsb"], writes=["D:vb:%d" % c])

        import os
        CUT = int(os.environ.get("K_CUT", "0"))
        if CUT == 1:
            return
        front_load(0)
        for b_ in range(4):
            front_block(0, b_)
        if CUT == 2:
            return
        for c in range(NCH):
            if c + 1 < NCH:
                front_load(c + 1)
                for b_ in range(4):
                    pending.append(lambda c=c, b_=b_: front_block(c + 1, b_))
            back(c)
            while pending:
                hook()
            if CUT:
                return

    def attn_epilogue(acc_bank, bc_bank, ebuf, row0, t0, ntok, c_list, tag):
        osb, rc, hi, lo = ebuf

        def part1():
            vcopy("dve", osb[0:65, 0:ntok], PS[acc_bank][0:65, 0:ntok], ["ps%d" % acc_bank], [tag + "osb"])
            P.op("dve", lambda e: e.reciprocal(out=rc[64:65, 0:ntok], in_=osb[64:65, 0:ntok]), [tag + "osb"], [tag + "rc"])
            vcopy("dve", hi[64:65, 0:ntok], rc[64:65, 0:ntok], [tag + "rc"], [tag + "hi"])
            tt("dve", lo[64:65, 0:ntok], rc[64:65, 0:ntok], hi[64:65, 0:ntok], ALU.subtract, [tag + "rc", tag + "hi"], [tag + "lo"])

        def part2():
            mm(PS[bc_bank][0:64, 0:ntok], sel_b, hi[:, 0:ntok], True, False, [tag + "hi", "sel_b"], ["ps%d" % bc_bank])
            mm(PS[bc_bank][0:64, 0:ntok], sel_b, lo[:, 0:ntok], False, True, [tag + "lo", "sel_b"], ["ps%d" % bc_bank])
            tt("dve", osb[0:64, 0:ntok], osb[0:64, 0:ntok], PS[bc_bank][0:64, 0:ntok], ALU.mult,
               [tag + "osb", "ps%d" % bc_bank], [tag + "osb"])
            simple_dma(oT[row0:row0 + 64, t0:t0 + ntok], osb[0:64, 0:ntok], tag + "osb", reads=[tag + "osb"],
                       writes=["D:oT%d:%d" % (row0, c) for c in c_list])

        return part1, part2

    def emit_sorted(items):
        items.sort(key=lambda t: (t[0], t[1]))
        for _, _, fn in items:
            fn()

    def phase_B_dense(l):
        P.barrier()
        A.off = GLOBAL_OFF
        SM = max(S for S, _ in slots)
        NB = SM // 128
        kbuf = [A.alloc([128, SM], BF16) for _ in range(2)]
        vbuf = [A.alloc([128, NB, 65], BF16) for _ in range(2)]
        qbuf = [A.alloc([128, SM], BF16) for _ in range(2)]
        NPT = 4
        pt = [A.alloc([128, 1024], BF16) for _ in range(NPT)]
        ebufs = [(A.alloc([128, 512], F32), A.alloc([128, 512], F32), A.alloc([128, 512], BF16), A.alloc([128, 512], BF16))
                 for _ in range(2)]
        for i_, eb_ in enumerate(ebufs):
            P.op("pool", lambda e, eb_=eb_: e.memset(eb_[2], 0.0), writes=["e%dhi" % i_])
            P.op("pool", lambda e, eb_=eb_: e.memset(eb_[3], 0.0), writes=["e%dlo" % i_])
        work = []
        for si, (S, nact) in enumerate(slots):
            na = S if l == 0 else nact
            for h in range(6):
                work.append((si, "b", h, [h], na))
            for kv in range(2):
                work.append((si, "c", kv, [3 * kv, 3 * kv + 1, 3 * kv + 2], na))
        qjobs = []
        for wi, (si, kind, kv, qhs, na) in enumerate(work):
            for qh in qhs:
                qjobs.append((wi, qh))
        for i_ in range(2):
            P.op("pool", lambda e, i_=i_: e.memset(kbuf[i_], 0.0), writes=["kbuf%d" % i_])
            P.op("pool", lambda e, i_=i_: e.memset(qbuf[i_], 0.0), writes=["qbuf%d" % i_])

        def load_kv(wi):
            si, kind, kv, qhs, na = work[wi]
            S = slots[si][0]
            t0 = slot_tok0[si]
            i = wi % 2
            cl = range(t0 // 512, (t0 + S) // 512)
            if kind == "b":
                simple_dma(kbuf[i][0:96, 0:S], kbT[kv, :, t0:t0 + S], "kb%d" % i,
                           reads=["D:kbTn:%d" % c for c in cl] + ["D:kbTp:%d" % c for c in cl], writes=["kbuf%d" % i])
                simple_dma(vbuf[i][:, 0:S // 128, :], vb[t0:t0 + S, kv, :].rearrange("(b p) d -> p b d", p=128), "vb%d" % i,
                           reads=["D:vb:%d" % c for c in cl], writes=["vbuf%d" % i])
            else:
                P.op("pool", lambda e: e.memset(kbuf[i][64:128, 0:S], 0.0), writes=["kbuf%d" % i])
                simple_dma(kbuf[i][0:64, 0:S], kcT[64 * kv:64 * kv + 64, t0:t0 + S], "kb%d" % i,
                           reads=["D:kcT:%d" % c for c in cl], writes=["kbuf%d" % i])
                simple_dma(vbuf[i][:, 0:S // 128, :], vc[t0:t0 + S, kv, :].rearrange("(b p) d -> p b d", p=128), "vb%d" % i,
                           reads=["D:vc:%d" % c for c in cl], writes=["vbuf%d" % i])

        def load_q(qi):
            wi, qh = qjobs[qi]
            si, kind, kv, qhs, na = work[wi]
            t0 = slot_tok0[si]
            i = qi % 2
            cl = range(t0 // 512, (t0 + na) // 512)
            if kind == "b":
                simple_dma(qbuf[i][0:96, 0:na], qbT[qh, :, t0:t0 + na], "qb%d" % i,
                           reads=["D:qbT:%d" % c for c in cl], writes=["qbuf%d" % i])
            else:
                simple_dma(qbuf[i][0:64, 0:na], qcT[64 * qh:64 * qh + 64, t0:t0 + na], "qb%d" % i,
                           reads=["D:qcT:%d" % c for c in cl], writes=["qbuf%d" % i])

        items = []
        seqn = [0]

        def add(pos, fn):
            items.append((pos, seqn[0], fn))
            seqn[0] += 1

        LOOK = 2
        n = 0
        nacc = 0
        add(-2.0, lambda: load_kv(0))
        add(-2.0, lambda: load_q(0))
        for qi, (wi, qh) in enumerate(qjobs):
            si, kind, kv, qhs, na = work[wi]
            S = slots[si][0]
            t0 = slot_tok0[si]
            if qi + 1 < len(qjobs):
                nwi = qjobs[qi + 1][0]
                if nwi != wi:
                    add(n - 1 + LOOK + 0.55, lambda nwi=nwi: load_kv(nwi))
                add(n - 0.5, lambda qi=qi: load_q(qi + 1))
            ki = wi % 2
            qb_ = qi % 2
            dk = 96 if kind == "b" else 128
            scale = 1.0 / math.sqrt(96 if kind == "b" else 64)
            row0 = (256 + 64 * qh) if kind == "b" else (640 + 64 * qh)
            nkb = S // 128
            npair = nkb // 2
            for qc in range(na // 512):
                ab, bb = 6, 7
                eb = ebufs[nacc % 2]
                etag = "e%d" % (nacc % 2)
                nacc += 1
                for kp in range(npair):
                    sp_ = n % 3
                    pi = n % NPT

                    def front(kp=kp, sp_=sp_, pi=pi, ki=ki, qb_=qb_, qc=qc, dk=dk, scale=scale):
                        for u in range(2):
                            kb = 2 * kp + u
                            mm(PS[2 * sp_ + u], kbuf[ki][0:dk, 128 * kb:128 * (kb + 1)], qbuf[qb_][0:dk, 512 * qc:512 * (qc + 1)], True, True,
                               ["kbuf%d" % ki, "qbuf%d" % qb_], ["ps%d" % (2 * sp_ + u)])
                        act(pt[pi], psall[:, 1024 * sp_:1024 * (sp_ + 1)], AF.Exp, ["ps%d" % (2 * sp_), "ps%d" % (2 * sp_ + 1)],
                            ["pt%d" % pi], scale=scale)

                    def back(kp=kp, pi=pi, ki=ki, nkb=nkb, ab=ab):
                        for u in range(2):
                            kb = 2 * kp + u
                            mm(PS[ab][0:65, :], vbuf[ki][:, kb, :], pt[pi][:, 512 * u:512 * (u + 1)], kb == 0, kb == nkb - 1,
                               ["vbuf%d" % ki, "pt%d" % pi], ["ps%d" % ab])

                    add(n, front)
                    add(n + LOOK + 0.5, back)
                    n += 1
                tq = t0 + 512 * qc
                p1, p2 = attn_epilogue(ab, bb, eb, row0, tq, 512, [tq // 512], etag)
                add(n - 1 + LOOK + 0.6, p1)
                add(n - 1 + LOOK + 4.7, p2)
        emit_sorted(items)

    def phase_B_na(l):
        import os
        NACUT = int(os.environ.get("K_NACUT", "0"))
        P.barrier()
        A.off = GLOBAL_OFF
        SM = max(S for S, _ in slots)
        NB = SM // 128
        qz = A.alloc([128, 4, SM], BF16)
        ka_sb = A.alloc([128, 2, SM], BF16)
        P.op("pool", lambda e: e.memset(qz, 0.0), writes=["qz"])
        va_sb = A.alloc([128, NB, 4 * 65], BF16)
        bint = A.alloc([128, 5, 512], F32)
        bsp = [A.alloc([128, 512], F32) for _ in range(3)]
        sbias = [A.alloc([128, 512], F32) for _ in range(3)]
        pt = [A.alloc([128, 512], BF16) for _ in range(3)]
        ebufs = [(A.alloc([128, 512], F32), A.alloc([128, 512], F32), A.alloc([128, 512], BF16), A.alloc([128, 512], BF16))
                 for _ in range(2)]
        for i_, eb_ in enumerate(ebufs):
            P.op("pool", lambda e, eb_=eb_: e.memset(eb_[2], 0.0), writes=["e%dhi" % i_])
            P.op("pool", lambda e, eb_=eb_: e.memset(eb_[3], 0.0), writes=["e%dlo" % i_])
        simple_dma(bint, nab_int[l], "bint", writes=["bint"])
        sp_base = 0
        items = []
        seqn = [0]

        def add(pos, fn):
            items.append((pos, seqn[0], fn))
            seqn[0] += 1

        LOOK = 2
        n = 0
        spn = 0
        nacc = 0
        for si, (S, nact) in enumerate(slots):
            na = S if l == 0 else nact
            t0 = slot_tok0[si]
            nb = S // 128
            cl = list(range(t0 // 512, (t0 + S) // 512))
            specials = na_specials(S, nact)

            def loads(t0=t0, S=S, nb=nb, cl=cl):
                dma(lambda e, s_: [e.dma_start(out=qz[64 * (hd % 2):64 * (hd % 2) + 64, hd, 0:S],
                                               in_=qaT[64 * hd:64 * hd + 64, t0:t0 + S]).then_inc(s_, 16) for hd in range(4)],
                    "qz", 4, reads=["D:qaT:%d" % c for c in cl], writes=["qz"])
                simple_dma(ka_sb[:, :, 0:S], kaT[:, t0:t0 + S].rearrange("(j p) t -> p j t", p=128), "ka_sb",
                           reads=["D:kaT:%d" % c for c in cl], writes=["ka_sb"])
                simple_dma(va_sb[:, 0:nb, :], va[t0:t0 + S].rearrange("(b p) h d -> p b (h d)", p=128), "va_sb",
                           reads=["D:va:%d" % c for c in cl], writes=["va_sb"])

            add(n + LOOK + 5.0 if si else -1.0, loads)
            n += (LOOK + 6) if si else 0
            for g in range(na // 512):
                for qi4 in range(4):
                    i = 4 * g + qi4
                    if i in specials:
                        sidx = sp_base + specials.index(i)
                        offs = [(d, ("sp", sidx * 7 + di)) for di, d in enumerate(NA_OFFS)]
                    else:
                        offs = [(d, ("int", d + 2)) for d in (-2, -1, 0, 1, 2)]
                    for oi, (d, (kind, bidx)) in enumerate(offs):
                        j = (i + d) % nb
                        sb_ = n % 3
                        bi = None
                        if kind == "sp":
                            bi = spn % 3
                            spn += 1

                        def front(i=i, j=j, sb_=sb_, kind=kind, bidx=bidx, bi=bi):
                            for c4 in range(4):
                                hd = HS[c4]
                                mm(PS[sb_][:, 128 * c4:128 * (c4 + 1)], ka_sb[:, hd // 2, 128 * j:128 * (j + 1)],
                                   qz[:, hd, 128 * i:128 * (i + 1)], True, True, ["ka_sb", "qz"], ["ps%d" % sb_])
                            if kind == "sp":
                                simple_dma(bsp[bi], nab_sp[l, bidx], "bsp%d" % bi, writes=["bsp%d" % bi])
                                bias_ap, bias_slot = bsp[bi], "bsp%d" % bi
                            else:
                                bias_ap, bias_slot = bint[:, bidx, :], "bint"
                            stt("dve", sbias[sb_], PS[sb_], 0.125, bias_ap, ALU.mult, ALU.add, ["ps%d" % sb_, bias_slot], ["sbias%d" % sb_])
                            act(pt[sb_], sbias[sb_], AF.Exp, ["sbias%d" % sb_], ["pt%d" % sb_])

                        def back(j=j, sb_=sb_, qi4=qi4, first=(oi == 0), last=(oi == len(offs) - 1)):
                            for c4 in range(4):
                                hd = HS[c4]
                                mm(PS[4 + c4][0:65, 128 * qi4:128 * (qi4 + 1)], va_sb[:, j, 65 * hd:65 * (hd + 1)],
                                   pt[sb_][:, 128 * c4:128 * (c4 + 1)], first, last, ["va_sb", "pt%d" % sb_], ["ps%d" % (4 + c4)])

                        add(n, front)
                        add(n + LOOK + 0.5, back)
                        n += 1
                tq = t0 + 512 * g
                for c4 in range(4):
                    eb = ebufs[nacc % 2]
                    etag = "e%d" % (nacc % 2)
                    nacc += 1
                    p1, p2 = attn_epilogue(4 + c4, 3, eb, 64 * HS[c4], tq, 512, [tq // 512], etag)
                    add(n - 1 + LOOK + 0.6 + 0.01 * c4, p1)
                    add(n - 1 + LOOK + 0.6 + 0.01 * c4 + 0.005, p2)
            sp_base += len(specials)
        emit_sorted(items)

    def phase_C1(l):
        P.barrier()
        A.off = GLOBAL_OFF
        x_src = x_in if l == 0 else x1
        wo_sb = A.alloc([128, 8, D], BF16)
        gsm_sb = A.alloc([128, 17], F32)
        ot = [A.alloc([128, 8, 512], F32) for _ in range(2)]
        sqb = [A.alloc([128, 512], BF16) for _ in range(3)]
        rs = [A.alloc([128, 512], F32) for _ in range(3)]
        mT = [A.alloc([128, 8, 512], BF16) for _ in range(2)]
        xa = [A.alloc([128, D], F32) for _ in range(4)]
        dma(lambda e, s: e.dma_start(out=wo_sb, in_=wout[l].rearrange("(k p) n -> p k n", p=128)).then_inc(s, 16),
            "w_o", 1, writes=["wo_sb"], q="pool")
        simple_dma(gsm_sb, gsm[l], "g0", writes=["gsm_sb"])
        chunks = [c for c in range(NCH) if chunk_active(c, l)]
        groups = ((0, 2, 256), (2, 5, 384), (5, 8, 384))

        def load(ci):
            c = chunks[ci]
            i = ci % 2
            simple_dma(ot[i], oT[:, 512 * c:512 * (c + 1)].rearrange("(j p) t -> p j t", p=128), "ot%d" % i,
                       reads=["D:oT%d:%d" % (r, c) for r in range(0, D, 64)], writes=["ot%d" % i])

        def norm(ci):
            i = ci % 2
            for gi, (j0, j1, nf) in enumerate(groups):
                for j in range(j0, j1):
                    q = j % 3
                    act(sqb[q], ot[i][:, j, :], AF.Square, ["ot%d" % i], ["sqb%d" % q])
                    mm(PS[gi], ones_bf, sqb[q], j == j0, j == j1 - 1, ["sqb%d" % q, "ones_bf"], ["ps%d" % gi])
                rsqrt(rs[gi], nf, PS[gi], ["ps%d" % gi], "rs%d" % gi)
                for j in range(j0, j1):
                    stt("dve", mT[i][:, j, :], ot[i][:, j, :], gsm_sb[:, 9 + j:10 + j], rs[gi], ALU.mult, ALU.mult,
                        ["ot%d" % i, "gsm_sb", "rs%d" % gi], ["mT%d" % i])

        load(0)
        if len(chunks) > 1:
            load(1)
        norm(0)
        xn = [0]
        for ci, c in enumerate(chunks):
            i = ci % 2
            if ci + 1 < len(chunks):
                norm(ci + 1)
            if ci + 2 < len(chunks):
                load(ci + 2)
            for b4 in range(4):
                xi = xn[0] % 4
                xn[0] += 1
                r0 = 512 * c + 128 * b4
                simple_dma(xa[xi], x_src[r0:r0 + 128, :], "xa%d" % xi, reads=["D:x%d:%d" % (l, c)], writes=["xa%d" % xi])
                for hf in range(2):
                    b = 4 + (2 * b4 + hf) % 4
                    for j in range(8):
                        mm(PS[b], mT[i][:, j, 128 * b4:128 * (b4 + 1)], wo_sb[:, j, 512 * hf:512 * (hf + 1)], j == 0, j == 7,
                           ["mT%d" % i, "wo_sb"], ["ps%d" % b])
                    tt("dve", xa[xi][:, 512 * hf:512 * (hf + 1)], xa[xi][:, 512 * hf:512 * (hf + 1)], PS[b], ALU.add,
                       ["xa%d" % xi, "ps%d" % b], ["xa%d" % xi])
                simple_dma(xmid[r0:r0 + 128, :], xa[xi], "xa%d" % xi, reads=["xa%d" % xi], writes=["D:xmid:%d" % c])

    def phase_C2(l):
        P.barrier()
        A.off = GLOBAL_OFF
        wg_sb = A.alloc([128, 8, DFF], BF16)
        wu_sb = A.alloc([128, 8, DFF], BF16)
        wd_sb = A.alloc([128, NFF, D], BF16)
        gF = A.alloc([128, D], F32)
        gL = A.alloc([128, D], F32)
        xm = [A.alloc([128, D], F32) for _ in range(4)]
        junk = A.alloc([128, D], BF16)
        ssv = A.alloc([128, 8], F32)
        hb = [A.alloc([128, D], BF16) for _ in range(2)]
        hT = A.alloc([128, 8, 512], BF16)
        aT = A.alloc([128, NFF, 512], BF16)
        sg = [A.alloc([128, 512], F32) for _ in range(2)]
        for k in range(8):
            dma(lambda e, s, k=k: e.dma_start(out=wg_sb[:, k, :], in_=wg[l, 128 * k:128 * (k + 1), :]).then_inc(s, 16),
                "w_g%d" % k, 1, writes=["wg_sb"], q="pool")
            dma(lambda e, s, k=k: e.dma_start(out=wu_sb[:, k, :], in_=wu[l, 128 * k:128 * (k + 1), :]).then_inc(s, 16),
                "w_u%d" % k, 1, writes=["wu_sb"], q="pool")
        for f0 in range(0, NFF, 4):
            f1 = min(NFF, f0 + 4)
            dma(lambda e, s, f0=f0, f1=f1: e.dma_start(out=wd_sb[:, f0:f1, :], in_=wd[l, 128 * f0:128 * f1, :].rearrange("(k p) n -> p k n", p=128)).then_inc(s, 16),
                "w_d%d" % f0, 1, writes=["wd_sb"], q="pool")
        simple_dma(gF, gbc[l, 1].partition_broadcast(128), "g1", writes=["gF"])
        if l == L - 1:
            simple_dma(gL, gfin.partition_broadcast(128), "g2", writes=["gL"])
        chunks = [c for c in range(NCH) if chunk_active(c, l)]

        def load_block(ci, b4):
            c = chunks[ci]
            r0 = 512 * c + 128 * b4
            simple_dma(xm[b4], xmid[r0:r0 + 128, :], "xm%d" % b4, reads=["D:xmid:%d" % c], writes=["xm%d" % b4])

        def front_block(b4):
            j = b4 % 2
            act(junk, xm[b4], AF.Square, ["xm%d" % b4], ["junk", "ssv"], accum_out=ssv[:, b4:b4 + 1])
            rsqrt(ssv[:, b4:b4 + 1], D, ssv[:, b4:b4 + 1], ["ssv"], "ssv")
            stt("dve", hb[j], xm[b4], ssv[:, b4:b4 + 1], gF, ALU.mult, ALU.mult, ["xm%d" % b4, "ssv", "gF"], ["hb%d" % j])
            for k in range(8):
                P.op("pe", lambda e, k=k, j=j: e.transpose(out=PSB[j][:, 128 * k:128 * (k + 1)], in_=hb[j][:, 128 * k:128 * (k + 1)], identity=ident),
                     ["hb%d" % j, "ident"], ["ps%d" % j])
            vcopy("act" if b4 % 2 else "dve", hT[:, :, 128 * b4:128 * (b4 + 1)],
                  PSB[j].rearrange("p (k t) -> p k t", t=128), ["ps%d" % j], ["hT"])

        for b4 in range(4):
            load_block(0, b4)
        for b4 in range(4):
            front_block(b4)
        for ci, c in enumerate(chunks):
            nxt = ci + 1 < len(chunks)
            for f in range(NFF):
                bg = 2 + f % 2
                bu = 4 + f % 2
                for k in range(8):
                    mm(PS[bg], wg_sb[:, k, 128 * f:128 * (f + 1)], hT[:, k, :], k == 0, k == 7, ["wg_sb", "hT"], ["ps%d" % bg])
                for k in range(8):
                    mm(PS[bu], wu_sb[:, k, 128 * f:128 * (f + 1)], hT[:, k, :], k == 0, k == 7, ["wu_sb", "hT"], ["ps%d" % bu])
                q = f % 2
                act(sg[q], PS[bg], AF.Silu, ["ps%d" % bg], ["sg%d" % q])
                tt("dve", aT[:, f, :], sg[q], PS[bu], ALU.mult, ["sg%d" % q, "ps%d" % bu], ["aT"])
            for b4 in range(4):
                r0 = 512 * c + 128 * b4
                for hf in range(2):
                    b = 6 + hf
                    for f in range(NFF):
                        mm(PS[b], aT[:, f, 128 * b4:128 * (b4 + 1)], wd_sb[:, f, 512 * hf:512 * (hf + 1)], f == 0, f == NFF - 1,
                           ["aT", "wd_sb"], ["ps%d" % b])
                    tt("dve", xm[b4][:, 512 * hf:512 * (hf + 1)], xm[b4][:, 512 * hf:512 * (hf + 1)], PS[b], ALU.add,
                       ["xm%d" % b4, "ps%d" % b], ["xm%d" % b4])
                if l < L - 1:
                    simple_dma(x1[r0:r0 + 128, :], xm[b4], "xm%d" % b4, reads=["xm%d" % b4], writes=["D:x%d:%d" % (l + 1, c)])
                else:
                    act(junk, xm[b4], AF.Square, ["xm%d" % b4], ["junk", "ssv"], accum_out=ssv[:, 4 + b4:5 + b4])
                    rsqrt(ssv[:, 4 + b4:5 + b4], D, ssv[:, 4 + b4:5 + b4], ["ssv"], "ssv")
                    stt("dve", xm[b4], xm[b4], ssv[:, 4 + b4:5 + b4], gL, ALU.mult, ALU.mult, ["xm%d" % b4, "ssv", "gL"], ["xm%d" % b4])
                    ro = out_row0(c) + 128 * b4
                    simple_dma(y_out[ro:ro + 128, :], xm[b4], "xm%d" % b4, reads=["xm%d" % b4], writes=["D:y:%d" % c])
                if nxt:
                    load_block(ci + 1, b4)
                    if b4 >= 1:
                        front_block(b4 - 1)
            if nxt:
                front_block(3)

    phases = []
    for l in range(L):
        phases += [("A", l), ("Bna", l), ("Bd", l), ("C1", l), ("C2", l)]
    for name, l in phases:
        {"A": phase_A, "Bna": phase_B_na, "Bd": phase_B_dense, "C1": phase_C1, "C2": phase_C2}[name](l)
        if stop_after == (name, l):
            break
    P.final_wait("sp")
    cc = P.emit(nc, st)
    st.close()
    return nc, (len(P.ops), {k: len(v) for k, v in P.streams.items()}, cc)


def rope_perm(d):
    half = d // 2
    return np.concatenate([np.arange(half, d), np.arange(0, half)])


def rope_tables(pos, d):
    half = d // 2
    inv = (10000.0 ** (-(np.arange(half, dtype=np.float32) * 2.0) / d)).astype(np.float32)
    ang = pos.astype(np.float32)[None, :] * inv[:, None]
    cos = np.cos(ang).astype(np.float32)
    sin = np.sin(ang).astype(np.float32)
    return np.concatenate([cos, cos], 0), np.concatenate([-sin, sin], 0)


def na_bias_tile(rpb_l, rows, tq, tk):
    out = np.full((128, 4, 128), NEG, np.float32)
    nb = rows // 2
    if tk is None or tk < 0 or tk >= nb:
        return out.reshape(128, 512)
    qi = np.arange(128)
    r = 2 * tq + qi // 64
    cq = qi % 64
    ki = np.arange(128)
    kr = 2 * tk + ki // 64
    ck = ki % 64
    kr_n = min(8, rows)
    rs = np.clip(r - kr_n // 2, 0, rows - kr_n)
    cs = np.clip(cq - 8, 0, GRID_W - 16)
    valid = ((kr[:, None] >= rs[None, :]) & (kr[:, None] < rs[None, :] + kr_n)
             & (ck[:, None] >= cs[None, :]) & (ck[:, None] < cs[None, :] + 16))
    dr = np.clip(kr[:, None] - r[None, :] + 7, 0, 14)
    dc = np.clip(ck[:, None] - cq[None, :] + 15, 0, 30)
    vals = rpb_l[:, dr, dc]
    out = np.where(valid[:, None, :], vals.transpose(1, 0, 2), np.float32(NEG)).astype(np.float32)
    return np.ascontiguousarray(out[:, list(HS), :]).reshape(128, 512)


def prep_weights(inp):
    f = lambda a: np.asarray(a, np.float32)
    w_in = f(inp["w_in"])
    Ls = w_in.shape[0]
    segs = np.cumsum([0, 256, 256, 256, 384, 256, 32, 384, 128, 128])
    o_qa, o_ka, o_va, o_cq, o_ckv, o_kpe, o_qc, o_kc, o_vc = segs[:9]
    p32 = rope_perm(32)
    p_ax = np.concatenate([rope_perm(32), 32 + rope_perm(32)])
    win = np.zeros((Ls, D, NIN), np.float32)
    win[:, :, C_QA:C_QA + 256] = w_in[:, :, o_qa:o_qa + 256]
    win[:, :, C_KA:C_KA + 256] = w_in[:, :, o_ka:o_ka + 256]
    win[:, :, C_CQ:C_CQ + 384] = w_in[:, :, o_cq:o_cq + 384]
    win[:, :, C_CKV:C_CKV + 256] = w_in[:, :, o_ckv:o_ckv + 256]
    win[:, :, C_KPA + 64:C_KPA + 96] = w_in[:, :, o_kpe:o_kpe + 32]
    win[:, :, C_KPB + 64:C_KPB + 96] = w_in[:, :, o_kpe + p32]
    qc_sw = np.concatenate([o_qc + 64 * h + p_ax for h in range(6)])
    kc_sw = np.concatenate([o_kc + 64 * h + p_ax for h in range(2)])
    win[:, :, C_QCA:C_QCA + 384] = w_in[:, :, o_qc:o_qc + 384]
    win[:, :, C_QCB:C_QCB + 384] = w_in[:, :, qc_sw]
    win[:, :, C_KCA:C_KCA + 128] = w_in[:, :, o_kc:o_kc + 128]
    win[:, :, C_KCB:C_KCB + 128] = w_in[:, :, kc_sw]
    win[:, :, C_V:C_V + 256] = w_in[:, :, o_va:o_va + 256]
    win[:, :, C_V + 256:C_V + 384] = w_in[:, :, o_vc:o_vc + 128]
    w_uq = f(inp["w_uq"])
    wuq = np.zeros((Ls, 384, 1152), np.float32)
    for h in range(6):
        wuq[:, :, 192 * h:192 * h + 96] = w_uq[:, :, 96 * h:96 * h + 96]
        wuq[:, :, 192 * h + 96:192 * h + 160] = w_uq[:, :, 96 * h:96 * h + 64]
        wuq[:, :, 192 * h + 160:192 * h + 192] = w_uq[:, :, 96 * h + 64 + p32]
    w_ukv = f(inp["w_ukv"])
    wukv = np.zeros((Ls, 256, 768), np.float32)
    for h in range(6):
        wukv[:, :, 64 * h:64 * h + 64] = w_ukv[:, :, 128 * h:128 * h + 64]
        wukv[:, :, 384 + 64 * h:384 + 64 * h + 64] = w_ukv[:, :, 128 * h + 64:128 * h + 128]
    gsm = np.zeros((Ls, 128, 17), np.float32)
    qan = f(inp["q_a_norm"])
    kvn = f(inp["kv_a_norm"])
    qn = f(inp["q_norm_c"])
    kn = f(inp["k_norm_c"])
    gout = np.concatenate([f(inp["g_out_a"]), f(inp["g_out_b"]), f(inp["g_out_c"])], 1)
    for l in range(Ls):
        gsm[l, :, 0:3] = qan[l].reshape(3, 128).T
        gsm[l, :, 3:5] = kvn[l].reshape(2, 128).T
        gsm[l, :, 5] = np.tile(qn[l], 2)
        gsm[l, :, 6] = np.tile(qn[l][p_ax], 2)
        gsm[l, :, 7] = np.tile(kn[l], 2)
        gsm[l, :, 8] = np.tile(kn[l][p_ax], 2)
        gsm[l, :, 9:17] = gout[l].reshape(8, 128).T
    gbc = np.stack([f(inp["attn_norm"]), f(inp["ffn_norm"])], 1)
    bd = np.zeros((128, 128), np.float32)
    bd[:64, :64] = 1
    bd[64:, 64:] = 1
    return dict(win=win, wuq=wuq, wukv=wukv, wout=f(inp["w_out"]), wg=f(inp["w_gate"]), wu=f(inp["w_up"]),
                wd=f(inp["w_down"]), gsm=gsm, gbc=np.ascontiguousarray(gbc), gfin=f(inp["final_norm"]),
                ident=np.eye(128, dtype=np.float32).astype(ml_dtypes.bfloat16), bdones=bd.astype(ml_dtypes.bfloat16))


def prep_core_tables(rpb, slots, true_pos):
    pos = np.concatenate(true_pos)
    row = pos // GRID_W
    col = pos % GRID_W
    cr, sr = rope_tables(row, 32)
    cc, sc = rope_tables(col, 32)
    cosC = np.concatenate([cr, cc], 0)
    sinC = np.concatenate([sr, sc], 0)
    ropeC = np.stack([np.tile(cosC, (2, 1)), np.tile(sinC, (2, 1))], 0).astype(np.float32)
    cb, sb = rope_tables(pos, 32)
    ropeB = np.stack([cb, sb], 0).astype(np.float32)
    Ls = rpb.shape[0]
    nab_int = np.zeros((Ls, 128, 5, 512), np.float32)
    sp_tiles = [[] for _ in range(Ls)]
    for l in range(Ls):
        for di, d in enumerate((-2, -1, 0, 1, 2)):
            nab_int[l, :, di, :] = na_bias_tile(rpb[l], 64, 8, 8 + d)
        for si, (S, nact) in enumerate(slots):
            nb = S // 128
            rows = S // GRID_W
            tb = true_pos[si][::128] // 128
            for i in na_specials(S, nact):
                for d in NA_OFFS:
                    tq = int(tb[i])
                    tk = tq + d
                    j = (i + d) % nb
                    if 0 <= tk < nb:
                        assert int(tb[j]) == tk
                    sp_tiles[l].append(na_bias_tile(rpb[l], rows, tq, tk))
    nab_sp = np.stack([np.stack(t, 0) for t in sp_tiles], 0)
    return ropeC, ropeB, nab_int, nab_sp


_CACHE = {}


def run_slots(slots, per_core_x, per_core_pos, inp, debug=False, stop_after=None, trace=False):
    key = (tuple(slots), debug, stop_after)
    if key not in _CACHE:
        _CACHE[key] = build_program(slots, debug=debug, stop_after=stop_after)
    nc, _ = _CACHE[key]
    W = prep_weights(inp)
    rpb = np.asarray(inp["rpb"], np.float32)
    in_maps = []
    for x, pos in zip(per_core_x, per_core_pos):
        ropeC, ropeB, nab_int, nab_sp = prep_core_tables(rpb, slots, pos)
        m = dict(W)
        m.update(x_in=np.ascontiguousarray(x, dtype=np.float32), ropeC=ropeC, ropeB=ropeB, nab_int=nab_int, nab_sp=nab_sp)
        in_maps.append(m)
    res = run_bass_kernel_spmd(nc, in_maps, core_ids=list(range(len(in_maps))), **({"trace": True} if trace else {}))
    return res


def kernel(**inp):
    xp = np.asarray(inp["x_prompt"], np.float32)
    xs = np.asarray(inp["x_sample"], np.float32)
    B, S, _ = xp.shape
    Bs, Ss, _ = xs.shape
    assert Bs == 2 * 8 and B * 2 == 8
    H = S // 2
    slots = [(Ss, Ss), (Ss, Ss), (S, H)]
    per_x, per_pos = [], []
    for c in range(8):
        p, h = c // 2, c % 2
        xl = np.concatenate([xs[2 * c], xs[2 * c + 1], np.roll(xp[p], -h * H, axis=0)], 0)
        per_x.append(xl)
        per_pos.append([np.arange(Ss), np.arange(Ss), (np.arange(S) + h * H) % S])
    res = run_slots(slots, per_x, per_pos, inp)
    yp = np.zeros_like(xp)
    ys = np.zeros_like(xs)
    for c in range(8):
        y = res.results[c]["y_out"]
        p, h = c // 2, c % 2
        ys[2 * c] = y[0:Ss]
        ys[2 * c + 1] = y[Ss:2 * Ss]
        yp[p, h * H:(h + 1) * H] = y[2 * Ss:2 * Ss + H]
    return yp, ys
```

```python
import contextlib
import math

import ml_dtypes
import numpy as np

import concourse.bass as bass
import concourse.mybir as mybir
from concourse.bass_utils import run_bass_kernel_spmd

F32 = mybir.dt.float32
BF16 = mybir.dt.bfloat16
ALU = mybir.AluOpType
AF = mybir.ActivationFunctionType
AX = mybir.AxisListType

D = 1024
L = 2
GRID_W = 64
DFF = 2816
NFF = DFF // 128
EPS = 1e-6
NEG = -30000.0
NIN = 2752
C_QA, C_KA, C_CQ, C_CKV, C_KPA, C_KPB, C_QCA, C_QCB, C_KCA, C_KCB, C_V = (
    0, 256, 512, 896, 1152, 1248, 1344, 1728, 2112, 2240, 2368)
NA_OFFS = (-3, -2, -1, 0, 1, 2, 3)
HS = (0, 2, 1, 3)

SAME_ENGINE_SYNC = {"pe": False, "act": True, "dve": True, "pool": True, "sp": False}


class Op:
    __slots__ = ("stream", "fn", "deps", "flag", "count", "semkey", "nparts", "waits")

    def __init__(self, stream, fn, semkey=None, nparts=0):
        self.stream = stream
        self.fn = fn
        self.deps = ()
        self.flag = False
        self.count = None
        self.semkey = semkey
        self.nparts = nparts
        self.waits = None


class Prog:
    def __init__(self):
        self.ops = []
        self.slots = {}
        self.streams = {s: [] for s in ("pe", "act", "dve", "pool", "sp")}
        self.last = {}

    def _slot(self, name):
        s = self.slots.get(name)
        if s is None:
            s = [None, {}]
            self.slots[name] = s
        return s

    def op(self, stream, fn, reads=(), writes=(), semkey=None, nparts=0, extra_deps=()):
        o = Op(stream, fn, semkey, nparts)
        deps = set(extra_deps)
        okey = semkey if semkey is not None else stream
        for r in reads:
            s = self._slot(r)
            if s[0] is not None:
                deps.add(s[0])
            if r.startswith("ps"):
                for k, rd in s[1].items():
                    if k != okey:
                        deps.add(rd)
        for w in writes:
            s = self._slot(w)
            if s[0] is not None:
                deps.add(s[0])
            for rd in s[1].values():
                deps.add(rd)
        for r in reads:
            self._slot(r)[1][okey] = o
        for w in writes:
            s = self._slot(w)
            s[0] = o
            s[1] = {}
        deps.discard(o)
        o.deps = tuple(deps)
        self.ops.append(o)
        self.streams[stream].append(o)
        if fn is not None:
            self.last[okey] = o
        return o

    def dma(self, stream, fn, semkey, nparts, reads=(), writes=()):
        return self.op(stream, fn, reads, writes, semkey=semkey, nparts=nparts)

    def barrier(self):
        lasts = list(self.last.values())
        for s in self.streams:
            self.op(s, None, extra_deps=lasts)
        self.slots = {k: v for k, v in self.slots.items() if k.startswith("D:")}

    def resolve(self):
        known = {s: {} for s in self.streams}
        seq = {}
        cnt = {}
        for o in self.ops:
            key = o.semkey if o.semkey is not None else o.stream
            cnt[key] = cnt.get(key, 0) + 1
            seq[o] = (key, cnt[key])
        for o in self.ops:
            need = {}
            kn = known[o.stream]
            for d in o.deps:
                key, n = seq[d]
                if d.semkey is None and d.stream == o.stream and not SAME_ENGINE_SYNC[o.stream]:
                    continue
                if kn.get(key, 0) >= n:
                    continue
                if key not in need or seq[need[key]][1] < n:
                    need[key] = d
            o.waits = list(need.values())
            for key, d in need.items():
                kn[key] = seq[d][1]
                d.flag = True
            o.deps = ()
        ccount = {}
        for o in self.ops:
            if o.semkey is not None:
                ccount[o.semkey] = ccount.get(o.semkey, 0) + 16 * o.nparts
                o.count = ccount[o.semkey]
            elif o.flag:
                ccount[o.stream] = ccount.get(o.stream, 0) + 1
                o.count = ccount[o.stream]
        return ccount

    def emit(self, nc, stack):
        ccount = self.resolve()
        sems = {}
        for i, key in enumerate(ccount):
            sems[key] = stack.enter_context(nc.semaphore("s%d" % i))
        engmap = {"pe": "tensor", "act": "scalar", "dve": "vector", "pool": "gpsimd", "sp": "sync"}
        block = stack.enter_context(nc.Block())
        for sname, ops in self.streams.items():
            if not ops:
                continue

            def body(e, ops=ops, sname=sname):
                for o in ops:
                    for d in o.waits:
                        key = d.semkey if d.semkey is not None else d.stream
                        e.wait_ge(sems[key], d.count)
                    if o.fn is None:
                        continue
                    if o.semkey is not None:
                        o.fn(e, sems[o.semkey])
                    else:
                        ins = o.fn(e)
                        if o.flag:
                            ins.then_inc(sems[sname], 1)

            getattr(block, engmap[sname])(body)
        return ccount

    def final_wait(self, stream="sp"):
        self.op(stream, None, extra_deps=list(self.last.values()))


class Arena:
    def __init__(self, base_ap, nwords):
        self.base = base_ap
        self.nwords = nwords
        self.off = 0

    def reset(self):
        self.off = 0

    def alloc(self, shape, dt):
        n = 1
        for s in shape[1:]:
            n *= s
        nb = n * (2 if dt == BF16 else 4)
        n4 = (nb + 3) // 4
        n4 = (n4 + 7) // 8 * 8
        assert self.off + n4 <= self.nwords, ("SBUF arena overflow", self.off + n4, self.nwords)
        a = self.base[:, self.off:self.off + n4]
        self.off += n4
        if dt == BF16:
            a = a.bitcast(BF16)
        a = a[:, 0:n]
        if len(shape) == 3:
            a = a.rearrange("p (a b) -> p a b", b=shape[2])
        elif len(shape) == 4:
            a = a.rearrange("p (a b c) -> p a b c", b=shape[2], c=shape[3])
        return a


def na_specials(S, nact):
    nb = S // 128
    half = nb // 2
    if nact == S:
        sp = {0, 1, nb - 2, nb - 1}
    else:
        sp = {0, 1, nb - 2, nb - 1, half - 2, half - 1, half, half + 1}
    return sorted(x for x in sp if 0 <= x < nb)


def build_program(slots, debug=False, stop_after=None):
    nc = bass.Bass("TRN2", target_bir_lowering=False)
    T = sum(S for S, _ in slots)
    TA = sum(a for _, a in slots)
    NCH = T // 512
    slot_tok0 = np.cumsum([0] + [S for S, _ in slots]).tolist()
    slot_out0 = np.cumsum([0] + [a for _, a in slots]).tolist()
    nsp_tot = sum(len(na_specials(S, a)) for S, a in slots)

    def din(name, shape, dt=F32):
        return nc.dram_tensor(name, list(shape), dt, kind="ExternalInput").ap()

    skind = "ExternalOutput" if debug else "Internal"

    def dscr(name, shape, dt):
        return nc.dram_tensor(name, list(shape), dt, kind=skind).ap()

    x_in = din("x_in", [T, D])
    y_out = nc.dram_tensor("y_out", [TA, D], F32, kind="ExternalOutput").ap()
    win = din("win", [L, D, NIN])
    wuq = din("wuq", [L, 384, 1152])
    wukv = din("wukv", [L, 256, 768])
    wout = din("wout", [L, D, D])
    wg = din("wg", [L, D, DFF])
    wu = din("wu", [L, D, DFF])
    wd = din("wd", [L, DFF, D])
    gsm = din("gsm", [L, 128, 17])
    gbc = din("gbc", [L, 2, D])
    gfin = din("gfin", [D])
    ropeC = din("ropeC", [2, 128, T])
    ropeB = din("ropeB", [2, 32, T])
    nab_int = din("nab_int", [L, 128, 5, 512])
    nab_sp = din("nab_sp", [L, nsp_tot * 7, 128, 512])
    ident_d = din("ident", [128, 128], BF16)
    bdones_d = din("bdones", [128, 128], BF16)

    x1 = dscr("x1", [T, D], F32)
    xmid = dscr("xmid", [T, D], F32)
    qaT = dscr("qaT", [256, T], BF16)
    kaT = dscr("kaT", [256, T], BF16)
    va = dscr("va", [T, 4, 65], BF16)
    qbT = dscr("qbT", [6, 96, T], BF16)
    kbT = dscr("kbT", [6, 96, T], BF16)
    vb = dscr("vb", [T, 6, 65], BF16)
    qcT = dscr("qcT", [384, T], BF16)
    kcT = dscr("kcT", [128, T], BF16)
    vc = dscr("vc", [T, 2, 65], BF16)
    oT = dscr("oT", [D, T], F32)

    P = Prog()
    st = contextlib.ExitStack()
    AW = 51200
    arena_t = st.enter_context(nc.sbuf_tensor("arena", [128, AW], F32))
    A = Arena(arena_t[:], AW)
    psall_t = st.enter_context(nc.psum_tensor("psall", [128, 4096], F32))
    psall = psall_t[:]
    PS = [psall[:, 512 * i:512 * (i + 1)] for i in range(8)]
    PSB = [psall[:, 512 * i:512 * (i + 1)].bitcast(BF16) for i in range(8)]

    dq = ["sp"]

    def dma(fn, key, n, reads=(), writes=(), q=None):
        return P.dma(q or "sp", fn, "q_" + key, n, reads, writes)

    def simple_dma(out, in_, key, reads=(), writes=(), q=None):
        return dma(lambda e, s: e.dma_start(out=out, in_=in_).then_inc(s, 16), key, 1, reads, writes, q)

    def mm(out, lhsT, rhs, start, stop, reads, writes):
        return P.op("pe", lambda e: e.matmul(out, lhsT=lhsT, rhs=rhs, start=start, stop=stop), reads, writes)

    def act(out, in_, func, reads, writes, **kw):
        return P.op("act", lambda e: e.activation(out=out, in_=in_, func=func, **kw), reads, writes)

    def vcopy(eng, out, in_, reads, writes):
        if eng == "act":
            return act(out, in_, AF.Copy, reads, writes)
        return P.op(eng, lambda e: e.tensor_copy(out=out, in_=in_), reads, writes)

    def tt(eng, out, in0, in1, op, reads, writes):
        return P.op(eng, lambda e: e.tensor_tensor(out=out, in0=in0, in1=in1, op=op), reads, writes)

    def stt(eng, out, in0, scalar, in1, op0, op1, reads, writes):
        return P.op(eng, lambda e: e.scalar_tensor_tensor(out=out, in0=in0, scalar=scalar, in1=in1, op0=op0, op1=op1),
                    reads, writes)

    consts = A.alloc([128, 8], F32)
    ident = A.alloc([128, 128], BF16)
    bdones = A.alloc([128, 128], BF16)
    ones_bf = A.alloc([128, 128], BF16)
    sel_b = A.alloc([128, 64], BF16)
    gfin_b = None
    GLOBAL_OFF = None

    P.op("dve", lambda e: e.memset(consts[:, 0:1], EPS), writes=["consts"])
    P.op("dve", lambda e: e.memset(consts[:, 1:2], 1.0), writes=["consts"])
    P.op("dve", lambda e: e.memset(ones_bf, 1.0), writes=["ones_bf"])
    P.op("dve", lambda e: e.memset(sel_b, 0.0), writes=["sel_b"])
    P.op("dve", lambda e: e.memset(sel_b[64:65, :], 1.0), writes=["sel_b"])
    simple_dma(ident, ident_d, "c0", writes=["ident"])
    simple_dma(bdones, bdones_d, "c1", writes=["bdones"])
    GLOBAL_OFF = A.off

    def eps_col(ap):
        b = ap.base_partition()
        return consts[b:b + ap.shape[0], 0:1]

    def rsqrt(ap, n, src, reads, slot):
        act(ap, src, AF.Sqrt, list(reads) + ["consts"], [slot], scale=1.0 / n, bias=eps_col(ap))
        P.op("dve", lambda e: e.reciprocal(out=ap, in_=ap), [slot], [slot])

    def chunk_slot(c):
        t = 512 * c
        for i in range(len(slots)):
            if slot_tok0[i] <= t < slot_tok0[i + 1]:
                return i
        raise AssertionError

    def chunk_active(c, l):
        if l == 0:
            return True
        i = chunk_slot(c)
        return 512 * c - slot_tok0[i] < slots[i][1]

    def out_row0(c):
        i = chunk_slot(c)
        return slot_out0[i] + 512 * c - slot_tok0[i]

    def phase_A(l):
        P.barrier()
        A.off = GLOBAL_OFF
        x_src = x_in if l == 0 else x1
        win_sb = A.alloc([128, 8, NIN], BF16)
        wuq_sb = A.alloc([128, 3, 1152], BF16)
        wukv_sb = A.alloc([128, 2, 768], BF16)
        gsm_sb = A.alloc([128, 17], F32)
        gA = A.alloc([128, D], F32)
        xa = [A.alloc([128, 4, D], F32) for _ in range(2)]
        junk = A.alloc([128, D], F32)
        ss = [A.alloc([128, 4], F32) for _ in range(2)]
        hb = [A.alloc([128, D], BF16) for _ in range(2)]
        hT = [A.alloc([128, 8, 512], BF16) for _ in range(2)]
        rC = [A.alloc([128, 2, 512], F32) for _ in range(2)]
        rB = [A.alloc([128, 2, 512], F32) for _ in range(2)]
        NSTG = 8
        stg = [A.alloc([128, 512], BF16) for _ in range(NSTG)]
        cqf = A.alloc([128, 5, 512], F32)
        sqb = [A.alloc([128, 512], BF16) for _ in range(2)]
        cn = A.alloc([128, 5, 512], BF16)
        rs = [A.alloc([128, 512], F32) for _ in range(2)]
        tA = [A.alloc([128, 512], F32) for _ in range(2)]
        tB = [A.alloc([128, 512], F32) for _ in range(2)]
        vst = A.alloc([128, 4, 6, 65], BF16)
        vsb = A.alloc([128, 4, 6, 65], BF16)
        kpe_s = A.alloc([128, 512], BF16)

        for k in range(8):
            dma(lambda e, s, k=k: e.dma_start(out=win_sb[:, k, :], in_=win[l, 128 * k:128 * (k + 1), :]).then_inc(s, 16),
                "w_in%d" % k, 1, writes=["win_sb"], q="pool")
        dma(lambda e, s: e.dma_start(out=wuq_sb, in_=wuq[l].rearrange("(k p) n -> p k n", p=128)).then_inc(s, 16),
            "w_uq", 1, writes=["wuq_sb"], q="pool")
        dma(lambda e, s: e.dma_start(out=wukv_sb, in_=wukv[l].rearrange("(k p) n -> p k n", p=128)).then_inc(s, 16),
            "w_ukv", 1, writes=["wukv_sb"], q="pool")
        simple_dma(gsm_sb, gsm[l], "g0", writes=["gsm_sb"])
        simple_dma(gA, gbc[l, 0].partition_broadcast(128), "g1", writes=["gA"])
        P.op("pool", lambda e: e.memset(vst, 1.0), writes=["vst"])
        P.op("pool", lambda e: e.memset(vsb, 1.0), writes=["vsb"])

        bank_rr = [0]

        def nbank():
            b = 2 + bank_rr[0] % 4
            bank_rr[0] += 1
            return b

        stg_rr = [0]

        def nstg():
            i = stg_rr[0] % NSTG
            stg_rr[0] += 1
            return i

        ev_rr = [0]

        def ev_eng():
            ev_rr[0] += 1
            return "act" if ev_rr[0] % 2 else "dve"

        def x_load(c):
            i = c % 2
            t0 = 512 * c
            dma(lambda e, s: e.dma_start(out=xa[i], in_=x_src[t0:t0 + 512, :].rearrange("(b p) d -> p b d", p=128)).then_inc(s, 16),
                "pxa%d" % i, 1, reads=["D:x%d:%d" % (l, c)], writes=["xa%d" % i], q="pool")

        def rope_load(c):
            i = c % 2
            t0 = 512 * c
            dma(lambda e, s: e.dma_start(out=rC[i], in_=ropeC[:, :, t0:t0 + 512].rearrange("a p t -> p a t")).then_inc(s, 16),
                "prC%d" % i, 1, writes=["rC%d" % i], q="pool")
            dma(lambda e, s: e.dma_start(out=rB[i][64:96], in_=ropeB[:, :, t0:t0 + 512].rearrange("a p t -> p a t")).then_inc(s, 16),
                "prB%d" % i, 1, writes=["rB%d" % i], q="pool")

        def front_load(c):
            i = c % 2
            for b in range(4):
                act(junk, xa[i][:, b, :], AF.Square, ["xa%d" % i], ["junk", "ss%d" % i], accum_out=ss[i][:, b:b + 1])
            rsqrt(ss[i], D, ss[i], ["ss%d" % i], "ss%d" % i)

        def front_block(c, b):
            i = c % 2
            j = b % 2
            stt("dve", hb[j], xa[i][:, b, :], ss[i][:, b:b + 1], gA, ALU.mult, ALU.mult,
                ["xa%d" % i, "ss%d" % i, "gA"], ["hb%d" % j])
            for k in range(8):
                P.op("pe", lambda e, k=k, j=j: e.transpose(out=PSB[j][:, 128 * k:128 * (k + 1)], in_=hb[j][:, 128 * k:128 * (k + 1)], identity=ident),
                     ["hb%d" % j, "ident"], ["ps%d" % j])
            vcopy("act" if b % 2 else "dve", hT[i][:, :, 128 * b:128 * (b + 1)],
                  PSB[j].rearrange("p (k t) -> p k t", t=128), ["ps%d" % j], ["hT%d" % i])

        pending = []

        def hook():
            if pending:
                pending.pop(0)()

        def fm(i, col0, M, wsb=None, nk=8, rhs_fn=None, rslot=None):
            wsb = win_sb if wsb is None else wsb
            b = nbank()
            for k in range(nk):
                rhs = hT[i][:, k, :] if rhs_fn is None else rhs_fn(k)
                mm(PS[b][0:M, :], wsb[:, k, col0:col0 + M], rhs, k == 0, k == nk - 1,
                   (list(rslot) if rslot else ["hT%d" % i]) + ["win_sb", "wuq_sb", "wukv_sb"], ["ps%d" % b])
            return b

        def store_fm(ap_sb, dst, key_i, c, name, npart=128):
            simple_dma(dst, ap_sb, "stg%d" % key_i, reads=["stg%d" % key_i], writes=["D:%s:%d" % (name, c)])

        def back(c):
            i = c % 2
            t0 = 512 * c
            ts = slice(t0, t0 + 512)
            hs = "hT%d" % i
            for name, col0, dst in (("qaT", C_QA, qaT), ("kaT", C_KA, kaT)):
                for j in range(2):
                    b = fm(i, col0 + 128 * j, 128)
                    si = nstg()
                    vcopy(ev_eng(), stg[si], PS[b], ["ps%d" % b], ["stg%d" % si])
                    store_fm(stg[si], dst[128 * j:128 * (j + 1), ts], si, c, name)
            hook()
            if CUT == 3:
                return
            for (col0, nchk, off, gcol, nfeat, nb_) in ((C_CQ, 3, 0, 0, 384, 6), (C_CKV, 2, 3, 3, 256, 7)):
                for j in range(nchk):
                    b = fm(i, col0 + 128 * j, 128)
                    q = j % 2
                    act(sqb[q], PS[b], AF.Square, ["ps%d" % b], ["sqb%d" % q])
                    vcopy("dve", cqf[:, off + j, :], PS[b], ["ps%d" % b], ["cqf%d" % (off + j)])
                    mm(PS[nb_], ones_bf, sqb[q], j == 0, j == nchk - 1, ["sqb%d" % q, "ones_bf"], ["ps%d" % nb_])
                r = (off // 3) % 2
                rsqrt(rs[r], nfeat, PS[nb_], ["ps%d" % nb_], "rs%d" % r)
                for j in range(nchk):
                    stt("dve", cn[:, off + j, :], cqf[:, off + j, :], gsm_sb[:, gcol + j:gcol + j + 1], rs[r], ALU.mult, ALU.mult,
                        ["cqf%d" % (off + j), "gsm_sb", "rs%d" % r], ["cn%d" % (off + j)])
            hook()
            if CUT == 4:
                return
            bA = fm(i, C_KPA, 96)
            bB = fm(i, C_KPB, 96)
            tt("dve", tA[0][64:96], PS[bA][64:96], rB[i][64:96, 0, :], ALU.mult, ["ps%d" % bA, "rB%d" % i], ["tA0"])
            tt("dve", tB[0][64:96], PS[bB][64:96], rB[i][64:96, 1, :], ALU.mult, ["ps%d" % bB, "rB%d" % i], ["tB0"])
            tt("dve", kpe_s[64:96], tA[0][64:96], tB[0][64:96], ALU.add, ["tA0", "tB0"], ["kpe_s"])
            dma(lambda e, s: [e.dma_start(out=kbT[h, 64:96, ts], in_=kpe_s[64:96]).then_inc(s, 16) for h in range(6)],
                "kpe", 6, reads=["kpe_s"], writes=["D:kbTp:%d" % c])
            if CUT == 5:
                return
            for (colA, colB, gc, name, dst, row0) in ([(C_QCA + 128 * j, C_QCB + 128 * j, 5, "qcT", qcT, 128 * j) for j in range(3)]
                                                      + [(C_KCA, C_KCB, 7, "kcT", kcT, 0)]):
                bA = fm(i, colA, 128)
                bB = fm(i, colB, 128)
                q = ev_rr[0] % 2
                ev_rr[0] += 1
                act(sqb[q], PS[bA], AF.Square, ["ps%d" % bA], ["sqb%d" % q])
                act(tA[q], PS[bA], AF.Copy, ["ps%d" % bA, "gsm_sb"], ["tA%d" % q], scale=gsm_sb[:, gc:gc + 1])
                act(tB[q], PS[bB], AF.Copy, ["ps%d" % bB, "gsm_sb"], ["tB%d" % q], scale=gsm_sb[:, gc + 1:gc + 2])
                nb_ = 6 + q
                mm(PS[nb_], bdones, sqb[q], True, True, ["sqb%d" % q, "bdones"], ["ps%d" % nb_])
                rsqrt(rs[q], 64, PS[nb_], ["ps%d" % nb_], "rs%d" % q)
                tt("pool", tB[q], tB[q], rC[i][:, 1, :], ALU.mult, ["tB%d" % q, "rC%d" % i], ["tB%d" % q])
                tt("dve", tA[q], tA[q], rC[i][:, 0, :], ALU.mult, ["tA%d" % q, "rC%d" % i], ["tA%d" % q])
                tt("dve", tA[q], tA[q], tB[q], ALU.add, ["tA%d" % q, "tB%d" % q], ["tA%d" % q])
                si = nstg()
                tt("dve", stg[si], tA[q], rs[q], ALU.mult, ["tA%d" % q, "rs%d" % q], ["stg%d" % si])
                store_fm(stg[si], dst[row0:row0 + 128, ts], si, c, name)
            hook()
            if CUT == 6:
                return
            for h in range(6):
                bA = fm(i, 192 * h, 96, wsb=wuq_sb, nk=3, rhs_fn=lambda k: cn[:, k, :], rslot=("cn0", "cn1", "cn2"))
                bB = fm(i, 192 * h + 96, 96, wsb=wuq_sb, nk=3, rhs_fn=lambda k: cn[:, k, :], rslot=("cn0", "cn1", "cn2"))
                si = nstg()
                q = h % 2
                vcopy("act", stg[si][0:64], PS[bA][0:64], ["ps%d" % bA], ["stg%d" % si])
                tt("dve", tA[q][64:96], PS[bA][64:96], rB[i][64:96, 0, :], ALU.mult, ["ps%d" % bA, "rB%d" % i], ["tA%d" % q])
                tt("dve", tB[q][64:96], PS[bB][64:96], rB[i][64:96, 1, :], ALU.mult, ["ps%d" % bB, "rB%d" % i], ["tB%d" % q])
                tt("dve", stg[si][64:96], tA[q][64:96], tB[q][64:96], ALU.add, ["tA%d" % q, "tB%d" % q], ["stg%d" % si])
                simple_dma(qbT[h, :, ts], stg[si][0:96], "stg%d" % si, reads=["stg%d" % si], writes=["D:qbT:%d" % c])
            hook()
            if CUT == 7:
                return
            for h in range(6):
                b = fm(i, 64 * h, 64, wsb=wukv_sb, nk=2, rhs_fn=lambda k: cn[:, 3 + k, :], rslot=("cn3", "cn4"))
                si = nstg()
                vcopy(ev_eng(), stg[si][0:64], PS[b][0:64], ["ps%d" % b], ["stg%d" % si])
                simple_dma(kbT[h, 0:64, ts], stg[si][0:64], "stg%d" % si, reads=["stg%d" % si], writes=["D:kbTn:%d" % c])
            if CUT == 8:
                return
            for b4 in range(4):
                b = nbank()
                for k in range(8):
                    mm(PS[b][:, 0:384], hT[i][:, k, 128 * b4:128 * (b4 + 1)], win_sb[:, k, C_V:C_V + 384], k == 0, k == 7,
                       [hs, "win_sb"], ["ps%d" % b])
                vcopy(ev_eng(), vst[:, b4, :, 0:64], PS[b][:, 0:384].rearrange("p (h d) -> p h d", d=64), ["ps%d" % b], ["vst"])
                b = nbank()
                for k in range(2):
                    mm(PS[b][:, 0:384], cn[:, 3 + k, 128 * b4:128 * (b4 + 1)], wukv_sb[:, k, 384:768], k == 0, k == 1,
                       ["cn3", "cn4", "wukv_sb"], ["ps%d" % b])
                vcopy(ev_eng(), vsb[:, b4, :, 0:64], PS[b][:, 0:384].rearrange("p (h d) -> p h d", d=64), ["ps%d" % b], ["vsb"])
            simple_dma(va[ts].rearrange("(b p) h d -> p b h d", p=128), vst[:, :, 0:4, :], "vst_a", reads=["vst"], writes=["D:va:%d" % c])
            simple_dma(vc[ts].rearrange("(b p) h d -> p b h d", p=128), vst[:, :, 4:6, :], "vst_c", reads=["vst"], writes=["D:vc:%d" % c])
            simple_dma(vb[ts].rearrange("(b p) h d -> p b h d", p=128), vsb, "vsb", reads=["vsb"], writes=["D:vb:%d" % c])

        import os
        CUT = int(os.environ.get("K_CUT", "0"))
        if CUT == 1:
            return
        x_load(0)
        rope_load(0)
        if NCH > 1:
            x_load(1)
            rope_load(1)
        front_load(0)
        for b_ in range(4):
            front_block(0, b_)
        if CUT == 2:
            return
        for c in range(NCH):
            if c + 2 < NCH:
                x_load(c + 2)
            if c + 1 < NCH:
                front_load(c + 1)
                for b_ in range(4):
                    pending.append(lambda c=c, b_=b_: front_block(c + 1, b_))
            back(c)
            while pending:
                hook()
            if c + 2 < NCH:
                rope_load(c + 2)
            if CUT:
                return

    def attn_epilogue(acc_bank, bc_bank, ebuf, row0, t0, ntok, c_list, tag):
        osb, rc, hi, lo = ebuf

        def part1():
            vcopy("dve", osb[0:65, 0:ntok], PS[acc_bank][0:65, 0:ntok], ["ps%d" % acc_bank], [tag + "osb"])
            P.op("dve", lambda e: e.reciprocal(out=rc[64:65, 0:ntok], in_=osb[64:65, 0:ntok]), [tag + "osb"], [tag + "rc"])
            vcopy("dve", hi[64:65, 0:ntok], rc[64:65, 0:ntok], [tag + "rc"], [tag + "hi"])
            tt("dve", lo[64:65, 0:ntok], rc[64:65, 0:ntok], hi[64:65, 0:ntok], ALU.subtract, [tag + "rc", tag + "hi"], [tag + "lo"])

        def part2():
            mm(PS[bc_bank][0:64, 0:ntok], sel_b, hi[:, 0:ntok], True, False, [tag + "hi", "sel_b"], ["ps%d" % bc_bank])
            mm(PS[bc_bank][0:64, 0:ntok], sel_b, lo[:, 0:ntok], False, True, [tag + "lo", "sel_b"], ["ps%d" % bc_bank])
            tt("dve", osb[0:64, 0:ntok], osb[0:64, 0:ntok], PS[bc_bank][0:64, 0:ntok], ALU.mult,
               [tag + "osb", "ps%d" % bc_bank], [tag + "osb"])
            simple_dma(oT[row0:row0 + 64, t0:t0 + ntok], osb[0:64, 0:ntok], tag + "osb", reads=[tag + "osb"],
                       writes=["D:oT%d:%d" % (row0, c) for c in c_list])

        return part1, part2

    def emit_sorted(items):
        items.sort(key=lambda t: (t[0], t[1]))
        for _, _, fn in items:
            fn()

    def phase_B_dense(l):
        P.barrier()
        A.off = GLOBAL_OFF
        SM = max(S for S, _ in slots)
        NB = SM // 128
        kbuf = [A.alloc([128, SM], BF16) for _ in range(2)]
        vbuf = [A.alloc([128, NB, 65], BF16) for _ in range(2)]
        qbuf = [A.alloc([128, SM], BF16) for _ in range(2)]
        NPT = 4
        pt = [A.alloc([128, 1024], BF16) for _ in range(NPT)]
        ebufs = [(A.alloc([128, 512], F32), A.alloc([128, 512], F32), A.alloc([128, 512], BF16), A.alloc([128, 512], BF16))
                 for _ in range(2)]
        for i_, eb_ in enumerate(ebufs):
            P.op("pool", lambda e, eb_=eb_: e.memset(eb_[2], 0.0), writes=["e%dhi" % i_])
            P.op("pool", lambda e, eb_=eb_: e.memset(eb_[3], 0.0), writes=["e%dlo" % i_])
        work = []
        for si, (S, nact) in enumerate(slots):
            na = S if l == 0 else nact
            for h in range(6):
                work.append((si, "b", h, [h], na))
            for kv in range(2):
                work.append((si, "c", kv, [3 * kv, 3 * kv + 1, 3 * kv + 2], na))
        qjobs = []
        for wi, (si, kind, kv, qhs, na) in enumerate(work):
            for qh in qhs:
                qjobs.append((wi, qh))
        for i_ in range(2):
            P.op("pool", lambda e, i_=i_: e.memset(kbuf[i_], 0.0), writes=["kbuf%d" % i_])
            P.op("pool", lambda e, i_=i_: e.memset(qbuf[i_], 0.0), writes=["qbuf%d" % i_])

        def load_kv(wi):
            si, kind, kv, qhs, na = work[wi]
            S = slots[si][0]
            t0 = slot_tok0[si]
            i = wi % 2
            cl = range(t0 // 512, (t0 + S) // 512)
            if kind == "b":
                simple_dma(kbuf[i][0:96, 0:S], kbT[kv, :, t0:t0 + S], "kb%d" % i,
                           reads=["D:kbTn:%d" % c for c in cl] + ["D:kbTp:%d" % c for c in cl], writes=["kbuf%d" % i])
                simple_dma(vbuf[i][:, 0:S // 128, :], vb[t0:t0 + S, kv, :].rearrange("(b p) d -> p b d", p=128), "vb%d" % i,
                           reads=["D:vb:%d" % c for c in cl], writes=["vbuf%d" % i])
            else:
                P.op("pool", lambda e: e.memset(kbuf[i][64:128, 0:S], 0.0), writes=["kbuf%d" % i])
                simple_dma(kbuf[i][0:64, 0:S], kcT[64 * kv:64 * kv + 64, t0:t0 + S], "kb%d" % i,
                           reads=["D:kcT:%d" % c for c in cl], writes=["kbuf%d" % i])
                simple_dma(vbuf[i][:, 0:S // 128, :], vc[t0:t0 + S, kv, :].rearrange("(b p) d -> p b d", p=128), "vb%d" % i,
                           reads=["D:vc:%d" % c for c in cl], writes=["vbuf%d" % i])

        def load_q(qi):
            wi, qh = qjobs[qi]
            si, kind, kv, qhs, na = work[wi]
            t0 = slot_tok0[si]
            i = qi % 2
            cl = range(t0 // 512, (t0 + na) // 512)
            if kind == "b":
                simple_dma(qbuf[i][0:96, 0:na], qbT[qh, :, t0:t0 + na], "qb%d" % i,
                           reads=["D:qbT:%d" % c for c in cl], writes=["qbuf%d" % i])
            else:
                simple_dma(qbuf[i][0:64, 0:na], qcT[64 * qh:64 * qh + 64, t0:t0 + na], "qb%d" % i,
                           reads=["D:qcT:%d" % c for c in cl], writes=["qbuf%d" % i])

        items = []
        seqn = [0]

        def add(pos, fn):
            items.append((pos, seqn[0], fn))
            seqn[0] += 1

        LOOK = 2
        n = 0
        nacc = 0
        add(-2.0, lambda: load_kv(0))
        add(-2.0, lambda: load_q(0))
        for qi, (wi, qh) in enumerate(qjobs):
            si, kind, kv, qhs, na = work[wi]
            S = slots[si][0]
            t0 = slot_tok0[si]
            if qi + 1 < len(qjobs):
                nwi = qjobs[qi + 1][0]
                if nwi != wi:
                    add(n - 1 + LOOK + 0.55, lambda nwi=nwi: load_kv(nwi))
                add(n - 0.5, lambda qi=qi: load_q(qi + 1))
            ki = wi % 2
            qb_ = qi % 2
            dk = 96 if kind == "b" else 128
            scale = 1.0 / math.sqrt(96 if kind == "b" else 64)
            row0 = (256 + 64 * qh) if kind == "b" else (640 + 64 * qh)
            nkb = S // 128
            npair = nkb // 2
            for qc in range(na // 512):
                ab, bb = 6, 7
                eb = ebufs[nacc % 2]
                etag = "e%d" % (nacc % 2)
                nacc += 1
                for kp in range(npair):
                    sp_ = n % 3
                    pi = n % NPT

                    def front(kp=kp, sp_=sp_, pi=pi, ki=ki, qb_=qb_, qc=qc, dk=dk, scale=scale):
                        for u in range(2):
                            kb = 2 * kp + u
                            mm(PS[2 * sp_ + u], kbuf[ki][0:dk, 128 * kb:128 * (kb + 1)], qbuf[qb_][0:dk, 512 * qc:512 * (qc + 1)], True, True,
                               ["kbuf%d" % ki, "qbuf%d" % qb_], ["ps%d" % (2 * sp_ + u)])
                        act(pt[pi], psall[:, 1024 * sp_:1024 * (sp_ + 1)], AF.Exp, ["ps%d" % (2 * sp_), "ps%d" % (2 * sp_ + 1)],
                            ["pt%d" % pi], scale=scale)

                    def back(kp=kp, pi=pi, ki=ki, nkb=nkb, ab=ab):
                        for u in range(2):
                            kb = 2 * kp + u
                            mm(PS[ab][0:65, :], vbuf[ki][:, kb, :], pt[pi][:, 512 * u:512 * (u + 1)], kb == 0, kb == nkb - 1,
                               ["vbuf%d" % ki, "pt%d" % pi], ["ps%d" % ab])

                    add(n, front)
                    add(n + LOOK + 0.5, back)
                    n += 1
                tq = t0 + 512 * qc
                p1, p2 = attn_epilogue(ab, bb, eb, row0, tq, 512, [tq // 512], etag)
                add(n - 1 + LOOK + 0.6, p1)
                add(n - 1 + LOOK + 4.7, p2)
        emit_sorted(items)

    def phase_B_na(l):
        import os
        NACUT = int(os.environ.get("K_NACUT", "0"))
        P.barrier()
        A.off = GLOBAL_OFF
        SM = max(S for S, _ in slots)
        NB = SM // 128
        qz = A.alloc([128, 4, SM], BF16)
        ka_sb = A.alloc([128, 2, SM], BF16)
        P.op("pool", lambda e: e.memset(qz, 0.0), writes=["qz"])
        va_sb = A.alloc([128, NB, 4 * 65], BF16)
        bint = A.alloc([128, 5, 512], F32)
        bsp = [A.alloc([128, 512], F32) for _ in range(3)]
        sbias = [A.alloc([128, 512], F32) for _ in range(3)]
        pt = [A.alloc([128, 512], BF16) for _ in range(3)]
        ebufs = [(A.alloc([128, 512], F32), A.alloc([128, 512], F32), A.alloc([128, 512], BF16), A.alloc([128, 512], BF16))
                 for _ in range(2)]
        for i_, eb_ in enumerate(ebufs):
            P.op("pool", lambda e, eb_=eb_: e.memset(eb_[2], 0.0), writes=["e%dhi" % i_])
            P.op("pool", lambda e, eb_=eb_: e.memset(eb_[3], 0.0), writes=["e%dlo" % i_])
        simple_dma(bint, nab_int[l], "bint", writes=["bint"])
        sp_base = 0
        items = []
        seqn = [0]

        def add(pos, fn):
            items.append((pos, seqn[0], fn))
            seqn[0] += 1

        LOOK = 2
        n = 0
        spn = 0
        nacc = 0
        for si, (S, nact) in enumerate(slots):
            na = S if l == 0 else nact
            t0 = slot_tok0[si]
            nb = S // 128
            cl = list(range(t0 // 512, (t0 + S) // 512))
            specials = na_specials(S, nact)

            def loads(t0=t0, S=S, nb=nb, cl=cl):
                dma(lambda e, s_: [e.dma_start(out=qz[64 * (hd % 2):64 * (hd % 2) + 64, hd, 0:S],
                                               in_=qaT[64 * hd:64 * hd + 64, t0:t0 + S]).then_inc(s_, 16) for hd in range(4)],
                    "qz", 4, reads=["D:qaT:%d" % c for c in cl], writes=["qz"])
                simple_dma(ka_sb[:, :, 0:S], kaT[:, t0:t0 + S].rearrange("(j p) t -> p j t", p=128), "ka_sb",
                           reads=["D:kaT:%d" % c for c in cl], writes=["ka_sb"])
                simple_dma(va_sb[:, 0:nb, :], va[t0:t0 + S].rearrange("(b p) h d -> p b (h d)", p=128), "va_sb",
                           reads=["D:va:%d" % c for c in cl], writes=["va_sb"])

            add(n + LOOK + 5.0 if si else -1.0, loads)
            n += (LOOK + 6) if si else 0
            for g in range(na // 512):
                for qi4 in range(4):
                    i = 4 * g + qi4
                    if i in specials:
                        sidx = sp_base + specials.index(i)
                        offs = [(d, ("sp", sidx * 7 + di)) for di, d in enumerate(NA_OFFS)]
                    else:
                        offs = [(d, ("int", d + 2)) for d in (-2, -1, 0, 1, 2)]
                    for oi, (d, (kind, bidx)) in enumerate(offs):
                        j = (i + d) % nb
                        sb_ = n % 3
                        bi = None
                        if kind == "sp":
                            bi = spn % 3
                            spn += 1

                        def front(i=i, j=j, sb_=sb_, kind=kind, bidx=bidx, bi=bi):
                            for c4 in range(4):
                                hd = HS[c4]
                                mm(PS[sb_][:, 128 * c4:128 * (c4 + 1)], ka_sb[:, hd // 2, 128 * j:128 * (j + 1)],
                                   qz[:, hd, 128 * i:128 * (i + 1)], True, True, ["ka_sb", "qz"], ["ps%d" % sb_])
                            if kind == "sp":
                                simple_dma(bsp[bi], nab_sp[l, bidx], "bsp%d" % bi, writes=["bsp%d" % bi])
                                bias_ap, bias_slot = bsp[bi], "bsp%d" % bi
                            else:
                                bias_ap, bias_slot = bint[:, bidx, :], "bint"
                            stt("dve", sbias[sb_], PS[sb_], 0.125, bias_ap, ALU.mult, ALU.add, ["ps%d" % sb_, bias_slot], ["sbias%d" % sb_])
                            act(pt[sb_], sbias[sb_], AF.Exp, ["sbias%d" % sb_], ["pt%d" % sb_])

                        def back(j=j, sb_=sb_, qi4=qi4, first=(oi == 0), last=(oi == len(offs) - 1)):
                            for c4 in range(4):
                                hd = HS[c4]
                                mm(PS[4 + c4][0:65, 128 * qi4:128 * (qi4 + 1)], va_sb[:, j, 65 * hd:65 * (hd + 1)],
                                   pt[sb_][:, 128 * c4:128 * (c4 + 1)], first, last, ["va_sb", "pt%d" % sb_], ["ps%d" % (4 + c4)])

                        add(n, front)
                        add(n + LOOK + 0.5, back)
                        n += 1
                tq = t0 + 512 * g
                for c4 in range(4):
                    eb = ebufs[nacc % 2]
                    etag = "e%d" % (nacc % 2)
                    nacc += 1
                    p1, p2 = attn_epilogue(4 + c4, 3, eb, 64 * HS[c4], tq, 512, [tq // 512], etag)
                    add(n - 1 + LOOK + 0.6 + 0.01 * c4, p1)
                    add(n - 1 + LOOK + 0.6 + 0.01 * c4 + 0.005, p2)
            sp_base += len(specials)
        emit_sorted(items)

    def phase_C1(l):
        P.barrier()
        A.off = GLOBAL_OFF
        x_src = x_in if l == 0 else x1
        wo_sb = A.alloc([128, 8, D], BF16)
        gsm_sb = A.alloc([128, 17], F32)
        ot = [A.alloc([128, 8, 512], F32) for _ in range(2)]
        sqb = [A.alloc([128, 512], BF16) for _ in range(3)]
        rs = [A.alloc([128, 512], F32) for _ in range(3)]
        mT = [A.alloc([128, 8, 512], BF16) for _ in range(2)]
        xa = [A.alloc([128, D], F32) for _ in range(4)]
        dma(lambda e, s: e.dma_start(out=wo_sb, in_=wout[l].rearrange("(k p) n -> p k n", p=128)).then_inc(s, 16),
            "w_o", 1, writes=["wo_sb"], q="pool")
        simple_dma(gsm_sb, gsm[l], "g0", writes=["gsm_sb"])
        chunks = [c for c in range(NCH) if chunk_active(c, l)]
        groups = ((0, 2, 256), (2, 5, 384), (5, 8, 384))

        def load(ci):
            c = chunks[ci]
            i = ci % 2
            simple_dma(ot[i], oT[:, 512 * c:512 * (c + 1)].rearrange("(j p) t -> p j t", p=128), "ot%d" % i,
                       reads=["D:oT%d:%d" % (r, c) for r in range(0, D, 64)], writes=["ot%d" % i])

        def norm(ci):
            i = ci % 2
            for gi, (j0, j1, nf) in enumerate(groups):
                for j in range(j0, j1):
                    q = j % 3
                    act(sqb[q], ot[i][:, j, :], AF.Square, ["ot%d" % i], ["sqb%d" % q])
                    mm(PS[gi], ones_bf, sqb[q], j == j0, j == j1 - 1, ["sqb%d" % q, "ones_bf"], ["ps%d" % gi])
                rsqrt(rs[gi], nf, PS[gi], ["ps%d" % gi], "rs%d" % gi)
                for j in range(j0, j1):
                    stt("dve", mT[i][:, j, :], ot[i][:, j, :], gsm_sb[:, 9 + j:10 + j], rs[gi], ALU.mult, ALU.mult,
                        ["ot%d" % i, "gsm_sb", "rs%d" % gi], ["mT%d" % i])

        load(0)
        if len(chunks) > 1:
            load(1)
        norm(0)
        xn = [0]
        for ci, c in enumerate(chunks):
            i = ci % 2
            if ci + 1 < len(chunks):
                norm(ci + 1)
            if ci + 2 < len(chunks):
                load(ci + 2)
            for b4 in range(4):
                xi = xn[0] % 4
                xn[0] += 1
                r0 = 512 * c + 128 * b4
                simple_dma(xa[xi], x_src[r0:r0 + 128, :], "xa%d" % xi, reads=["D:x%d:%d" % (l, c)], writes=["xa%d" % xi])
                for hf in range(2):
                    b = 4 + (2 * b4 + hf) % 4
                    for j in range(8):
                        mm(PS[b], mT[i][:, j, 128 * b4:128 * (b4 + 1)], wo_sb[:, j, 512 * hf:512 * (hf + 1)], j == 0, j == 7,
                           ["mT%d" % i, "wo_sb"], ["ps%d" % b])
                    tt("dve", xa[xi][:, 512 * hf:512 * (hf + 1)], xa[xi][:, 512 * hf:512 * (hf + 1)], PS[b], ALU.add,
                       ["xa%d" % xi, "ps%d" % b], ["xa%d" % xi])
                simple_dma(xmid[r0:r0 + 128, :], xa[xi], "xa%d" % xi, reads=["xa%d" % xi], writes=["D:xmid:%d" % c])

    def phase_C2(l):
        P.barrier()
        A.off = GLOBAL_OFF
        wg_sb = A.alloc([128, 8, DFF], BF16)
        wu_sb = A.alloc([128, 8, DFF], BF16)
        wd_sb = A.alloc([128, NFF, D], BF16)
        gF = A.alloc([128, D], F32)
        gL = A.alloc([128, D], F32)
        xm = [A.alloc([128, D], F32) for _ in range(4)]
        junk = A.alloc([128, D], BF16)
        ssv = A.alloc([128, 8], F32)
        hb = [A.alloc([128, D], BF16) for _ in range(2)]
        hT = A.alloc([128, 8, 512], BF16)
        aT = A.alloc([128, NFF, 512], BF16)
        sg = [A.alloc([128, 512], F32) for _ in range(2)]
        for k in range(8):
            dma(lambda e, s, k=k: e.dma_start(out=wg_sb[:, k, :], in_=wg[l, 128 * k:128 * (k + 1), :]).then_inc(s, 16),
                "w_g%d" % k, 1, writes=["wg_sb"], q="pool")
            dma(lambda e, s, k=k: e.dma_start(out=wu_sb[:, k, :], in_=wu[l, 128 * k:128 * (k + 1), :]).then_inc(s, 16),
                "w_u%d" % k, 1, writes=["wu_sb"], q="pool")
        for f0 in range(0, NFF, 4):
            f1 = min(NFF, f0 + 4)
            dma(lambda e, s, f0=f0, f1=f1: e.dma_start(out=wd_sb[:, f0:f1, :], in_=wd[l, 128 * f0:128 * f1, :].rearrange("(k p) n -> p k n", p=128)).then_inc(s, 16),
                "w_d%d" % f0, 1, writes=["wd_sb"], q="pool")
        simple_dma(gF, gbc[l, 1].partition_broadcast(128), "g1", writes=["gF"])
        if l == L - 1:
            simple_dma(gL, gfin.partition_broadcast(128), "g2", writes=["gL"])
        chunks = [c for c in range(NCH) if chunk_active(c, l)]

        def load_block(ci, b4):
            c = chunks[ci]
            r0 = 512 * c + 128 * b4
            simple_dma(xm[b4], xmid[r0:r0 + 128, :], "xm%d" % b4, reads=["D:xmid:%d" % c], writes=["xm%d" % b4])

        def front_block(b4):
            j = b4 % 2
            act(junk, xm[b4], AF.Square, ["xm%d" % b4], ["junk", "ssv"], accum_out=ssv[:, b4:b4 + 1])
            rsqrt(ssv[:, b4:b4 + 1], D, ssv[:, b4:b4 + 1], ["ssv"], "ssv")
            stt("dve", hb[j], xm[b4], ssv[:, b4:b4 + 1], gF, ALU.mult, ALU.mult, ["xm%d" % b4, "ssv", "gF"], ["hb%d" % j])
            for k in range(8):
                P.op("pe", lambda e, k=k, j=j: e.transpose(out=PSB[j][:, 128 * k:128 * (k + 1)], in_=hb[j][:, 128 * k:128 * (k + 1)], identity=ident),
                     ["hb%d" % j, "ident"], ["ps%d" % j])
            vcopy("act" if b4 % 2 else "dve", hT[:, :, 128 * b4:128 * (b4 + 1)],
                  PSB[j].rearrange("p (k t) -> p k t", t=128), ["ps%d" % j], ["hT"])

        for b4 in range(4):
            load_block(0, b4)
        for b4 in range(4):
            front_block(b4)
        for ci, c in enumerate(chunks):
            nxt = ci + 1 < len(chunks)
            for f in range(NFF):
                bg = 2 + f % 2
                bu = 4 + f % 2
                for k in range(8):
                    mm(PS[bg], wg_sb[:, k, 128 * f:128 * (f + 1)], hT[:, k, :], k == 0, k == 7, ["wg_sb", "hT"], ["ps%d" % bg])
                for k in range(8):
                    mm(PS[bu], wu_sb[:, k, 128 * f:128 * (f + 1)], hT[:, k, :], k == 0, k == 7, ["wu_sb", "hT"], ["ps%d" % bu])
                q = f % 2
                act(sg[q], PS[bg], AF.Silu, ["ps%d" % bg], ["sg%d" % q])
                tt("dve", aT[:, f, :], sg[q], PS[bu], ALU.mult, ["sg%d" % q, "ps%d" % bu], ["aT"])
            for b4 in range(4):
                r0 = 512 * c + 128 * b4
                for hf in range(2):
                    b = 6 + hf
                    for f in range(NFF):
                        mm(PS[b], aT[:, f, 128 * b4:128 * (b4 + 1)], wd_sb[:, f, 512 * hf:512 * (hf + 1)], f == 0, f == NFF - 1,
                           ["aT", "wd_sb"], ["ps%d" % b])
                    tt("dve", xm[b4][:, 512 * hf:512 * (hf + 1)], xm[b4][:, 512 * hf:512 * (hf + 1)], PS[b], ALU.add,
                       ["xm%d" % b4, "ps%d" % b], ["xm%d" % b4])
                if l < L - 1:
                    simple_dma(x1[r0:r0 + 128, :], xm[b4], "xm%d" % b4, reads=["xm%d" % b4], writes=["D:x%d:%d" % (l + 1, c)])
                else:
                    act(junk, xm[b4], AF.Square, ["xm%d" % b4], ["junk", "ssv"], accum_out=ssv[:, 4 + b4:5 + b4])
                    rsqrt(ssv[:, 4 + b4:5 + b4], D, ssv[:, 4 + b4:5 + b4], ["ssv"], "ssv")
                    stt("dve", xm[b4], xm[b4], ssv[:, 4 + b4:5 + b4], gL, ALU.mult, ALU.mult, ["xm%d" % b4, "ssv", "gL"], ["xm%d" % b4])
                    ro = out_row0(c) + 128 * b4
                    simple_dma(y_out[ro:ro + 128, :], xm[b4], "xm%d" % b4, reads=["xm%d" % b4], writes=["D:y:%d" % c])
                if nxt:
                    load_block(ci + 1, b4)
                    if b4 >= 1:
                        front_block(b4 - 1)
            if nxt:
                front_block(3)

    phases = []
    for l in range(L):
        phases += [("A", l), ("Bna", l), ("Bd", l), ("C1", l), ("C2", l)]
    for name, l in phases:
        {"A": phase_A, "Bna": phase_B_na, "Bd": phase_B_dense, "C1": phase_C1, "C2": phase_C2}[name](l)
        if stop_after == (name, l):
            break
    P.final_wait("sp")
    cc = P.emit(nc, st)
    st.close()
    return nc, (len(P.ops), {k: len(v) for k, v in P.streams.items()}, cc)


def rope_perm(d):
    half = d // 2
    return np.concatenate([np.arange(half, d), np.arange(0, half)])


def rope_tables(pos, d):
    half = d // 2
    inv = (10000.0 ** (-(np.arange(half, dtype=np.float32) * 2.0) / d)).astype(np.float32)
    ang = pos.astype(np.float32)[None, :] * inv[:, None]
    cos = np.cos(ang).astype(np.float32)
    sin = np.sin(ang).astype(np.float32)
    return np.concatenate([cos, cos], 0), np.concatenate([-sin, sin], 0)


def na_bias_tile(rpb_l, rows, tq, tk):
    out = np.full((128, 4, 128), NEG, np.float32)
    nb = rows // 2
    if tk is None or tk < 0 or tk >= nb:
        return out.reshape(128, 512)
    qi = np.arange(128)
    r = 2 * tq + qi // 64
    cq = qi % 64
    ki = np.arange(128)
    kr = 2 * tk + ki // 64
    ck = ki % 64
    kr_n = min(8, rows)
    rs = np.clip(r - kr_n // 2, 0, rows - kr_n)
    cs = np.clip(cq - 8, 0, GRID_W - 16)
    valid = ((kr[:, None] >= rs[None, :]) & (kr[:, None] < rs[None, :] + kr_n)
             & (ck[:, None] >= cs[None, :]) & (ck[:, None] < cs[None, :] + 16))
    dr = np.clip(kr[:, None] - r[None, :] + 7, 0, 14)
    dc = np.clip(ck[:, None] - cq[None, :] + 15, 0, 30)
    vals = rpb_l[:, dr, dc]
    out = np.where(valid[:, None, :], vals.transpose(1, 0, 2), np.float32(NEG)).astype(np.float32)
    return np.ascontiguousarray(out[:, list(HS), :]).reshape(128, 512)


def prep_weights(inp):
    f = lambda a: np.asarray(a, np.float32)
    w_in = f(inp["w_in"])
    Ls = w_in.shape[0]
    segs = np.cumsum([0, 256, 256, 256, 384, 256, 32, 384, 128, 128])
    o_qa, o_ka, o_va, o_cq, o_ckv, o_kpe, o_qc, o_kc, o_vc = segs[:9]
    p32 = rope_perm(32)
    p_ax = np.concatenate([rope_perm(32), 32 + rope_perm(32)])
    win = np.zeros((Ls, D, NIN), np.float32)
    win[:, :, C_QA:C_QA + 256] = w_in[:, :, o_qa:o_qa + 256]
    win[:, :, C_KA:C_KA + 256] = w_in[:, :, o_ka:o_ka + 256]
    win[:, :, C_CQ:C_CQ + 384] = w_in[:, :, o_cq:o_cq + 384]
    win[:, :, C_CKV:C_CKV + 256] = w_in[:, :, o_ckv:o_ckv + 256]
    win[:, :, C_KPA + 64:C_KPA + 96] = w_in[:, :, o_kpe:o_kpe + 32]
    win[:, :, C_KPB + 64:C_KPB + 96] = w_in[:, :, o_kpe + p32]
    qc_sw = np.concatenate([o_qc + 64 * h + p_ax for h in range(6)])
    kc_sw = np.concatenate([o_kc + 64 * h + p_ax for h in range(2)])
    win[:, :, C_QCA:C_QCA + 384] = w_in[:, :, o_qc:o_qc + 384]
    win[:, :, C_QCB:C_QCB + 384] = w_in[:, :, qc_sw]
    win[:, :, C_KCA:C_KCA + 128] = w_in[:, :, o_kc:o_kc + 128]
    win[:, :, C_KCB:C_KCB + 128] = w_in[:, :, kc_sw]
    win[:, :, C_V:C_V + 256] = w_in[:, :, o_va:o_va + 256]
    win[:, :, C_V + 256:C_V + 384] = w_in[:, :, o_vc:o_vc + 128]
    w_uq = f(inp["w_uq"])
    wuq = np.zeros((Ls, 384, 1152), np.float32)
    for h in range(6):
        wuq[:, :, 192 * h:192 * h + 96] = w_uq[:, :, 96 * h:96 * h + 96]
        wuq[:, :, 192 * h + 96:192 * h + 160] = w_uq[:, :, 96 * h:96 * h + 64]
        wuq[:, :, 192 * h + 160:192 * h + 192] = w_uq[:, :, 96 * h + 64 + p32]
    w_ukv = f(inp["w_ukv"])
    wukv = np.zeros((Ls, 256, 768), np.float32)
    for h in range(6):
        wukv[:, :, 64 * h:64 * h + 64] = w_ukv[:, :, 128 * h:128 * h + 64]
        wukv[:, :, 384 + 64 * h:384 + 64 * h + 64] = w_ukv[:, :, 128 * h + 64:128 * h + 128]
    gsm = np.zeros((Ls, 128, 17), np.float32)
    qan = f(inp["q_a_norm"])
    kvn = f(inp["kv_a_norm"])
    qn = f(inp["q_norm_c"])
    kn = f(inp["k_norm_c"])
    gout = np.concatenate([f(inp["g_out_a"]), f(inp["g_out_b"]), f(inp["g_out_c"])], 1)
    for l in range(Ls):
        gsm[l, :, 0:3] = qan[l].reshape(3, 128).T
        gsm[l, :, 3:5] = kvn[l].reshape(2, 128).T
        gsm[l, :, 5] = np.tile(qn[l], 2)
        gsm[l, :, 6] = np.tile(qn[l][p_ax], 2)
        gsm[l, :, 7] = np.tile(kn[l], 2)
        gsm[l, :, 8] = np.tile(kn[l][p_ax], 2)
        gsm[l, :, 9:17] = gout[l].reshape(8, 128).T
    gbc = np.stack([f(inp["attn_norm"]), f(inp["ffn_norm"])], 1)
    bd = np.zeros((128, 128), np.float32)
    bd[:64, :64] = 1
    bd[64:, 64:] = 1
    return dict(win=win, wuq=wuq, wukv=wukv, wout=f(inp["w_out"]), wg=f(inp["w_gate"]), wu=f(inp["w_up"]),
                wd=f(inp["w_down"]), gsm=gsm, gbc=np.ascontiguousarray(gbc), gfin=f(inp["final_norm"]),
                ident=np.eye(128, dtype=np.float32).astype(ml_dtypes.bfloat16), bdones=bd.astype(ml_dtypes.bfloat16))


def prep_core_tables(rpb, slots, true_pos):
    pos = np.concatenate(true_pos)
    row = pos // GRID_W
    col = pos % GRID_W
    cr, sr = rope_tables(row, 32)
    cc, sc = rope_tables(col, 32)
    cosC = np.concatenate([cr, cc], 0)
    sinC = np.concatenate([sr, sc], 0)
    ropeC = np.stack([np.tile(cosC, (2, 1)), np.tile(sinC, (2, 1))], 0).astype(np.float32)
    cb, sb = rope_tables(pos, 32)
    ropeB = np.stack([cb, sb], 0).astype(np.float32)
    Ls = rpb.shape[0]
    nab_int = np.zeros((Ls, 128, 5, 512), np.float32)
    sp_tiles = [[] for _ in range(Ls)]
    for l in range(Ls):
        for di, d in enumerate((-2, -1, 0, 1, 2)):
            nab_int[l, :, di, :] = na_bias_tile(rpb[l], 64, 8, 8 + d)
        for si, (S, nact) in enumerate(slots):
            nb = S // 128
            rows = S // GRID_W
            tb = true_pos[si][::128] // 128
            for i in na_specials(S, nact):
                for d in NA_OFFS:
                    tq = int(tb[i])
                    tk = tq + d
                    j = (i + d) % nb
                    if 0 <= tk < nb:
                        assert int(tb[j]) == tk
                    sp_tiles[l].append(na_bias_tile(rpb[l], rows, tq, tk))
    nab_sp = np.stack([np.stack(t, 0) for t in sp_tiles], 0)
    return ropeC, ropeB, nab_int, nab_sp


_CACHE = {}


def run_slots(slots, per_core_x, per_core_pos, inp, debug=False, stop_after=None, trace=False):
    key = (tuple(slots), debug, stop_after)
    if key not in _CACHE:
        _CACHE[key] = build_program(slots, debug=debug, stop_after=stop_after)
    nc, _ = _CACHE[key]
    W = prep_weights(inp)
    rpb = np.asarray(inp["rpb"], np.float32)
    in_maps = []
    for x, pos in zip(per_core_x, per_core_pos):
        ropeC, ropeB, nab_int, nab_sp = prep_core_tables(rpb, slots, pos)
        m = dict(W)
        m.update(x_in=np.ascontiguousarray(x, dtype=np.float32), ropeC=ropeC, ropeB=ropeB, nab_int=nab_int, nab_sp=nab_sp)
        in_maps.append(m)
    res = run_bass_kernel_spmd(nc, in_maps, core_ids=list(range(len(in_maps))), **({"trace": True} if trace else {}))
    return res


def kernel(**inp):
    xp = np.asarray(inp["x_prompt"], np.float32)
    xs = np.asarray(inp["x_sample"], np.float32)
    B, S, _ = xp.shape
    Bs, Ss, _ = xs.shape
    assert Bs == 2 * 8 and B * 2 == 8
    H = S // 2
    slots = [(Ss, Ss), (Ss, Ss), (S, H)]
    per_x, per_pos = [], []
    for c in range(8):
        p, h = c // 2, c % 2
        xl = np.concatenate([xs[2 * c], xs[2 * c + 1], np.roll(xp[p], -h * H, axis=0)], 0)
        per_x.append(xl)
        per_pos.append([np.arange(Ss), np.arange(Ss), (np.arange(S) + h * H) % S])
    res = run_slots(slots, per_x, per_pos, inp)
    yp = np.zeros_like(xp)
    ys = np.zeros_like(xs)
    for c in range(8):
        y = res.results[c]["y_out"]
        p, h = c // 2, c % 2
        ys[2 * c] = y[0:Ss]
        ys[2 * c + 1] = y[Ss:2 * Ss]
        yp[p, h * H:(h + 1) * H] = y[2 * Ss:2 * Ss + H]
    return yp, ys
```

```python
import contextlib
import math

import ml_dtypes
import numpy as np

import concourse.bass as bass
import concourse.mybir as mybir
from concourse.bass_utils import run_bass_kernel_spmd

F32 = mybir.dt.float32
BF16 = mybir.dt.bfloat16
ALU = mybir.AluOpType
AF = mybir.ActivationFunctionType
AX = mybir.AxisListType

D = 1024
L = 2
GRID_W = 64
DFF = 2816
NFF = DFF // 128
EPS = 1e-6
NEG = -30000.0
NIN = 2752
C_QA, C_KA, C_CQ, C_CKV, C_KPA, C_KPB, C_QCA, C_QCB, C_KCA, C_KCB, C_V = (
    0, 256, 512, 896, 1152, 1248, 1344, 1728, 2112, 2240, 2368)
NA_OFFS = (-3, -2, -1, 0, 1, 2, 3)
HS = (0, 2, 1, 3)

SAME_ENGINE_SYNC = {"pe": False, "act": True, "dve": True, "pool": True, "sp": False}


class Op:
    __slots__ = ("stream", "fn", "deps", "flag", "count", "semkey", "nparts", "waits")

    def __init__(self, stream, fn, semkey=None, nparts=0):
        self.stream = stream
        self.fn = fn
        self.deps = ()
        self.flag = False
        self.count = None
        self.semkey = semkey
        self.nparts = nparts
        self.waits = None


class Prog:
    def __init__(self):
        self.ops = []
        self.slots = {}
        self.streams = {s: [] for s in ("pe", "act", "dve", "pool", "sp")}
        self.last = {}

    def _slot(self, name):
        s = self.slots.get(name)
        if s is None:
            s = [None, {}]
            self.slots[name] = s
        return s

    def op(self, stream, fn, reads=(), writes=(), semkey=None, nparts=0, extra_deps=()):
        o = Op(stream, fn, semkey, nparts)
        deps = set(extra_deps)
        okey = semkey if semkey is not None else stream
        for r in reads:
            s = self._slot(r)
            if s[0] is not None:
                deps.add(s[0])
            if r.startswith("ps"):
                for k, rd in s[1].items():
                    if k != okey:
                        deps.add(rd)
        for w in writes:
            s = self._slot(w)
            if s[0] is not None:
                deps.add(s[0])
            for rd in s[1].values():
                deps.add(rd)
        for r in reads:
            self._slot(r)[1][okey] = o
        for w in writes:
            s = self._slot(w)
            s[0] = o
            s[1] = {}
        deps.discard(o)
        o.deps = tuple(deps)
        self.ops.append(o)
        self.streams[stream].append(o)
        if fn is not None:
            self.last[okey] = o
        return o

    def dma(self, stream, fn, semkey, nparts, reads=(), writes=()):
        return self.op(stream, fn, reads, writes, semkey=semkey, nparts=nparts)

    def barrier(self):
        lasts = list(self.last.values())
        for s in self.streams:
            self.op(s, None, extra_deps=lasts)
        self.slots = {k: v for k, v in self.slots.items() if k.startswith("D:")}

    def resolve(self):
        known = {s: {} for s in self.streams}
        seq = {}
        cnt = {}
        for o in self.ops:
            key = o.semkey if o.semkey is not None else o.stream
            cnt[key] = cnt.get(key, 0) + 1
            seq[o] = (key, cnt[key])
        for o in self.ops:
            need = {}
            kn = known[o.stream]
            for d in o.deps:
                key, n = seq[d]
                if d.semkey is None and d.stream == o.stream and not SAME_ENGINE_SYNC[o.stream]:
                    continue
                if kn.get(key, 0) >= n:
                    continue
                if key not in need or seq[need[key]][1] < n:
                    need[key] = d
            o.waits = list(need.values())
            for key, d in need.items():
                kn[key] = seq[d][1]
                d.flag = True
            o.deps = ()
        ccount = {}
        for o in self.ops:
            if o.semkey is not None:
                ccount[o.semkey] = ccount.get(o.semkey, 0) + 16 * o.nparts
                o.count = ccount[o.semkey]
            elif o.flag:
                ccount[o.stream] = ccount.get(o.stream, 0) + 1
                o.count = ccount[o.stream]
        return ccount

    def emit(self, nc, stack):
        ccount = self.resolve()
        sems = {}
        for i, key in enumerate(ccount):
            sems[key] = stack.enter_context(nc.semaphore("s%d" % i))
        engmap = {"pe": "tensor", "act": "scalar", "dve": "vector", "pool": "gpsimd", "sp": "sync"}
        block = stack.enter_context(nc.Block())
        for sname, ops in self.streams.items():
            if not ops:
                continue

            def body(e, ops=ops, sname=sname):
                for o in ops:
                    for d in o.waits:
                        key = d.semkey if d.semkey is not None else d.stream
                        e.wait_ge(sems[key], d.count)
                    if o.fn is None:
                        continue
                    if o.semkey is not None:
                        o.fn(e, sems[o.semkey])
                    else:
                        ins = o.fn(e)
                        if o.flag:
                            ins.then_inc(sems[sname], 1)

            getattr(block, engmap[sname])(body)
        return ccount

    def final_wait(self, stream="sp"):
        self.op(stream, None, extra_deps=list(self.last.values()))


class Arena:
    def __init__(self, base_ap, nwords):
        self.base = base_ap
        self.nwords = nwords
        self.off = 0

    def reset(self):
        self.off = 0

    def alloc(self, shape, dt):
        n = 1
        for s in shape[1:]:
            n *= s
        nb = n * (2 if dt == BF16 else 4)
        n4 = (nb + 3) // 4
        n4 = (n4 + 7) // 8 * 8
        assert self.off + n4 <= self.nwords, ("SBUF arena overflow", self.off + n4, self.nwords)
        a = self.base[:, self.off:self.off + n4]
        self.off += n4
        if dt == BF16:
            a = a.bitcast(BF16)
        a = a[:, 0:n]
        if len(shape) == 3:
            a = a.rearrange("p (a b) -> p a b", b=shape[2])
        elif len(shape) == 4:
            a = a.rearrange("p (a b c) -> p a b c", b=shape[2], c=shape[3])
        return a


def na_specials(S, nact):
    nb = S // 128
    half = nb // 2
    if nact == S:
        sp = {0, 1, nb - 2, nb - 1}
    else:
        sp = {0, 1, nb - 2, nb - 1, half - 2, half - 1, half, half + 1}
    return sorted(x for x in sp if 0 <= x < nb)


def build_program(slots, debug=False, stop_after=None):
    nc = bass.Bass("TRN2", target_bir_lowering=False)
    T = sum(S for S, _ in slots)
    TA = sum(a for _, a in slots)
    NCH = T // 512
    slot_tok0 = np.cumsum([0] + [S for S, _ in slots]).tolist()
    slot_out0 = np.cumsum([0] + [a for _, a in slots]).tolist()
    nsp_tot = sum(len(na_specials(S, a)) for S, a in slots)

    def din(name, shape, dt=F32):
        return nc.dram_tensor(name, list(shape), dt, kind="ExternalInput").ap()

    skind = "ExternalOutput" if debug else "Internal"

    def dscr(name, shape, dt):
        return nc.dram_tensor(name, list(shape), dt, kind=skind).ap()

    x_in = din("x_in", [T, D])
    y_out = nc.dram_tensor("y_out", [TA, D], F32, kind="ExternalOutput").ap()
    win = din("win", [L, D, NIN])
    wuq = din("wuq", [L, 384, 1152])
    wukv = din("wukv", [L, 256, 768])
    wout = din("wout", [L, D, D])
    wg = din("wg", [L, D, DFF])
    wu = din("wu", [L, D, DFF])
    wd = din("wd", [L, DFF, D])
    gsm = din("gsm", [L, 128, 17])
    gbc = din("gbc", [L, 2, D])
    gfin = din("gfin", [D])
    ropeC = din("ropeC", [2, 128, T])
    ropeB = din("ropeB", [2, 32, T])
    nab_int = din("nab_int", [L, 128, 5, 512])
    nab_sp = din("nab_sp", [L, nsp_tot * 7, 128, 512])
    ident_d = din("ident", [128, 128], BF16)
    bdones_d = din("bdones", [128, 128], BF16)

    x1 = dscr("x1", [T, D], F32)
    xmid = dscr("xmid", [T, D], F32)
    qaT = dscr("qaT", [256, T], BF16)
    kaT = dscr("kaT", [256, T], BF16)
    va = dscr("va", [T, 4, 65], BF16)
    qbT = dscr("qbT", [6, 96, T], BF16)
    kbT = dscr("kbT", [6, 96, T], BF16)
    vb = dscr("vb", [T, 6, 65], BF16)
    qcT = dscr("qcT", [384, T], BF16)
    kcT = dscr("kcT", [128, T], BF16)
    vc = dscr("vc", [T, 2, 65], BF16)
    oT = dscr("oT", [D, T], F32)

    P = Prog()
    st = contextlib.ExitStack()
    AW = 51200
    arena_t = st.enter_context(nc.sbuf_tensor("arena", [128, AW], F32))
    A = Arena(arena_t[:], AW)
    psall_t = st.enter_context(nc.psum_tensor("psall", [128, 4096], F32))
    psall = psall_t[:]
    PS = [psall[:, 512 * i:512 * (i + 1)] for i in range(8)]
    PSB = [psall[:, 512 * i:512 * (i + 1)].bitcast(BF16) for i in range(8)]

    dq = ["sp"]

    def dma(fn, key, n, reads=(), writes=(), q=None):
        return P.dma(q or "sp", fn, "q_" + key, n, reads, writes)

    def simple_dma(out, in_, key, reads=(), writes=(), q=None):
        return dma(lambda e, s: e.dma_start(out=out, in_=in_).then_inc(s, 16), key, 1, reads, writes, q)

    def mm(out, lhsT, rhs, start, stop, reads, writes):
        return P.op("pe", lambda e: e.matmul(out, lhsT=lhsT, rhs=rhs, start=start, stop=stop), reads, writes)

    def act(out, in_, func, reads, writes, **kw):
        return P.op("act", lambda e: e.activation(out=out, in_=in_, func=func, **kw), reads, writes)

    def vcopy(eng, out, in_, reads, writes):
        if eng == "act":
            return act(out, in_, AF.Copy, reads, writes)
        return P.op(eng, lambda e: e.tensor_copy(out=out, in_=in_), reads, writes)

    def tt(eng, out, in0, in1, op, reads, writes):
        return P.op(eng, lambda e: e.tensor_tensor(out=out, in0=in0, in1=in1, op=op), reads, writes)

    def stt(eng, out, in0, scalar, in1, op0, op1, reads, writes):
        return P.op(eng, lambda e: e.scalar_tensor_tensor(out=out, in0=in0, scalar=scalar, in1=in1, op0=op0, op1=op1),
                    reads, writes)

    consts = A.alloc([128, 8], F32)
    ident = A.alloc([128, 128], BF16)
    bdones = A.alloc([128, 128], BF16)
    ones_bf = A.alloc([128, 128], BF16)
    sel_b = A.alloc([128, 64], BF16)
    gfin_b = None
    GLOBAL_OFF = None

    P.op("dve", lambda e: e.memset(consts[:, 0:1], EPS), writes=["consts"])
    P.op("dve", lambda e: e.memset(consts[:, 1:2], 1.0), writes=["consts"])
    P.op("dve", lambda e: e.memset(ones_bf, 1.0), writes=["ones_bf"])
    P.op("dve", lambda e: e.memset(sel_b, 0.0), writes=["sel_b"])
    P.op("dve", lambda e: e.memset(sel_b[64:65, :], 1.0), writes=["sel_b"])
    simple_dma(ident, ident_d, "c0", writes=["ident"])
    simple_dma(bdones, bdones_d, "c1", writes=["bdones"])
    GLOBAL_OFF = A.off

    def eps_col(ap):
        b = ap.base_partition()
        return consts[b:b + ap.shape[0], 0:1]

    def rsqrt(ap, n, src, reads, slot):
        act(ap, src, AF.Sqrt, list(reads) + ["consts"], [slot], scale=1.0 / n, bias=eps_col(ap))
        P.op("dve", lambda e: e.reciprocal(out=ap, in_=ap), [slot], [slot])

    def chunk_slot(c):
        t = 512 * c
        for i in range(len(slots)):
            if slot_tok0[i] <= t < slot_tok0[i + 1]:
                return i
        raise AssertionError

    def chunk_active(c, l):
        if l == 0:
            return True
        i = chunk_slot(c)
        return 512 * c - slot_tok0[i] < slots[i][1]

    def out_row0(c):
        i = chunk_slot(c)
        return slot_out0[i] + 512 * c - slot_tok0[i]

    def phase_A(l):
        P.barrier()
        A.off = GLOBAL_OFF
        x_src = x_in if l == 0 else x1
        win_sb = A.alloc([128, 8, NIN], BF16)
        wuq_sb = A.alloc([128, 3, 1152], BF16)
        wukv_sb = A.alloc([128, 2, 768], BF16)
        gsm_sb = A.alloc([128, 17], F32)
        gA = A.alloc([128, D], F32)
        xa = [A.alloc([128, 4, D], F32) for _ in range(2)]
        junk = A.alloc([128, D], F32)
        ss = [A.alloc([128, 4], F32) for _ in range(2)]
        hb = [A.alloc([128, D], BF16) for _ in range(2)]
        hT = [A.alloc([128, 8, 512], BF16) for _ in range(2)]
        rC = [A.alloc([128, 2, 512], F32) for _ in range(2)]
        rB = [A.alloc([128, 2, 512], F32) for _ in range(2)]
        NSTG = 8
        stg = [A.alloc([128, 512], BF16) for _ in range(NSTG)]
        cqf = A.alloc([128, 5, 512], F32)
        sqb = [A.alloc([128, 512], BF16) for _ in range(2)]
        cn = A.alloc([128, 5, 512], BF16)
        rs = [A.alloc([128, 512], F32) for _ in range(2)]
        tA = [A.alloc([128, 512], F32) for _ in range(2)]
        tB = [A.alloc([128, 512], F32) for _ in range(2)]
        vst = A.alloc([128, 4, 6, 65], BF16)
        vsb = A.alloc([128, 4, 6, 65], BF16)
        kpe_s = A.alloc([128, 512], BF16)

        for k in range(8):
            dma(lambda e, s, k=k: e.dma_start(out=win_sb[:, k, :], in_=win[l, 128 * k:128 * (k + 1), :]).then_inc(s, 16),
                "w_in%d" % k, 1, writes=["win_sb"], q="pool")
        dma(lambda e, s: e.dma_start(out=wuq_sb, in_=wuq[l].rearrange("(k p) n -> p k n", p=128)).then_inc(s, 16),
            "w_uq", 1, writes=["wuq_sb"], q="pool")
        dma(lambda e, s: e.dma_start(out=wukv_sb, in_=wukv[l].rearrange("(k p) n -> p k n", p=128)).then_inc(s, 16),
            "w_ukv", 1, writes=["wukv_sb"], q="pool")
        simple_dma(gsm_sb, gsm[l], "g0", writes=["gsm_sb"])
        simple_dma(gA, gbc[l, 0].partition_broadcast(128), "g1", writes=["gA"])
        P.op("pool", lambda e: e.memset(vst, 1.0), writes=["vst"])
        P.op("pool", lambda e: e.memset(vsb, 1.0), writes=["vsb"])

        bank_rr = [0]

        def nbank():
            b = 2 + bank_rr[0] % 4
            bank_rr[0] += 1
            return b

        stg_rr = [0]

        def nstg():
            i = stg_rr[0] % NSTG
            stg_rr[0] += 1
            return i

        ev_rr = [0]

        def ev_eng():
            ev_rr[0] += 1
            return "act" if ev_rr[0] % 2 else "dve"

        def x_load(c):
            i = c % 2
            t0 = 512 * c
            dma(lambda e, s: e.dma_start(out=xa[i], in_=x_src[t0:t0 + 512, :].rearrange("(b p) d -> p b d", p=128)).then_inc(s, 16),
                "pxa%d" % i, 1, reads=["D:x%d:%d" % (l, c)], writes=["xa%d" % i], q="pool")

        def rope_load(c):
            i = c % 2
            t0 = 512 * c
            dma(lambda e, s: e.dma_start(out=rC[i], in_=ropeC[:, :, t0:t0 + 512].rearrange("a p t -> p a t")).then_inc(s, 16),
                "prC%d" % i, 1, writes=["rC%d" % i], q="pool")
            dma(lambda e, s: e.dma_start(out=rB[i][64:96], in_=ropeB[:, :, t0:t0 + 512].rearrange("a p t -> p a t")).then_inc(s, 16),
                "prB%d" % i, 1, writes=["rB%d" % i], q="pool")

        def front_load(c):
            i = c % 2
            for b in range(4):
                act(junk, xa[i][:, b, :], AF.Square, ["xa%d" % i], ["junk", "ss%d" % i], accum_out=ss[i][:, b:b + 1])
            rsqrt(ss[i], D, ss[i], ["ss%d" % i], "ss%d" % i)

        def front_block(c, b):
            i = c % 2
            j = b % 2
            stt("dve", hb[j], xa[i][:, b, :], ss[i][:, b:b + 1], gA, ALU.mult, ALU.mult,
                ["xa%d" % i, "ss%d" % i, "gA"], ["hb%d" % j])
            for k in range(8):
                P.op("pe", lambda e, k=k, j=j: e.transpose(out=PSB[j][:, 128 * k:128 * (k + 1)], in_=hb[j][:, 128 * k:128 * (k + 1)], identity=ident),
                     ["hb%d" % j, "ident"], ["ps%d" % j])
            vcopy("act" if b % 2 else "dve", hT[i][:, :, 128 * b:128 * (b + 1)],
                  PSB[j].rearrange("p (k t) -> p k t", t=128), ["ps%d" % j], ["hT%d" % i])

        pending = []

        def hook():
            if pending:
                pending.pop(0)()

        def fm(i, col0, M, wsb=None, nk=8, rhs_fn=None, rslot=None):
            wsb = win_sb if wsb is None else wsb
            b = nbank()
            for k in range(nk):
                rhs = hT[i][:, k, :] if rhs_fn is None else rhs_fn(k)
                mm(PS[b][0:M, :], wsb[:, k, col0:col0 + M], rhs, k == 0, k == nk - 1,
                   (list(rslot) if rslot else ["hT%d" % i]) + ["win_sb", "wuq_sb", "wukv_sb"], ["ps%d" % b])
            return b

        def store_fm(ap_sb, dst, key_i, c, name, npart=128):
            simple_dma(dst, ap_sb, "stg%d" % key_i, reads=["stg%d" % key_i], writes=["D:%s:%d" % (name, c)])

        def back(c):
            i = c % 2
            t0 = 512 * c
            ts = slice(t0, t0 + 512)
            hs = "hT%d" % i
            for name, col0, dst in (("qaT", C_QA, qaT), ("kaT", C_KA, kaT)):
                for j in range(2):
                    b = fm(i, col0 + 128 * j, 128)
                    si = nstg()
                    vcopy(ev_eng(), stg[si], PS[b], ["ps%d" % b], ["stg%d" % si])
                    store_fm(stg[si], dst[128 * j:128 * (j + 1), ts], si, c, name)
            hook()
            if CUT == 3:
                return
            for (col0, nchk, off, gcol, nfeat, nb_) in ((C_CQ, 3, 0, 0, 384, 6), (C_CKV, 2, 3, 3, 256, 7)):
                for j in range(nchk):
                    b = fm(i, col0 + 128 * j, 128)
                    q = j % 2
                    act(sqb[q], PS[b], AF.Square, ["ps%d" % b], ["sqb%d" % q])
                    vcopy("dve", cqf[:, off + j, :], PS[b], ["ps%d" % b], ["cqf%d" % (off + j)])
                    mm(PS[nb_], ones_bf, sqb[q], j == 0, j == nchk - 1, ["sqb%d" % q, "ones_bf"], ["ps%d" % nb_])
                r = (off // 3) % 2
                rsqrt(rs[r], nfeat, PS[nb_], ["ps%d" % nb_], "rs%d" % r)
                for j in range(nchk):
                    stt("dve", cn[:, off + j, :], cqf[:, off + j, :], gsm_sb[:, gcol + j:gcol + j + 1], rs[r], ALU.mult, ALU.mult,
                        ["cqf%d" % (off + j), "gsm_sb", "rs%d" % r], ["cn%d" % (off + j)])
            hook()
            if CUT == 4:
                return
            bA = fm(i, C_KPA, 96)
            bB = fm(i, C_KPB, 96)
            tt("dve", tA[0][64:96], PS[bA][64:96], rB[i][64:96, 0, :], ALU.mult, ["ps%d" % bA, "rB%d" % i], ["tA0"])
            tt("dve", tB[0][64:96], PS[bB][64:96], rB[i][64:96, 1, :], ALU.mult, ["ps%d" % bB, "rB%d" % i], ["tB0"])
            tt("dve", kpe_s[64:96], tA[0][64:96], tB[0][64:96], ALU.add, ["tA0", "tB0"], ["kpe_s"])
            dma(lambda e, s: [e.dma_start(out=kbT[h, 64:96, ts], in_=kpe_s[64:96]).then_inc(s, 16) for h in range(6)],
                "kpe", 6, reads=["kpe_s"], writes=["D:kbTp:%d" % c])
            if CUT == 5:
                return
            for (colA, colB, gc, name, dst, row0) in ([(C_QCA + 128 * j, C_QCB + 128 * j, 5, "qcT", qcT, 128 * j) for j in range(3)]
                                                      + [(C_KCA, C_KCB, 7, "kcT", kcT, 0)]):
                bA = fm(i, colA, 128)
                bB = fm(i, colB, 128)
                q = ev_rr[0] % 2
                ev_rr[0] += 1
                act(sqb[q], PS[bA], AF.Square, ["ps%d" % bA], ["sqb%d" % q])
                act(tA[q], PS[bA], AF.Copy, ["ps%d" % bA, "gsm_sb"], ["tA%d" % q], scale=gsm_sb[:, gc:gc + 1])
                act(tB[q], PS[bB], AF.Copy, ["ps%d" % bB, "gsm_sb"], ["tB%d" % q], scale=gsm_sb[:, gc + 1:gc + 2])
                nb_ = 6 + q
                mm(PS[nb_], bdones, sqb[q], True, True, ["sqb%d" % q, "bdones"], ["ps%d" % nb_])
                rsqrt(rs[q], 64, PS[nb_], ["ps%d" % nb_], "rs%d" % q)
                tt("pool", tB[q], tB[q], rC[i][:, 1, :], ALU.mult, ["tB%d" % q, "rC%d" % i], ["tB%d" % q])
                tt("dve", tA[q], tA[q], rC[i][:, 0, :], ALU.mult, ["tA%d" % q, "rC%d" % i], ["tA%d" % q])
                tt("dve", tA[q], tA[q], tB[q], ALU.add, ["tA%d" % q, "tB%d" % q], ["tA%d" % q])
                si = nstg()
                tt("dve", stg[si], tA[q], rs[q], ALU.mult, ["tA%d" % q, "rs%d" % q], ["stg%d" % si])
                store_fm(stg[si], dst[row0:row0 + 128, ts], si, c, name)
            hook()
            if CUT == 6:
                return
            for h in range(6):
                bA = fm(i, 192 * h, 96, wsb=wuq_sb, nk=3, rhs_fn=lambda k: cn[:, k, :], rslot=("cn0", "cn1", "cn2"))
                bB = fm(i, 192 * h + 96, 96, wsb=wuq_sb, nk=3, rhs_fn=lambda k: cn[:, k, :], rslot=("cn0", "cn1", "cn2"))
                si = nstg()
                q = h % 2
                vcopy("act", stg[si][0:64], PS[bA][0:64], ["ps%d" % bA], ["stg%d" % si])
                tt("dve", tA[q][64:96], PS[bA][64:96], rB[i][64:96, 0, :], ALU.mult, ["ps%d" % bA, "rB%d" % i], ["tA%d" % q])
                tt("dve", tB[q][64:96], PS[bB][64:96], rB[i][64:96, 1, :], ALU.mult, ["ps%d" % bB, "rB%d" % i], ["tB%d" % q])
                tt("dve", stg[si][64:96], tA[q][64:96], tB[q][64:96], ALU.add, ["tA%d" % q, "tB%d" % q], ["stg%d" % si])
                simple_dma(qbT[h, :, ts], stg[si][0:96], "stg%d" % si, reads=["stg%d" % si], writes=["D:qbT:%d" % c])
            hook()
            if CUT == 7:
                return
            for h in range(6):
                b = fm(i, 64 * h, 64, wsb=wukv_sb, nk=2, rhs_fn=lambda k: cn[:, 3 + k, :], rslot=("cn3", "cn4"))
                si = nstg()
                vcopy(ev_eng(), stg[si][0:64], PS[b][0:64], ["ps%d" % b], ["stg%d" % si])
                simple_dma(kbT[h, 0:64, ts], stg[si][0:64], "stg%d" % si, reads=["stg%d" % si], writes=["D:kbTn:%d" % c])
            if CUT == 8:
                return
            for b4 in range(4):
                b = nbank()
                for k in range(8):
                    mm(PS[b][:, 0:384], hT[i][:, k, 128 * b4:128 * (b4 + 1)], win_sb[:, k, C_V:C_V + 384], k == 0, k == 7,
                       [hs, "win_sb"], ["ps%d" % b])
                vcopy(ev_eng(), vst[:, b4, :, 0:64], PS[b][:, 0:384].rearrange("p (h d) -> p h d", d=64), ["ps%d" % b], ["vst"])
                b = nbank()
                for k in range(2):
                    mm(PS[b][:, 0:384], cn[:, 3 + k, 128 * b4:128 * (b4 + 1)], wukv_sb[:, k, 384:768], k == 0, k == 1,
                       ["cn3", "cn4", "wukv_sb"], ["ps%d" % b])
                vcopy(ev_eng(), vsb[:, b4, :, 0:64], PS[b][:, 0:384].rearrange("p (h d) -> p h d", d=64), ["ps%d" % b], ["vsb"])
            simple_dma(va[ts].rearrange("(b p) h d -> p b h d", p=128), vst[:, :, 0:4, :], "vst_a", reads=["vst"], writes=["D:va:%d" % c])
            simple_dma(vc[ts].rearrange("(b p) h d -> p b h d", p=128), vst[:, :, 4:6, :], "vst_c", reads=["vst"], writes=["D:vc:%d" % c])
            simple_dma(vb[ts].rearrange("(b p) h d -> p b h d", p=128), vsb, "vsb", reads=["vsb"], writes=["D:vb:%d" % c])

        CUT = 0
        if CUT == 1:
            return
        x_load(0)
        rope_load(0)
        if NCH > 1:
            x_load(1)
            rope_load(1)
        front_load(0)
        for b_ in range(4):
            front_block(0, b_)
        if CUT == 2:
            return
        for c in range(NCH):
            if c + 2 < NCH:
                x_load(c + 2)
            if c + 1 < NCH:
                front_load(c + 1)
                for b_ in range(4):
                    pending.append(lambda c=c, b_=b_: front_block(c + 1, b_))
            back(c)
            while pending:
                hook()
            if c + 2 < NCH:
                rope_load(c + 2)
            if CUT:
                return

    def attn_epilogue(acc_bank, bc_bank, ebuf, row0, t0, ntok, c_list, tag):
        osb, rc, hi, lo = ebuf

        def part1():
            vcopy("dve", osb[0:65, 0:ntok], PS[acc_bank][0:65, 0:ntok], ["ps%d" % acc_bank], [tag + "osb"])
            P.op("dve", lambda e: e.reciprocal(out=rc[64:65, 0:ntok], in_=osb[64:65, 0:ntok]), [tag + "osb"], [tag + "rc"])
            vcopy("dve", hi[64:65, 0:ntok], rc[64:65, 0:ntok], [tag + "rc"], [tag + "hi"])
            tt("dve", lo[64:65, 0:ntok], rc[64:65, 0:ntok], hi[64:65, 0:ntok], ALU.subtract, [tag + "rc", tag + "hi"], [tag + "lo"])

        def part2():
            mm(PS[bc_bank][0:64, 0:ntok], sel_b, hi[:, 0:ntok], True, False, [tag + "hi", "sel_b"], ["ps%d" % bc_bank])
            mm(PS[bc_bank][0:64, 0:ntok], sel_b, lo[:, 0:ntok], False, True, [tag + "lo", "sel_b"], ["ps%d" % bc_bank])
            tt("dve", osb[0:64, 0:ntok], osb[0:64, 0:ntok], PS[bc_bank][0:64, 0:ntok], ALU.mult,
               [tag + "osb", "ps%d" % bc_bank], [tag + "osb"])
            simple_dma(oT[row0:row0 + 64, t0:t0 + ntok], osb[0:64, 0:ntok], tag + "osb", reads=[tag + "osb"],
                       writes=["D:oT%d:%d" % (row0, c) for c in c_list])

        return part1, part2

    def emit_sorted(items):
        items.sort(key=lambda t: (t[0], t[1]))
        for _, _, fn in items:
            fn()

    def phase_B_dense(l):
        P.barrier()
        A.off = GLOBAL_OFF
        SM = max(S for S, _ in slots)
        NB = SM // 128
        kbuf = [A.alloc([128, SM], BF16) for _ in range(2)]
        vbuf = [A.alloc([128, NB, 65], BF16) for _ in range(2)]
        qbuf = [A.alloc([128, SM], BF16) for _ in range(2)]
        NPT = 4
        pt = [A.alloc([128, 1024], BF16) for _ in range(NPT)]
        ebufs = [(A.alloc([128, 512], F32), A.alloc([128, 512], F32), A.alloc([128, 512], BF16), A.alloc([128, 512], BF16))
                 for _ in range(2)]
        for i_, eb_ in enumerate(ebufs):
            P.op("pool", lambda e, eb_=eb_: e.memset(eb_[2], 0.0), writes=["e%dhi" % i_])
            P.op("pool", lambda e, eb_=eb_: e.memset(eb_[3], 0.0), writes=["e%dlo" % i_])
        work = []
        for si, (S, nact) in enumerate(slots):
            na = S if l == 0 else nact
            for h in range(6):
                work.append((si, "b", h, [h], na))
            for kv in range(2):
                work.append((si, "c", kv, [3 * kv, 3 * kv + 1, 3 * kv + 2], na))
        qjobs = []
        for wi, (si, kind, kv, qhs, na) in enumerate(work):
            for qh in qhs:
                qjobs.append((wi, qh))
        for i_ in range(2):
            P.op("pool", lambda e, i_=i_: e.memset(kbuf[i_], 0.0), writes=["kbuf%d" % i_])
            P.op("pool", lambda e, i_=i_: e.memset(qbuf[i_], 0.0), writes=["qbuf%d" % i_])

        def load_kv(wi):
            si, kind, kv, qhs, na = work[wi]
            S = slots[si][0]
            t0 = slot_tok0[si]
            i = wi % 2
            cl = range(t0 // 512, (t0 + S) // 512)
            if kind == "b":
                simple_dma(kbuf[i][0:96, 0:S], kbT[kv, :, t0:t0 + S], "kb%d" % i,
                           reads=["D:kbTn:%d" % c for c in cl] + ["D:kbTp:%d" % c for c in cl], writes=["kbuf%d" % i])
                simple_dma(vbuf[i][:, 0:S // 128, :], vb[t0:t0 + S, kv, :].rearrange("(b p) d -> p b d", p=128), "vb%d" % i,
                           reads=["D:vb:%d" % c for c in cl], writes=["vbuf%d" % i])
            else:
                P.op("pool", lambda e: e.memset(kbuf[i][64:128, 0:S], 0.0), writes=["kbuf%d" % i])
                simple_dma(kbuf[i][0:64, 0:S], kcT[64 * kv:64 * kv + 64, t0:t0 + S], "kb%d" % i,
                           reads=["D:kcT:%d" % c for c in cl], writes=["kbuf%d" % i])
                simple_dma(vbuf[i][:, 0:S // 128, :], vc[t0:t0 + S, kv, :].rearrange("(b p) d -> p b d", p=128), "vb%d" % i,
                           reads=["D:vc:%d" % c for c in cl], writes=["vbuf%d" % i])

        def load_q(qi):
            wi, qh = qjobs[qi]
            si, kind, kv, qhs, na = work[wi]
            t0 = slot_tok0[si]
            i = qi % 2
            cl = range(t0 // 512, (t0 + na) // 512)
            if kind == "b":
                simple_dma(qbuf[i][0:96, 0:na], qbT[qh, :, t0:t0 + na], "qb%d" % i,
                           reads=["D:qbT:%d" % c for c in cl], writes=["qbuf%d" % i])
            else:
                simple_dma(qbuf[i][0:64, 0:na], qcT[64 * qh:64 * qh + 64, t0:t0 + na], "qb%d" % i,
                           reads=["D:qcT:%d" % c for c in cl], writes=["qbuf%d" % i])

        items = []
        seqn = [0]

        def add(pos, fn):
            items.append((pos, seqn[0], fn))
            seqn[0] += 1

        LOOK = 2
        n = 0
        nacc = 0
        add(-2.0, lambda: load_kv(0))
        add(-2.0, lambda: load_q(0))
        for qi, (wi, qh) in enumerate(qjobs):
            si, kind, kv, qhs, na = work[wi]
            S = slots[si][0]
            t0 = slot_tok0[si]
            if qi + 1 < len(qjobs):
                nwi = qjobs[qi + 1][0]
                if nwi != wi:
                    add(n - 1 + LOOK + 0.55, lambda nwi=nwi: load_kv(nwi))
                add(n - 0.5, lambda qi=qi: load_q(qi + 1))
            ki = wi % 2
            qb_ = qi % 2
            dk = 96 if kind == "b" else 128
            scale = 1.0 / math.sqrt(96 if kind == "b" else 64)
            row0 = (256 + 64 * qh) if kind == "b" else (640 + 64 * qh)
            nkb = S // 128
            npair = nkb // 2
            for qc in range(na // 512):
                ab, bb = 6, 7
                eb = ebufs[nacc % 2]
                etag = "e%d" % (nacc % 2)
                nacc += 1
                for kp in range(npair):
                    sp_ = n % 3
                    pi = n % NPT

                    def front(kp=kp, sp_=sp_, pi=pi, ki=ki, qb_=qb_, qc=qc, dk=dk, scale=scale):
                        for u in range(2):
                            kb = 2 * kp + u
                            mm(PS[2 * sp_ + u], kbuf[ki][0:dk, 128 * kb:128 * (kb + 1)], qbuf[qb_][0:dk, 512 * qc:512 * (qc + 1)], True, True,
                               ["kbuf%d" % ki, "qbuf%d" % qb_], ["ps%d" % (2 * sp_ + u)])
                        act(pt[pi], psall[:, 1024 * sp_:1024 * (sp_ + 1)], AF.Exp, ["ps%d" % (2 * sp_), "ps%d" % (2 * sp_ + 1)],
                            ["pt%d" % pi], scale=scale)

                    def back(kp=kp, pi=pi, ki=ki, nkb=nkb, ab=ab):
                        for u in range(2):
                            kb = 2 * kp + u
                            mm(PS[ab][0:65, :], vbuf[ki][:, kb, :], pt[pi][:, 512 * u:512 * (u + 1)], kb == 0, kb == nkb - 1,
                               ["vbuf%d" % ki, "pt%d" % pi], ["ps%d" % ab])

                    add(n, front)
                    add(n + LOOK + 0.5, back)
                    n += 1
                tq = t0 + 512 * qc
                p1, p2 = attn_epilogue(ab, bb, eb, row0, tq, 512, [tq // 512], etag)
                add(n - 1 + LOOK + 0.6, p1)
                add(n - 1 + LOOK + 4.7, p2)
        emit_sorted(items)

    def phase_B_na(l):
        NACUT = 0
        P.barrier()
        A.off = GLOBAL_OFF
        SM = max(S for S, _ in slots)
        NB = SM // 128
        qz = A.alloc([128, 4, SM], BF16)
        ka_sb = A.alloc([128, 2, SM], BF16)
        P.op("pool", lambda e: e.memset(qz, 0.0), writes=["qz"])
        va_sb = A.alloc([128, NB, 4 * 65], BF16)
        bint = A.alloc([128, 5, 512], F32)
        bsp = [A.alloc([128, 512], F32) for _ in range(3)]
        sbias = [A.alloc([128, 512], F32) for _ in range(3)]
        pt = [A.alloc([128, 512], BF16) for _ in range(3)]
        ebufs = [(A.alloc([128, 512], F32), A.alloc([128, 512], F32), A.alloc([128, 512], BF16), A.alloc([128, 512], BF16))
                 for _ in range(2)]
        for i_, eb_ in enumerate(ebufs):
            P.op("pool", lambda e, eb_=eb_: e.memset(eb_[2], 0.0), writes=["e%dhi" % i_])
            P.op("pool", lambda e, eb_=eb_: e.memset(eb_[3], 0.0), writes=["e%dlo" % i_])
        simple_dma(bint, nab_int[l], "bint", writes=["bint"])
        sp_base = 0
        items = []
        seqn = [0]

        def add(pos, fn):
            items.append((pos, seqn[0], fn))
            seqn[0] += 1

        LOOK = 2
        n = 0
        spn = 0
        nacc = 0
        for si, (S, nact) in enumerate(slots):
            na = S if l == 0 else nact
            t0 = slot_tok0[si]
            nb = S // 128
            cl = list(range(t0 // 512, (t0 + S) // 512))
            specials = na_specials(S, nact)

            def loads(t0=t0, S=S, nb=nb, cl=cl):
                dma(lambda e, s_: [e.dma_start(out=qz[64 * (hd % 2):64 * (hd % 2) + 64, hd, 0:S],
                                               in_=qaT[64 * hd:64 * hd + 64, t0:t0 + S]).then_inc(s_, 16) for hd in range(4)],
                    "qz", 4, reads=["D:qaT:%d" % c for c in cl], writes=["qz"])
                simple_dma(ka_sb[:, :, 0:S], kaT[:, t0:t0 + S].rearrange("(j p) t -> p j t", p=128), "ka_sb",
                           reads=["D:kaT:%d" % c for c in cl], writes=["ka_sb"])
                simple_dma(va_sb[:, 0:nb, :], va[t0:t0 + S].rearrange("(b p) h d -> p b (h d)", p=128), "va_sb",
                           reads=["D:va:%d" % c for c in cl], writes=["va_sb"])

            add(n + LOOK + 5.0 if si else -1.0, loads)
            n += (LOOK + 6) if si else 0
            for g in range(na // 512):
                for qi4 in range(4):
                    i = 4 * g + qi4
                    if i in specials:
                        sidx = sp_base + specials.index(i)
                        offs = [(d, ("sp", sidx * 7 + di)) for di, d in enumerate(NA_OFFS)]
                    else:
                        offs = [(d, ("int", d + 2)) for d in (-2, -1, 0, 1, 2)]
                    for oi, (d, (kind, bidx)) in enumerate(offs):
                        j = (i + d) % nb
                        sb_ = n % 3
                        bi = None
                        if kind == "sp":
                            bi = spn % 3
                            spn += 1

                        def front(i=i, j=j, sb_=sb_, kind=kind, bidx=bidx, bi=bi):
                            for c4 in range(4):
                                hd = HS[c4]
                                mm(PS[sb_][:, 128 * c4:128 * (c4 + 1)], ka_sb[:, hd // 2, 128 * j:128 * (j + 1)],
                                   qz[:, hd, 128 * i:128 * (i + 1)], True, True, ["ka_sb", "qz"], ["ps%d" % sb_])
                            if kind == "sp":
                                simple_dma(bsp[bi], nab_sp[l, bidx], "bsp%d" % bi, writes=["bsp%d" % bi])
                                bias_ap, bias_slot = bsp[bi], "bsp%d" % bi
                            else:
                                bias_ap, bias_slot = bint[:, bidx, :], "bint"
                            stt("dve", sbias[sb_], PS[sb_], 0.125, bias_ap, ALU.mult, ALU.add, ["ps%d" % sb_, bias_slot], ["sbias%d" % sb_])
                            act(pt[sb_], sbias[sb_], AF.Exp, ["sbias%d" % sb_], ["pt%d" % sb_])

                        def back(j=j, sb_=sb_, qi4=qi4, first=(oi == 0), last=(oi == len(offs) - 1)):
                            for c4 in range(4):
                                hd = HS[c4]
                                mm(PS[4 + c4][0:65, 128 * qi4:128 * (qi4 + 1)], va_sb[:, j, 65 * hd:65 * (hd + 1)],
                                   pt[sb_][:, 128 * c4:128 * (c4 + 1)], first, last, ["va_sb", "pt%d" % sb_], ["ps%d" % (4 + c4)])

                        add(n, front)
                        add(n + LOOK + 0.5, back)
                        n += 1
                tq = t0 + 512 * g
                for c4 in range(4):
                    eb = ebufs[nacc % 2]
                    etag = "e%d" % (nacc % 2)
                    nacc += 1
                    p1, p2 = attn_epilogue(4 + c4, 3, eb, 64 * HS[c4], tq, 512, [tq // 512], etag)
                    add(n - 1 + LOOK + 0.6 + 0.01 * c4, p1)
                    add(n - 1 + LOOK + 0.6 + 0.01 * c4 + 0.005, p2)
            sp_base += len(specials)
        emit_sorted(items)

    def phase_C1(l):
        P.barrier()
        A.off = GLOBAL_OFF
        x_src = x_in if l == 0 else x1
        wo_sb = A.alloc([128, 8, D], BF16)
        gsm_sb = A.alloc([128, 17], F32)
        ot = [A.alloc([128, 8, 512], F32) for _ in range(2)]
        sqb = [A.alloc([128, 512], BF16) for _ in range(3)]
        rs = [A.alloc([128, 512], F32) for _ in range(3)]
        mT = [A.alloc([128, 8, 512], BF16) for _ in range(2)]
        xa = [A.alloc([128, D], F32) for _ in range(4)]
        dma(lambda e, s: e.dma_start(out=wo_sb, in_=wout[l].rearrange("(k p) n -> p k n", p=128)).then_inc(s, 16),
            "w_o", 1, writes=["wo_sb"], q="pool")
        simple_dma(gsm_sb, gsm[l], "g0", writes=["gsm_sb"])
        chunks = [c for c in range(NCH) if chunk_active(c, l)]
        groups = ((0, 2, 256), (2, 5, 384), (5, 8, 384))

        def load(ci):
            c = chunks[ci]
            i = ci % 2
            simple_dma(ot[i], oT[:, 512 * c:512 * (c + 1)].rearrange("(j p) t -> p j t", p=128), "ot%d" % i,
                       reads=["D:oT%d:%d" % (r, c) for r in range(0, D, 64)], writes=["ot%d" % i])

        def norm(ci):
            i = ci % 2
            for gi, (j0, j1, nf) in enumerate(groups):
                for j in range(j0, j1):
                    q = j % 3
                    act(sqb[q], ot[i][:, j, :], AF.Square, ["ot%d" % i], ["sqb%d" % q])
                    mm(PS[gi], ones_bf, sqb[q], j == j0, j == j1 - 1, ["sqb%d" % q, "ones_bf"], ["ps%d" % gi])
                rsqrt(rs[gi], nf, PS[gi], ["ps%d" % gi], "rs%d" % gi)
                for j in range(j0, j1):
                    stt("dve", mT[i][:, j, :], ot[i][:, j, :], gsm_sb[:, 9 + j:10 + j], rs[gi], ALU.mult, ALU.mult,
                        ["ot%d" % i, "gsm_sb", "rs%d" % gi], ["mT%d" % i])

        load(0)
        if len(chunks) > 1:
            load(1)
        norm(0)
        xn = [0]
        for ci, c in enumerate(chunks):
            i = ci % 2
            if ci + 1 < len(chunks):
                norm(ci + 1)
            if ci + 2 < len(chunks):
                load(ci + 2)
            for b4 in range(4):
                xi = xn[0] % 4
                xn[0] += 1
                r0 = 512 * c + 128 * b4
                simple_dma(xa[xi], x_src[r0:r0 + 128, :], "xa%d" % xi, reads=["D:x%d:%d" % (l, c)], writes=["xa%d" % xi])
                for hf in range(2):
                    b = 4 + (2 * b4 + hf) % 4
                    for j in range(8):
                        mm(PS[b], mT[i][:, j, 128 * b4:128 * (b4 + 1)], wo_sb[:, j, 512 * hf:512 * (hf + 1)], j == 0, j == 7,
                           ["mT%d" % i, "wo_sb"], ["ps%d" % b])
                    tt("dve", xa[xi][:, 512 * hf:512 * (hf + 1)], xa[xi][:, 512 * hf:512 * (hf + 1)], PS[b], ALU.add,
                       ["xa%d" % xi, "ps%d" % b], ["xa%d" % xi])
                simple_dma(xmid[r0:r0 + 128, :], xa[xi], "xa%d" % xi, reads=["xa%d" % xi], writes=["D:xmid:%d" % c])

    def phase_C2(l):
        P.barrier()
        A.off = GLOBAL_OFF
        wg_sb = A.alloc([128, 8, DFF], BF16)
        wu_sb = A.alloc([128, 8, DFF], BF16)
        wd_sb = A.alloc([128, NFF, D], BF16)
        gF = A.alloc([128, D], F32)
        gL = A.alloc([128, D], F32)
        xm = [A.alloc([128, D], F32) for _ in range(4)]
        junk = A.alloc([128, D], BF16)
        ssv = A.alloc([128, 8], F32)
        hb = [A.alloc([128, D], BF16) for _ in range(2)]
        hT = A.alloc([128, 8, 512], BF16)
        aT = A.alloc([128, NFF, 512], BF16)
        sg = [A.alloc([128, 512], F32) for _ in range(2)]
        for k in range(8):
            dma(lambda e, s, k=k: e.dma_start(out=wg_sb[:, k, :], in_=wg[l, 128 * k:128 * (k + 1), :]).then_inc(s, 16),
                "w_g%d" % k, 1, writes=["wg_sb"], q="pool")
            dma(lambda e, s, k=k: e.dma_start(out=wu_sb[:, k, :], in_=wu[l, 128 * k:128 * (k + 1), :]).then_inc(s, 16),
                "w_u%d" % k, 1, writes=["wu_sb"], q="pool")
        for f0 in range(0, NFF, 4):
            f1 = min(NFF, f0 + 4)
            dma(lambda e, s, f0=f0, f1=f1: e.dma_start(out=wd_sb[:, f0:f1, :], in_=wd[l, 128 * f0:128 * f1, :].rearrange("(k p) n -> p k n", p=128)).then_inc(s, 16),
                "w_d%d" % f0, 1, writes=["wd_sb"], q="pool")
        simple_dma(gF, gbc[l, 1].partition_broadcast(128), "g1", writes=["gF"])
        if l == L - 1:
            simple_dma(gL, gfin.partition_broadcast(128), "g2", writes=["gL"])
        chunks = [c for c in range(NCH) if chunk_active(c, l)]

        def load_block(ci, b4):
            c = chunks[ci]
            r0 = 512 * c + 128 * b4
            simple_dma(xm[b4], xmid[r0:r0 + 128, :], "xm%d" % b4, reads=["D:xmid:%d" % c], writes=["xm%d" % b4])

        def front_block(b4):
            j = b4 % 2
            act(junk, xm[b4], AF.Square, ["xm%d" % b4], ["junk", "ssv"], accum_out=ssv[:, b4:b4 + 1])
            rsqrt(ssv[:, b4:b4 + 1], D, ssv[:, b4:b4 + 1], ["ssv"], "ssv")
            stt("dve", hb[j], xm[b4], ssv[:, b4:b4 + 1], gF, ALU.mult, ALU.mult, ["xm%d" % b4, "ssv", "gF"], ["hb%d" % j])
            for k in range(8):
                P.op("pe", lambda e, k=k, j=j: e.transpose(out=PSB[j][:, 128 * k:128 * (k + 1)], in_=hb[j][:, 128 * k:128 * (k + 1)], identity=ident),
                     ["hb%d" % j, "ident"], ["ps%d" % j])
            vcopy("act" if b4 % 2 else "dve", hT[:, :, 128 * b4:128 * (b4 + 1)],
                  PSB[j].rearrange("p (k t) -> p k t", t=128), ["ps%d" % j], ["hT"])

        for b4 in range(4):
            load_block(0, b4)
        for b4 in range(4):
            front_block(b4)
        for ci, c in enumerate(chunks):
            nxt = ci + 1 < len(chunks)
            for f in range(NFF):
                bg = 2 + f % 2
                bu = 4 + f % 2
                for k in range(8):
                    mm(PS[bg], wg_sb[:, k, 128 * f:128 * (f + 1)], hT[:, k, :], k == 0, k == 7, ["wg_sb", "hT"], ["ps%d" % bg])
                for k in range(8):
                    mm(PS[bu], wu_sb[:, k, 128 * f:128 * (f + 1)], hT[:, k, :], k == 0, k == 7, ["wu_sb", "hT"], ["ps%d" % bu])
                q = f % 2
                act(sg[q], PS[bg], AF.Silu, ["ps%d" % bg], ["sg%d" % q])
                tt("dve", aT[:, f, :], sg[q], PS[bu], ALU.mult, ["sg%d" % q, "ps%d" % bu], ["aT"])
            for b4 in range(4):
                r0 = 512 * c + 128 * b4
                for hf in range(2):
                    b = 6 + hf
                    for f in range(NFF):
                        mm(PS[b], aT[:, f, 128 * b4:128 * (b4 + 1)], wd_sb[:, f, 512 * hf:512 * (hf + 1)], f == 0, f == NFF - 1,
                           ["aT", "wd_sb"], ["ps%d" % b])
                    tt("dve", xm[b4][:, 512 * hf:512 * (hf + 1)], xm[b4][:, 512 * hf:512 * (hf + 1)], PS[b], ALU.add,
                       ["xm%d" % b4, "ps%d" % b], ["xm%d" % b4])
                if l < L - 1:
                    simple_dma(x1[r0:r0 + 128, :], xm[b4], "xm%d" % b4, reads=["xm%d" % b4], writes=["D:x%d:%d" % (l + 1, c)])
                else:
                    act(junk, xm[b4], AF.Square, ["xm%d" % b4], ["junk", "ssv"], accum_out=ssv[:, 4 + b4:5 + b4])
                    rsqrt(ssv[:, 4 + b4:5 + b4], D, ssv[:, 4 + b4:5 + b4], ["ssv"], "ssv")
                    stt("dve", xm[b4], xm[b4], ssv[:, 4 + b4:5 + b4], gL, ALU.mult, ALU.mult, ["xm%d" % b4, "ssv", "gL"], ["xm%d" % b4])
                    ro = out_row0(c) + 128 * b4
                    simple_dma(y_out[ro:ro + 128, :], xm[b4], "xm%d" % b4, reads=["xm%d" % b4], writes=["D:y:%d" % c])
                if nxt:
                    load_block(ci + 1, b4)
                    if b4 >= 1:
                        front_block(b4 - 1)
            if nxt:
                front_block(3)

    phases = []
    for l in range(L):
        phases += [("A", l), ("Bna", l), ("Bd", l), ("C1", l), ("C2", l)]
    for name, l in phases:
        {"A": phase_A, "Bna": phase_B_na, "Bd": phase_B_dense, "C1": phase_C1, "C2": phase_C2}[name](l)
        if stop_after == (name, l):
            break
    P.final_wait("sp")
    cc = P.emit(nc, st)
    st.close()
    return nc, (len(P.ops), {k: len(v) for k, v in P.streams.items()}, cc)


def rope_perm(d):
    half = d // 2
    return np.concatenate([np.arange(half, d), np.arange(0, half)])


def rope_tables(pos, d):
    half = d // 2
    inv = (10000.0 ** (-(np.arange(half, dtype=np.float32) * 2.0) / d)).astype(np.float32)
    ang = pos.astype(np.float32)[None, :] * inv[:, None]
    cos = np.cos(ang).astype(np.float32)
    sin = np.sin(ang).astype(np.float32)
    return np.concatenate([cos, cos], 0), np.concatenate([-sin, sin], 0)


def na_bias_tile(rpb_l, rows, tq, tk):
    out = np.full((128, 4, 128), NEG, np.float32)
    nb = rows // 2
    if tk is None or tk < 0 or tk >= nb:
        return out.reshape(128, 512)
    qi = np.arange(128)
    r = 2 * tq + qi // 64
    cq = qi % 64
    ki = np.arange(128)
    kr = 2 * tk + ki // 64
    ck = ki % 64
    kr_n = min(8, rows)
    rs = np.clip(r - kr_n // 2, 0, rows - kr_n)
    cs = np.clip(cq - 8, 0, GRID_W - 16)
    valid = ((kr[:, None] >= rs[None, :]) & (kr[:, None] < rs[None, :] + kr_n)
             & (ck[:, None] >= cs[None, :]) & (ck[:, None] < cs[None, :] + 16))
    dr = np.clip(kr[:, None] - r[None, :] + 7, 0, 14)
    dc = np.clip(ck[:, None] - cq[None, :] + 15, 0, 30)
    vals = rpb_l[:, dr, dc]
    out = np.where(valid[:, None, :], vals.transpose(1, 0, 2), np.float32(NEG)).astype(np.float32)
    return np.ascontiguousarray(out[:, list(HS), :]).reshape(128, 512)


def prep_weights(inp):
    f = lambda a: np.asarray(a, np.float32)
    w_in = f(inp["w_in"])
    Ls = w_in.shape[0]
    segs = np.cumsum([0, 256, 256, 256, 384, 256, 32, 384, 128, 128])
    o_qa, o_ka, o_va, o_cq, o_ckv, o_kpe, o_qc, o_kc, o_vc = segs[:9]
    p32 = rope_perm(32)
    p_ax = np.concatenate([rope_perm(32), 32 + rope_perm(32)])
    win = np.zeros((Ls, D, NIN), np.float32)
    win[:, :, C_QA:C_QA + 256] = w_in[:, :, o_qa:o_qa + 256]
    win[:, :, C_KA:C_KA + 256] = w_in[:, :, o_ka:o_ka + 256]
    win[:, :, C_CQ:C_CQ + 384] = w_in[:, :, o_cq:o_cq + 384]
    win[:, :, C_CKV:C_CKV + 256] = w_in[:, :, o_ckv:o_ckv + 256]
    win[:, :, C_KPA + 64:C_KPA + 96] = w_in[:, :, o_kpe:o_kpe + 32]
    win[:, :, C_KPB + 64:C_KPB + 96] = w_in[:, :, o_kpe + p32]
    qc_sw = np.concatenate([o_qc + 64 * h + p_ax for h in range(6)])
    kc_sw = np.concatenate([o_kc + 64 * h + p_ax for h in range(2)])
    win[:, :, C_QCA:C_QCA + 384] = w_in[:, :, o_qc:o_qc + 384]
    win[:, :, C_QCB:C_QCB + 384] = w_in[:, :, qc_sw]
    win[:, :, C_KCA:C_KCA + 128] = w_in[:, :, o_kc:o_kc + 128]
    win[:, :, C_KCB:C_KCB + 128] = w_in[:, :, kc_sw]
    win[:, :, C_V:C_V + 256] = w_in[:, :, o_va:o_va + 256]
    win[:, :, C_V + 256:C_V + 384] = w_in[:, :, o_vc:o_vc + 128]
    w_uq = f(inp["w_uq"])
    wuq = np.zeros((Ls, 384, 1152), np.float32)
    for h in range(6):
        wuq[:, :, 192 * h:192 * h + 96] = w_uq[:, :, 96 * h:96 * h + 96]
        wuq[:, :, 192 * h + 96:192 * h + 160] = w_uq[:, :, 96 * h:96 * h + 64]
        wuq[:, :, 192 * h + 160:192 * h + 192] = w_uq[:, :, 96 * h + 64 + p32]
    w_ukv = f(inp["w_ukv"])
    wukv = np.zeros((Ls, 256, 768), np.float32)
    for h in range(6):
        wukv[:, :, 64 * h:64 * h + 64] = w_ukv[:, :, 128 * h:128 * h + 64]
        wukv[:, :, 384 + 64 * h:384 + 64 * h + 64] = w_ukv[:, :, 128 * h + 64:128 * h + 128]
    gsm = np.zeros((Ls, 128, 17), np.float32)
    qan = f(inp["q_a_norm"])
    kvn = f(inp["kv_a_norm"])
    qn = f(inp["q_norm_c"])
    kn = f(inp["k_norm_c"])
    gout = np.concatenate([f(inp["g_out_a"]), f(inp["g_out_b"]), f(inp["g_out_c"])], 1)
    for l in range(Ls):
        gsm[l, :, 0:3] = qan[l].reshape(3, 128).T
        gsm[l, :, 3:5] = kvn[l].reshape(2, 128).T
        gsm[l, :, 5] = np.tile(qn[l], 2)
        gsm[l, :, 6] = np.tile(qn[l][p_ax], 2)
        gsm[l, :, 7] = np.tile(kn[l], 2)
        gsm[l, :, 8] = np.tile(kn[l][p_ax], 2)
        gsm[l, :, 9:17] = gout[l].reshape(8, 128).T
    gbc = np.stack([f(inp["attn_norm"]), f(inp["ffn_norm"])], 1)
    bd = np.zeros((128, 128), np.float32)
    bd[:64, :64] = 1
    bd[64:, 64:] = 1
    return dict(win=win, wuq=wuq, wukv=wukv, wout=f(inp["w_out"]), wg=f(inp["w_gate"]), wu=f(inp["w_up"]),
                wd=f(inp["w_down"]), gsm=gsm, gbc=np.ascontiguousarray(gbc), gfin=f(inp["final_norm"]),
                ident=np.eye(128, dtype=np.float32).astype(ml_dtypes.bfloat16), bdones=bd.astype(ml_dtypes.bfloat16))


def prep_core_tables(rpb, slots, true_pos):
    pos = np.concatenate(true_pos)
    row = pos // GRID_W
    col = pos % GRID_W
    cr, sr = rope_tables(row, 32)
    cc, sc = rope_tables(col, 32)
    cosC = np.concatenate([cr, cc], 0)
    sinC = np.concatenate([sr, sc], 0)
    ropeC = np.stack([np.tile(cosC, (2, 1)), np.tile(sinC, (2, 1))], 0).astype(np.float32)
    cb, sb = rope_tables(pos, 32)
    ropeB = np.stack([cb, sb], 0).astype(np.float32)
    Ls = rpb.shape[0]
    nab_int = np.zeros((Ls, 128, 5, 512), np.float32)
    sp_tiles = [[] for _ in range(Ls)]
    for l in range(Ls):
        for di, d in enumerate((-2, -1, 0, 1, 2)):
            nab_int[l, :, di, :] = na_bias_tile(rpb[l], 64, 8, 8 + d)
        for si, (S, nact) in enumerate(slots):
            nb = S // 128
            rows = S // GRID_W
            tb = true_pos[si][::128] // 128
            for i in na_specials(S, nact):
                for d in NA_OFFS:
                    tq = int(tb[i])
                    tk = tq + d
                    j = (i + d) % nb
                    if 0 <= tk < nb:
                        assert int(tb[j]) == tk
                    sp_tiles[l].append(na_bias_tile(rpb[l], rows, tq, tk))
    nab_sp = np.stack([np.stack(t, 0) for t in sp_tiles], 0)
    return ropeC, ropeB, nab_int, nab_sp


_CACHE = {}


def run_slots(slots, per_core_x, per_core_pos, inp, debug=False, stop_after=None, trace=False):
    key = (tuple(slots), debug, stop_after)
    if key not in _CACHE:
        _CACHE[key] = build_program(slots, debug=debug, stop_after=stop_after)
    nc, _ = _CACHE[key]
    W = prep_weights(inp)
    rpb = np.asarray(inp["rpb"], np.float32)
    in_maps = []
    for x, pos in zip(per_core_x, per_core_pos):
        ropeC, ropeB, nab_int, nab_sp = prep_core_tables(rpb, slots, pos)
        m = dict(W)
        m.update(x_in=np.ascontiguousarray(x, dtype=np.float32), ropeC=ropeC, ropeB=ropeB, nab_int=nab_int, nab_sp=nab_sp)
        in_maps.append(m)
    res = run_bass_kernel_spmd(nc, in_maps, core_ids=list(range(len(in_maps))), **({"trace": True} if trace else {}))
    return res


def kernel(**inp):
    xp = np.asarray(inp["x_prompt"], np.float32)
    xs = np.asarray(inp["x_sample"], np.float32)
    B, S, _ = xp.shape
    Bs, Ss, _ = xs.shape
    assert Bs == 2 * 8 and B * 2 == 8
    H = S // 2
    slots = [(Ss, Ss), (Ss, Ss), (S, H)]
    per_x, per_pos = [], []
    for c in range(8):
        p, h = c // 2, c % 2
        xl = np.concatenate([xs[2 * c], xs[2 * c + 1], np.roll(xp[p], -h * H, axis=0)], 0)
        per_x.append(xl)
        per_pos.append([np.arange(Ss), np.arange(Ss), (np.arange(S) + h * H) % S])
    res = run_slots(slots, per_x, per_pos, inp)
    yp = np.zeros_like(xp)
    ys = np.zeros_like(xs)
    for c in range(8):
        y = res.results[c]["y_out"]
        p, h = c // 2, c % 2
        ys[2 * c] = y[0:Ss]
        ys[2 * c + 1] = y[Ss:2 * Ss]
        yp[p, h * H:(h + 1) * H] = y[2 * Ss:2 * Ss + H]
    return yp, ys
```
